# Optimizing a Trainium2 kernel written in Bass

```python
import math
import jax, jax.numpy as jnp
from jax import lax
import numpy as np

D_MODEL = 1024
BATCH = 16
SEQ = 4096
DEPTH = 4

MEM_LEN = 256
EPS = 1e-6
BRANCH_WIDTH = 512
N_BRANCH = 3

A_WIDTH = BRANCH_WIDTH
A_HEADS = 8
A_HEAD_DIM = A_WIDTH // A_HEADS
A_CONV = 4
RG_C = 8.0

B_HEADS = 4
B_DK = BRANCH_WIDTH // B_HEADS
B_DV = BRANCH_WIDTH // B_HEADS
B_CONV = 4
B_CHUNK = 64

C_WIDTH = BRANCH_WIDTH
C_GROUP = 16
C_GROUPS = C_WIDTH // C_GROUP
C_STATE = 64

X_HEADS = 4
X_HEAD_DIM = D_MODEL // X_HEADS

D_FF = 3 * D_MODEL
FFN_CONV = 3

IN_SPLITS = (A_WIDTH, A_WIDTH,
             B_HEADS * B_DK, B_HEADS * B_DK,
             B_HEADS * B_DV, B_HEADS * B_DV,
             B_HEADS, B_HEADS,
             C_WIDTH,
             N_BRANCH * D_MODEL)
D_IN = sum(IN_SPLITS)

kernel_name = "hybrid_rglru_deltanet_s5_block"


def rmsnorm(x, g):
    xf = x.astype(jnp.float32)
    var = jnp.mean(xf * xf, axis=-1, keepdims=True)
    return (xf * lax.rsqrt(var + EPS) * g.astype(jnp.float32)).astype(x.dtype)


def l2norm(x):
    return x * lax.rsqrt(jnp.sum(x * x, axis=-1, keepdims=True) + EPS)


def causal_dwconv(x, w):
    k, ch = w.shape
    return lax.conv_general_dilated(x, w[:, None, :].astype(x.dtype), (1,), [(k - 1, 0)],
                                    dimension_numbers=('NWC', 'WIO', 'NWC'),
                                    feature_group_count=ch)


def _linear_combine(e1, e2):
    a1, b1 = e1
    a2, b2 = e2
    return a1 * a2, a2 * b1 + b2


def _complex_linear_combine(e1, e2):
    ar1, ai1, br1, bi1 = e1
    ar2, ai2, br2, bi2 = e2
    return (ar1 * ar2 - ai1 * ai2,
            ar1 * ai2 + ai1 * ar2,
            ar2 * br1 - ai2 * bi1 + br2,
            ar2 * bi1 + ai2 * br1 + bi2)


def rg_lru(x, w_r, b_r, w_i, b_i, lam):
    bsz, s, _ = x.shape
    xf = x.astype(jnp.float32)
    xh = xf.reshape(bsz, s, A_HEADS, A_HEAD_DIM)
    r = jax.nn.sigmoid(jnp.einsum('bshi,hij->bshj', xh, w_r.astype(jnp.float32)).reshape(bsz, s, A_WIDTH) + b_r.astype(jnp.float32))
    ig = jax.nn.sigmoid(jnp.einsum('bshi,hij->bshj', xh, w_i.astype(jnp.float32)).reshape(bsz, s, A_WIDTH) + b_i.astype(jnp.float32))
    log_a = -RG_C * r * jax.nn.softplus(-lam.astype(jnp.float32))
    a = jnp.exp(log_a)
    first = (jnp.arange(s) == 0)[None, :, None]
    mult = jnp.where(first, 1.0, jnp.sqrt(-jnp.expm1(2.0 * log_a)))
    b = mult * ig * xf
    _, h = lax.associative_scan(_linear_combine, (a, b), axis=1)
    return h.astype(x.dtype)


def chunk_gated_delta_rule(q, k, v, g, beta):
    bsz, s, h, dk = q.shape
    dv = v.shape[-1]
    c = B_CHUNK
    n = s // c

    def chunks(t):
        return t.reshape(bsz, n, c, h, t.shape[-1]).transpose(0, 3, 1, 2, 4)

    qc = chunks(q * dk ** -0.5)
    kc = chunks(k)
    vc = chunks(v)
    gc = jnp.cumsum(g.reshape(bsz, n, c, h).transpose(0, 3, 1, 2), axis=-1)
    bc = beta.reshape(bsz, n, c, h).transpose(0, 3, 1, 2)
    incl = jnp.tril(jnp.ones((c, c), dtype=bool))
    strict = jnp.tril(jnp.ones((c, c), dtype=bool), k=-1)
    decay = jnp.exp(jnp.where(incl, gc[..., :, None] - gc[..., None, :], -jnp.inf))
    kb = kc * bc[..., None]
    a_mat = jnp.where(strict, jnp.einsum('bhnik,bhnjk->bhnij', kb, kc) * decay, 0.0)
    lower = a_mat + jnp.eye(c, dtype=a_mat.dtype)
    rhs = jnp.concatenate([vc * bc[..., None], kb * jnp.exp(gc)[..., None]], axis=-1)
    sol = lax.linalg.triangular_solve(lower, rhs, left_side=True, lower=True, unit_diagonal=True)
    u, w = sol[..., :dv], sol[..., dv:]
    qk = jnp.where(incl, jnp.einsum('bhnik,bhnjk->bhnij', qc, kc) * decay, 0.0)
    q_dec = qc * jnp.exp(gc)[..., None]
    k_dec = kc * jnp.exp(gc[..., -1:] - gc)[..., None]
    g_tot = jnp.exp(gc[..., -1])
    xs = tuple(jnp.moveaxis(t, 2, 0) for t in (u, w, qk, q_dec, k_dec, g_tot))

    def step(state, inp):
        u_n, w_n, qk_n, qd_n, kd_n, gt_n = inp
        v_new = u_n - jnp.einsum('bhck,bhkv->bhcv', w_n, state)
        o = jnp.einsum('bhck,bhkv->bhcv', qd_n, state) + jnp.einsum('bhij,bhjv->bhiv', qk_n, v_new)
        state = state * gt_n[..., None, None] + jnp.einsum('bhck,bhcv->bhkv', kd_n, v_new)
        return state, o

    s0 = jnp.zeros((bsz, h, dk, dv), jnp.float32)
    _, o = lax.scan(step, s0, xs)
    return o.transpose(1, 0, 3, 2, 4).reshape(bsz, s, h, dv)


def s5_layer(u, lam_re, lam_im, log_dt, b_re, b_im, c_re, c_im, d):
    bsz, s, _ = u.shape
    f32 = jnp.float32
    uf = u.astype(f32)
    ug = uf.reshape(bsz, s, C_GROUPS, C_GROUP)
    lr, li = lam_re.astype(f32), lam_im.astype(f32)
    dt = jnp.exp(log_dt.astype(f32))[:, None]
    mag = jnp.exp(lr * dt)
    ar, ai = mag * jnp.cos(li * dt), mag * jnp.sin(li * dt)
    den = lr * lr + li * li
    fr = ((ar - 1.0) * lr + ai * li) / den
    fi = (ai * lr - (ar - 1.0) * li) / den
    br, bi = b_re.astype(f32), b_im.astype(f32)
    bbr = fr[..., None] * br - fi[..., None] * bi
    bbi = fr[..., None] * bi + fi[..., None] * br
    bu_r = jnp.einsum('bsgc,gpc->bsgp', ug, bbr)
    bu_i = jnp.einsum('bsgc,gpc->bsgp', ug, bbi)
    a_r = jnp.broadcast_to(ar, (1, s, C_GROUPS, C_STATE))
    a_i = jnp.broadcast_to(ai, (1, s, C_GROUPS, C_STATE))
    _, _, hr, hi = lax.associative_scan(_complex_linear_combine, (a_r, a_i, bu_r, bu_i), axis=1)
    y = (jnp.einsum('bsgp,gcp->bsgc', hr, c_re.astype(f32))
         - jnp.einsum('bsgp,gcp->bsgc', hi, c_im.astype(f32))).reshape(bsz, s, C_WIDTH)
    return y + d.astype(f32) * uf


def hybrid_mixer(h, w_in, b_gate, a_conv_w, a_conv_b, a_w_r, a_b_r, a_w_i, a_b_i, a_lam,
                 b_conv_w, b_a_log, b_dt_bias, b_norm,
                 c_lam_re, c_lam_im, c_log_dt, c_b_re, c_b_im, c_c_re, c_c_im, c_d, c_glu_w, c_glu_b,
                 w_branch, w_out):
    bsz, s, _ = h.shape
    f32 = jnp.float32
    offs = np.cumsum(IN_SPLITS)[:-1].tolist()
    xa, ga, q, k, v, z, beta_raw, alpha_raw, uc, gates = jnp.split(h @ w_in, offs, axis=-1)

    ya = rg_lru(causal_dwconv(xa, a_conv_w) + a_conv_b, a_w_r, a_b_r, a_w_i, a_b_i, a_lam) * jax.nn.gelu(ga)

    qkv = jax.nn.silu(causal_dwconv(jnp.concatenate([q, k, v], axis=-1), b_conv_w)).astype(f32)
    q, k, v = jnp.split(qkv, [B_HEADS * B_DK, 2 * B_HEADS * B_DK], axis=-1)
    q = l2norm(q.reshape(bsz, s, B_HEADS, B_DK))
    k = l2norm(k.reshape(bsz, s, B_HEADS, B_DK))
    v = v.reshape(bsz, s, B_HEADS, B_DV)
    beta = jax.nn.sigmoid(beta_raw.astype(f32))
    g = -jnp.exp(b_a_log.astype(f32)) * jax.nn.softplus(alpha_raw.astype(f32) + b_dt_bias.astype(f32))
    o = chunk_gated_delta_rule(q, k, v, g, beta)
    o = rmsnorm(o, b_norm) * jax.nn.silu(z.astype(f32).reshape(bsz, s, B_HEADS, B_DV))
    yb = o.reshape(bsz, s, B_HEADS * B_DV).astype(h.dtype)

    yc = jax.nn.gelu(s5_layer(uc, c_lam_re, c_lam_im, c_log_dt, c_b_re, c_b_im, c_c_re, c_c_im, c_d))
    yc = (yc * jax.nn.sigmoid(yc @ c_glu_w.astype(f32) + c_glu_b.astype(f32))).astype(h.dtype)

    branches = jnp.stack([ya, yb, yc], axis=2)
    proj = jnp.einsum('bskc,kcd->bskd', branches, w_branch)
    gate = jax.nn.sigmoid((gates + b_gate).reshape(bsz, s, N_BRANCH, D_MODEL))
    return jnp.sum(gate * proj, axis=2) @ w_out


def cross_attention(h, mem_n, w_q, w_kv, w_o):
    bsz, s, _ = h.shape
    m = mem_n.shape[1]
    q = (h @ w_q).reshape(bsz, s, X_HEADS, X_HEAD_DIM)
    k, v = jnp.split(mem_n @ w_kv, 2, axis=-1)
    k = k.reshape(bsz, m, X_HEADS, X_HEAD_DIM)
    v = v.reshape(bsz, m, X_HEADS, X_HEAD_DIM)
    sc = jnp.einsum('bshd,bmhd->bhsm', q, k).astype(jnp.float32) * (X_HEAD_DIM ** -0.5)
    p = jax.nn.softmax(sc, axis=-1).astype(v.dtype)
    o = jnp.einsum('bhsm,bmhd->bshd', p, v).reshape(bsz, s, D_MODEL)
    return o @ w_o


def conv_ffn(h, w_up, conv_w, conv_b, w_down):
    u = causal_dwconv(h @ w_up, conv_w) + conv_b
    gate, val = jnp.split(u, 2, axis=-1)
    return (jax.nn.gelu(gate) * val) @ w_down


def setup_inputs(seed: int = 0) -> dict:
    key = jax.random.key(seed)
    ks = iter(jax.random.split(key, 64))
    L = DEPTH
    f32 = jnp.float32

    def nrm(shape, scale):
        return jax.random.normal(next(ks), shape, f32) * scale

    def gain(shape):
        return 1.0 + 0.02 * jax.random.normal(next(ks), shape, f32)

    def unif(shape, lo, hi):
        return jax.random.uniform(next(ks), shape, f32, minval=lo, maxval=hi)

    x = nrm((BATCH, SEQ, D_MODEL), 1.0)
    mem = nrm((BATCH, MEM_LEN, D_MODEL), 1.0)
    mix_norm = gain((L, D_MODEL))
    w_in = nrm((L, D_MODEL, D_IN), D_MODEL ** -0.5)
    b_gate = nrm((L, N_BRANCH * D_MODEL), 0.01)
    a_conv_w = nrm((L, A_CONV, A_WIDTH), A_CONV ** -0.5)
    a_conv_b = nrm((L, A_WIDTH), 0.01)
    a_w_r = nrm((L, A_HEADS, A_HEAD_DIM, A_HEAD_DIM), A_HEAD_DIM ** -0.5)
    a_b_r = nrm((L, A_WIDTH), 0.01)
    a_w_i = nrm((L, A_HEADS, A_HEAD_DIM, A_HEAD_DIM), A_HEAD_DIM ** -0.5)
    a_b_i = nrm((L, A_WIDTH), 0.01)
    a_pow = unif((L, A_WIDTH), 0.9, 0.999) ** (1.0 / RG_C)
    a_lam = jnp.log(a_pow) - jnp.log1p(-a_pow)
    b_conv_w = nrm((L, B_CONV, 2 * B_HEADS * B_DK + B_HEADS * B_DV), B_CONV ** -0.5)
    b_a_log = jnp.log(unif((L, B_HEADS), 1.0, 16.0))
    dt = jnp.exp(unif((L, B_HEADS), math.log(0.001), math.log(0.1)))
    b_dt_bias = dt + jnp.log(-jnp.expm1(-dt))
    b_norm = gain((L, B_DV))
    c_lam_re = -0.5 + 0.01 * jax.random.normal(next(ks), (L, C_GROUPS, C_STATE), f32)
    c_lam_im = math.pi * jnp.arange(C_STATE, dtype=f32) + 0.01 * jax.random.normal(next(ks), (L, C_GROUPS, C_STATE), f32)
    c_log_dt = unif((L, C_GROUPS), math.log(0.001), math.log(0.1))
    c_b_re = nrm((L, C_GROUPS, C_STATE, C_GROUP), (2 * C_GROUP) ** -0.5)
    c_b_im = nrm((L, C_GROUPS, C_STATE, C_GROUP), (2 * C_GROUP) ** -0.5)
    c_c_re = nrm((L, C_GROUPS, C_GROUP, C_STATE), C_STATE ** -0.5)
    c_c_im = nrm((L, C_GROUPS, C_GROUP, C_STATE), C_STATE ** -0.5)
    c_d = nrm((L, C_WIDTH), 1.0)
    c_glu_w = nrm((L, C_WIDTH, C_WIDTH), C_WIDTH ** -0.5)
    c_glu_b = nrm((L, C_WIDTH), 0.01)
    w_branch = nrm((L, N_BRANCH, BRANCH_WIDTH, D_MODEL), BRANCH_WIDTH ** -0.5)
    w_out = nrm((L, D_MODEL, D_MODEL), D_MODEL ** -0.5)
    xa_norm = gain((L, D_MODEL))
    mem_norm = gain((L, D_MODEL))
    xa_w_q = nrm((L, D_MODEL, D_MODEL), D_MODEL ** -0.5)
    xa_w_kv = nrm((L, D_MODEL, 2 * D_MODEL), D_MODEL ** -0.5)
    xa_w_o = nrm((L, D_MODEL, D_MODEL), D_MODEL ** -0.5)
    ffn_norm = gain((L, D_MODEL))
    ffn_w_up = nrm((L, D_MODEL, 2 * D_FF), D_MODEL ** -0.5)
    ffn_conv_w = nrm((L, FFN_CONV, 2 * D_FF), FFN_CONV ** -0.5)
    ffn_conv_b = nrm((L, 2 * D_FF), 0.01)
    ffn_w_down = nrm((L, D_FF, D_MODEL), D_FF ** -0.5)
    final_norm = gain((D_MODEL,))
    return {"x": x, "mem": mem, "mix_norm": mix_norm, "w_in": w_in, "b_gate": b_gate,
            "a_conv_w": a_conv_w, "a_conv_b": a_conv_b, "a_w_r": a_w_r, "a_b_r": a_b_r,
            "a_w_i": a_w_i, "a_b_i": a_b_i, "a_lam": a_lam,
            "b_conv_w": b_conv_w, "b_a_log": b_a_log, "b_dt_bias": b_dt_bias, "b_norm": b_norm,
            "c_lam_re": c_lam_re, "c_lam_im": c_lam_im, "c_log_dt": c_log_dt,
            "c_b_re": c_b_re, "c_b_im": c_b_im, "c_c_re": c_c_re, "c_c_im": c_c_im, "c_d": c_d,
            "c_glu_w": c_glu_w, "c_glu_b": c_glu_b, "w_branch": w_branch, "w_out": w_out,
            "xa_norm": xa_norm, "mem_norm": mem_norm, "xa_w_q": xa_w_q, "xa_w_kv": xa_w_kv, "xa_w_o": xa_w_o,
            "ffn_norm": ffn_norm, "ffn_w_up": ffn_w_up, "ffn_conv_w": ffn_conv_w, "ffn_conv_b": ffn_conv_b,
            "ffn_w_down": ffn_w_down, "final_norm": final_norm}


def reference(x, mem, mix_norm, w_in, b_gate, a_conv_w, a_conv_b, a_w_r, a_b_r, a_w_i, a_b_i, a_lam,
              b_conv_w, b_a_log, b_dt_bias, b_norm, c_lam_re, c_lam_im, c_log_dt, c_b_re, c_b_im,
              c_c_re, c_c_im, c_d, c_glu_w, c_glu_b, w_branch, w_out, xa_norm, mem_norm, xa_w_q,
              xa_w_kv, xa_w_o, ffn_norm, ffn_w_up, ffn_conv_w, ffn_conv_b, ffn_w_down, final_norm):
    for l in range(DEPTH):
        h = rmsnorm(x, mix_norm[l])
        x = x + hybrid_mixer(h, w_in[l], b_gate[l], a_conv_w[l], a_conv_b[l], a_w_r[l], a_b_r[l],
                             a_w_i[l], a_b_i[l], a_lam[l], b_conv_w[l], b_a_log[l], b_dt_bias[l], b_norm[l],
                             c_lam_re[l], c_lam_im[l], c_log_dt[l], c_b_re[l], c_b_im[l], c_c_re[l],
                             c_c_im[l], c_d[l], c_glu_w[l], c_glu_b[l], w_branch[l], w_out[l])
        h = rmsnorm(x, xa_norm[l])
        x = x + cross_attention(h, rmsnorm(mem, mem_norm[l]), xa_w_q[l], xa_w_kv[l], xa_w_o[l])
        h = rmsnorm(x, ffn_norm[l])
        x = x + conv_ffn(h, ffn_w_up[l], ffn_conv_w[l], ffn_conv_b[l], ffn_w_down[l])
    return rmsnorm(x, final_norm)
```

```python
import numpy as np
from contextlib import ExitStack
import concourse.bass as bass
import concourse.mybir as mybir
from concourse.bass_utils import run_bass_kernel_spmd

F32 = mybir.dt.float32
BF16 = mybir.dt.bfloat16
ALU = mybir.AluOpType
AF = mybir.ActivationFunctionType
AX = mybir.AxisListType

D = 1024
KC = 8
MEM = 256
DIN = 6664
DFF = 3072
EPS = 1e-6
P = 128
ST = 512
NSUB = ST // P
import os as _os
SAME_ENG_SYNC = _os.environ.get("K_SES", "1") == "1"
GELU_C = 0.7978845608028654


class Res:
    __slots__ = ("w", "r")

    def __init__(self):
        self.w = None
        self.r = {}


class Tl:
    def __init__(self, S, name, shape, dtype, psum=False):
        if psum:
            self.t = S.es.enter_context(S.nc.psum_tensor(name, shape, dtype))
        else:
            self.t = S.es.enter_context(S.nc.sbuf_tensor(name, shape, dtype))
        self.res = Res()

    def __getitem__(self, k):
        return self.t[k]


def _res(x):
    return x.res if isinstance(x, Tl) else x


class Sched:
    NDS = 40

    def __init__(self, nc, es):
        self.nc = nc
        self.es = es
        self.names = ["pe", "act", "dve", "pool", "sp"]
        self.esem = {k: es.enter_context(nc.semaphore("sem_" + k)) for k in self.names}
        self.ecnt = {k: 0 for k in self.names}
        self.prog = {k: [] for k in self.names}
        self.seen = {k: {} for k in self.names}
        self.dsem = [es.enter_context(nc.semaphore("dsem%d" % i)) for i in range(self.NDS)]
        self.dcnt = [0] * self.NDS
        self.dnext = 0
        self.psb = [Tl(self, "psb%d" % i, [P, 512], F32, psum=True) for i in range(8)]
        self.psn = 0
        self.psmod = 8
        self.ninstr = 0

    def psum(self):
        self.psn = self.psn % self.psmod
        t = self.psb[self.psn]
        self.psn = (self.psn + 1) % self.psmod
        return t

    def _wait(self, e, toks):
        for (sem, val, key) in toks:
            if self.seen[e].get(key, 0) < val:
                self.prog[e].append(lambda eng, s=sem, v=val: eng.wait_ge(s, v))
                self.seen[e][key] = val
                self.ninstr += 1

    def _deps(self, e, outs, ins):
        toks = []
        for r in ins:
            r = _res(r)
            if r.w is not None:
                toks.append(r.w)
        for r in outs:
            r = _res(r)
            if r.w is not None:
                toks.append(r.w)
            toks.extend(r.r.values())
        if e == "pe" or not SAME_ENG_SYNC:
            toks = [t for t in toks if t[2] != e]
        self._wait(e, toks)

    def _mark(self, tok, outs, ins):
        for r in ins:
            _res(r).r[tok[2]] = tok
        for r in outs:
            r = _res(r)
            r.w = tok
            r.r = {}

    def op(self, e, fn, outs=(), ins=()):
        self._deps(e, outs, ins)
        self.ecnt[e] += 1
        sem = self.esem[e]
        self.prog[e].append(lambda eng, f=fn, s=sem: f(eng).then_inc(s, 1))
        self.ninstr += 1
        self._mark((sem, self.ecnt[e], e), outs, ins)

    def dma(self, e, out_ap, in_ap, outs=(), ins=()):
        k = self.dnext
        self.dnext = (k + 1) % self.NDS
        key = "d%d" % k
        if self.dcnt[k] > 0:
            self._wait(e, [(self.dsem[k], self.dcnt[k], key)])
        self._deps(e, outs, ins)
        self.dcnt[k] += 16
        sem = self.dsem[k]
        self.prog[e].append(lambda eng, o=out_ap, i=in_ap, s=sem: eng.dma_start(out=o, in_=i).then_inc(s, 16))
        self.ninstr += 1
        self._mark((sem, self.dcnt[k], key), outs, ins)

    def barrier(self):
        toks = [(self.esem[k], self.ecnt[k], k) for k in self.names if self.ecnt[k] > 0]
        toks += [(self.dsem[k], self.dcnt[k], "d%d" % k) for k in range(self.NDS) if self.dcnt[k] > 0]
        for e in self.names:
            self._wait(e, [t for t in toks if t[2] != e])

    def emit(self):
        blk = self.es.enter_context(self.nc.Block())

        @blk.tensor
        def _(e):
            for f in self.prog["pe"]:
                f(e)

        @blk.scalar
        def _(e):
            for f in self.prog["act"]:
                f(e)

        @blk.vector
        def _(e):
            for f in self.prog["dve"]:
                f(e)

        @blk.gpsimd
        def _(e):
            for f in self.prog["pool"]:
                f(e)

        @blk.sync
        def _(e):
            for f in self.prog["sp"]:
                f(e)


WEIGHT_NAMES = ["mix_norm", "w_in", "b_gate", "a_conv_w", "a_conv_b", "a_w_r", "a_b_r", "a_w_i", "a_b_i",
                "a_lam", "b_conv_w", "b_a_log", "b_dt_bias", "b_norm", "c_lam_re", "c_lam_im", "c_log_dt",
                "c_b_re", "c_b_im", "c_c_re", "c_c_im", "c_d", "c_glu_w", "c_glu_b", "w_branch", "w_out",
                "xa_norm", "mem_norm", "xa_w_q", "xa_w_kv", "xa_w_o", "ffn_norm", "ffn_w_up", "ffn_conv_w",
                "ffn_conv_b", "ffn_w_down", "final_norm"]


def host_consts():
    i = np.arange(P)
    c = {}
    c["ident"] = np.eye(P, dtype=np.float32)
    c["uincl"] = (i[:, None] <= i[None, :]).astype(np.float32)
    c["lstrict"] = (i[:, None] > i[None, :]).astype(np.float32)
    c["mincl_t"] = (i[:, None] <= i[None, :]).astype(np.float32)
    lv = []
    for b in [1, 2, 4, 8, 16, 32, 64]:
        m = ((i[:, None] // (2 * b) == i[None, :] // (2 * b)) & (i[:, None] % (2 * b) >= b)
             & (i[None, :] % (2 * b) < b)).astype(np.float32)
        lv.append(m)
    c["lvmask"] = np.stack(lv + [m.T for m in lv], 0).astype(np.float32)
    c["svec"] = np.tile(np.arange(P, dtype=np.float32)[None, :], (P, 1))
    c["ones"] = np.ones((P, P), np.float32)
    return c


class Prog:
    def __init__(self, cfg):
        self.cfg = cfg
        self.L = cfg["depth"]
        self.SEQ = cfg["seq"]
        self.NSEQ = cfg["nseq"]
        self.NTOK = self.SEQ * self.NSEQ
        self.NST = self.SEQ // ST
        import os
        dflt = os.environ.get("K_PHASES", "A,B,C,M2,X,F1,F2").split(",")
        self.phases = cfg.get("phases", dflt)
        self.build()

    def build(self):
        nc = bass.Bass("TRN2", target_bir_lowering=False)
        self.nc = nc
        L, NTOK = self.L, self.NTOK
        shp = {"mix_norm": [L, D], "w_in": [L, D, DIN], "b_gate": [L, 3 * D], "a_conv_w": [L, 4, 512],
               "a_conv_b": [L, 512], "a_w_r": [L, 8, 64, 64], "a_b_r": [L, 512], "a_w_i": [L, 8, 64, 64],
               "a_b_i": [L, 512], "a_lam": [L, 512], "b_conv_w": [L, 4, 1536], "b_a_log": [L, 4],
               "b_dt_bias": [L, 4], "b_norm": [L, 128], "c_lam_re": [L, 32, 64], "c_lam_im": [L, 32, 64],
               "c_log_dt": [L, 32], "c_b_re": [L, 32, 64, 16], "c_b_im": [L, 32, 64, 16],
               "c_c_re": [L, 32, 16, 64], "c_c_im": [L, 32, 16, 64], "c_d": [L, 512],
               "c_glu_w": [L, 512, 512], "c_glu_b": [L, 512], "w_branch": [L, 3, 512, D], "w_out": [L, D, D],
               "xa_norm": [L, D], "mem_norm": [L, D], "xa_w_q": [L, D, D], "xa_w_kv": [L, D, 2 * D],
               "xa_w_o": [L, D, D], "ffn_norm": [L, D], "ffn_w_up": [L, D, 2 * DFF],
               "ffn_conv_w": [L, 3, 2 * DFF], "ffn_conv_b": [L, 2 * DFF], "ffn_w_down": [L, DFF, D],
               "final_norm": [D]}
        self.W = {k: nc.dram_tensor(k, shp[k], F32, kind="ExternalInput").ap() for k in WEIGHT_NAMES}
        self.x_in = nc.dram_tensor("x", [NTOK, D], F32, kind="ExternalInput").ap()
        self.mem_in = nc.dram_tensor("mem", [self.NSEQ * MEM, D], F32, kind="ExternalInput").ap()
        hc = host_consts()
        self.C = {k: nc.dram_tensor("c_" + k, list(v.shape), F32, kind="ExternalInput").ap() for k, v in hc.items()}
        self.out = nc.dram_tensor("out", [NTOK, D], F32, kind="ExternalOutput").ap()
        self.xT = nc.dram_tensor("xT_s", [D, NTOK], F32, kind="Internal").ap()
        self.hTs = nc.dram_tensor("hT_s", [D, NTOK], BF16, kind="Internal").ap()
        self.yT = nc.dram_tensor("yT_s", [1536, NTOK], BF16, kind="Internal").ap()
        self.dbg = self.cfg.get("debug", False)
        if self.dbg:
            self.dbg_y = nc.dram_tensor("dbg_y", [1536, NTOK], BF16, kind="ExternalOutput").ap()
            self.dbg_x = nc.dram_tensor("dbg_x", [D, NTOK], F32, kind="ExternalOutput").ap()
        nsup = NTOK // ST
        self.r_xT = [Res() for _ in range(nsup)]
        self.r_hT = [Res() for _ in range(nsup)]
        self.r_yT = [[Res() for _ in range(nsup)] for _ in range(3)]

        with ExitStack() as es:
            es.enter_context(nc.allow_non_contiguous_dma(reason="small parameter vectors / layout loads"))
            S = Sched(nc, es)
            self.S = S
            self.alloc()
            self.load_consts()
            self.phase_in()
            for l in range(L):
                for ph in self.phases:
                    S.barrier()
                    getattr(self, "phase_" + ph)(l)
            S.barrier()
            if self.dbg:
                self.phase_dbg()
            self.phase_out()
            S.barrier()
            S.emit()
        self.ninstr = S.ninstr

    def alloc(self):
        S = self.S
        T = lambda n, s, d=F32: Tl(S, n, s, d)
        self.warena = T("warena", [P, 44 * 1024], BF16)
        self.ident = T("ident", [P, P])
        self.identb = T("identb", [P, P], BF16)
        self.ones_b = T("ones_b", [P, P], BF16)
        self.ones_f = T("ones_f", [P, P])
        self.uincl = T("uincl", [P, P])
        self.lstrict = T("lstrict", [P, P])
        self.mincl_t = T("mincl_t", [P, P])
        self.lvmask = T("lvmask", [P, 14, P])
        self.svec = T("svec", [P, P])
        self.epsc = T("epsc", [P, 4])
        self.xs = T("xs", [P, KC, ST])
        self.rstd = T("rstd", [P, ST])
        self.hT = T("hT", [P, KC, ST], BF16)
        self.gv = T("gv", [P, KC])
        self.vec = T("vec", [P, 256])
        self.work = [T("work%d" % i, [P, ST]) for i in range(10)]
        self.workb = [T("workb%d" % i, [P, ST], BF16) for i in range(6)]
        self.big_b = T("big_b", [P, 12, ST], BF16)
        self.xsq = self.big_b
        self.big_c = T("big_c", [P, KC, ST], BF16)
        self.tok = T("tok", [P, D])
        self.kT = [T("kT0", [P, KC, MEM], BF16)] * self.NSEQ
        self.vv = [T("vv0", [P, 2, D], BF16)] * self.NSEQ
        self.tail = T("tail", [P, 24, 2])
        self.halo = T("halo", [P, 16, 4])
        self.hprev = T("hprev", [P, 4])
        self.ubuf = [T("ubuf%d" % i, [P, ST + 4]) for i in range(4)]
        self.cset = T("cset", [P, 24, 16])
        self.bx = [T("bx%d" % i, [P, ST]) for i in range(2)]
        self.Sst = T("Sst", [P, 4, P])
        self.bnb = T("bnb", [P, P])
        self.bgt = T("bgt", [P, 4, 12])
        self.eg = T("eg", [P, 24])
        self.ccar = T("ccar", [P, 4, 16])
        self.ctmp = Res()

    def warena_view(self, off, shape):
        n = int(np.prod(shape))
        ap = self.warena[:, off:off + n]
        if len(shape) == 2:
            ap = ap.rearrange("p (a b) -> p a b", a=shape[0])
        return ap, off + n

    def load_consts(self):
        S = self.S
        for nm, tl in [("ident", self.ident), ("uincl", self.uincl), ("lstrict", self.lstrict),
                       ("mincl_t", self.mincl_t), ("svec", self.svec), ("ones", self.ones_f)]:
            S.dma("sp", tl[:], self.C[nm], outs=[tl])
        S.dma("sp", self.lvmask[:], self.C["lvmask"].rearrange("l p q -> p l q"), outs=[self.lvmask])
        S.op("dve", lambda e: e.tensor_copy(out=self.identb[:], in_=self.ident[:]), outs=[self.identb], ins=[self.ident])
        S.op("pool", lambda e: e.memset(self.epsc[:, 0:1], EPS), outs=[self.epsc])
        S.op("pool", lambda e: e.memset(self.epsc[:, 1:2], 1.0), outs=[self.epsc])
        S.op("dve", lambda e: e.tensor_scalar(out=self.ones_b[:], in0=self.ones_f[:], scalar1=1.0 / D, scalar2=None,
                                              op0=ALU.mult), outs=[self.ones_b], ins=[self.ones_f])

    def load_w(self, dst_ap, src_ap, dst_res):
        S = self.S
        nk = src_ap.shape[0] // P
        v = src_ap.rearrange("(kc p) n -> p kc n", p=P)
        for kc in range(nk):
            S.dma("pool", dst_ap[:, kc, :], v[:, kc, :], outs=[dst_res])

    def load_vec(self, dst_ap, src_ap, dst_res):
        self.S.dma("sp", dst_ap, src_ap.rearrange("(c p) -> p c", p=P), outs=[dst_res])

    def mm(self, ps, ps_ap, lhsT, rhs, start, stop, ins):
        self.S.op("pe", lambda e: e.matmul(ps_ap, lhsT=lhsT, rhs=rhs, start=start, stop=stop), outs=[ps], ins=ins)

    def load_x_norm(self, sup, gres_ready=True, want_h=True, from_hts=False):
        S = self.S
        c0 = sup * ST
        xv = self.xT.rearrange("(kc p) t -> p kc t", p=P)
        S.dma("sp", self.xs[:], xv[:, :, c0:c0 + ST], outs=[self.xs], ins=[self.r_xT[sup]])
        if not want_h:
            return
        if from_hts:
            hv = self.hTs.rearrange("(kc p) t -> p kc t", p=P)
            S.dma("sp", self.hT[:], hv[:, :, c0:c0 + ST], outs=[self.hT], ins=[self.r_hT[sup]])
            return
        self.norm_from_xs()

    def load_xs(self, sup):
        c0 = sup * ST
        xv = self.xT.rearrange("(kc p) t -> p kc t", p=P)
        self.S.dma("sp", self.xs[:], xv[:, :, c0:c0 + ST], outs=[self.xs], ins=[self.r_xT[sup]])

    def load_norm_prefetch(self, sup):
        nsup = self.NTOK // ST
        if sup == 0:
            self.load_xs(0)
        self.norm_from_xs()
        if sup + 1 < nsup:
            self.load_xs(sup + 1)

    def norm_from_xs(self):
        S = self.S
        S.op("act", lambda e: e.activation(out=self.xsq[:, 0:KC, :], in_=self.xs[:], func=AF.Square), outs=[self.xsq], ins=[self.xs])
        ps = S.psum()
        for kc in range(KC):
            self.mm(ps, ps[:, :], self.ones_b[:], self.xsq[:, kc, :], kc == 0, kc == KC - 1, [self.ones_b, self.xsq])
        S.op("act", lambda e: e.activation(out=self.rstd[:], in_=ps[:, :], func=AF.Sqrt, bias=self.epsc[:, 0:1], scale=1.0), outs=[self.rstd], ins=[ps, self.epsc])
        S.op("dve", lambda e: e.reciprocal(out=self.rstd[:], in_=self.rstd[:]), outs=[self.rstd], ins=[self.rstd])
        for kc in range(KC):
            eng = "dve"
            S.op(eng, lambda e, kc=kc: e.scalar_tensor_tensor(out=self.hT[:, kc, :], in0=self.xs[:, kc, :],
                                                               scalar=self.gv[:, kc:kc + 1], in1=self.rstd[:],
                                                               op0=ALU.mult, op1=ALU.mult),
                 outs=[self.hT], ins=[self.xs, self.gv, self.rstd])

    def store_x(self, sup):
        c0 = sup * ST
        xv = self.xT.rearrange("(kc p) t -> p kc t", p=P)
        self.S.dma("sp", xv[:, :, c0:c0 + ST], self.xs[:], outs=[self.r_xT[sup]], ins=[self.xs])

    def gelu(self, out_ap, out_res, in_ap, in_res, wa, wb):
        S = self.S
        n = in_ap.shape[-1]
        S.op("act", lambda e: e.activation(out=wa[:, :n], in_=in_ap, func=AF.Square), outs=[wa], ins=[in_res])
        S.op("pool", lambda e: e.tensor_scalar(out=wa[:, :n], in0=wa[:, :n], scalar1=0.044715, scalar2=1.0, op0=ALU.mult,
                                               op1=ALU.add), outs=[wa], ins=[wa])
        S.op("pool", lambda e: e.tensor_tensor(out=wb[:, :n], in0=wa[:, :n], in1=in_ap, op=ALU.mult), outs=[wb], ins=[wa, in_res])
        S.op("act", lambda e: e.activation(out=wb[:, :n], in_=wb[:, :n], func=AF.Sigmoid, scale=2.0 * GELU_C), outs=[wb], ins=[wb])
        S.op("dve", lambda e: e.tensor_tensor(out=out_ap, in0=wb[:, :n], in1=in_ap, op=ALU.mult), outs=[out_res], ins=[wb, in_res])

    def phase_in(self):
        S = self.S
        for sup in range(self.NTOK // ST):
            for sub in range(NSUB):
                t0 = sup * ST + sub * P
                S.dma("sp", self.tok[:], self.x_in[t0:t0 + P, :], outs=[self.tok])
                for half in range(2):
                    ps = S.psum()
                    for q in range(4):
                        kc = half * 4 + q
                        S.op("pe", lambda e, ps=ps, q=q, kc=kc: e.transpose(out=ps[:, q * P:(q + 1) * P],
                                                                            in_=self.tok[:, kc * P:(kc + 1) * P],
                                                                            identity=self.ident[:]),
                             outs=[ps], ins=[self.tok, self.ident])
                    eng = "act" if half == 0 else "dve"
                    if eng == "act":
                        S.op("act", lambda e, ps=ps, half=half, sub=sub: e.copy(
                            out=self.xs[:, half * 4:half * 4 + 4, sub * P:(sub + 1) * P],
                            in_=ps[:, :].rearrange("p (a b) -> p a b", a=4)), outs=[self.xs], ins=[ps])
                    else:
                        S.op("dve", lambda e, ps=ps, half=half, sub=sub: e.tensor_copy(
                            out=self.xs[:, half * 4:half * 4 + 4, sub * P:(sub + 1) * P],
                            in_=ps[:, :].rearrange("p (a b) -> p a b", a=4)), outs=[self.xs], ins=[ps])
            self.store_x(sup)

    def phase_out(self):
        S = self.S
        self.load_vec(self.gv[:], self.W["final_norm"], self.gv)
        hf = self.work
        for sup in range(self.NTOK // ST):
            c0 = sup * ST
            xv = self.xT.rearrange("(kc p) t -> p kc t", p=P)
            S.dma("sp", self.xs[:], xv[:, :, c0:c0 + ST], outs=[self.xs], ins=[self.r_xT[sup]])
            S.op("act", lambda e: e.activation(out=self.xsq[:, 0:KC, :], in_=self.xs[:], func=AF.Square), outs=[self.xsq], ins=[self.xs])
            ps = S.psum()
            for kc in range(KC):
                self.mm(ps, ps[:, :], self.ones_b[:], self.xsq[:, kc, :], kc == 0, kc == KC - 1, [self.ones_b, self.xsq])
            S.op("act", lambda e, ps=ps: e.activation(out=self.rstd[:], in_=ps[:, :], func=AF.Sqrt, bias=self.epsc[:, 0:1], scale=1.0), outs=[self.rstd], ins=[ps, self.epsc])
            S.op("dve", lambda e: e.reciprocal(out=self.rstd[:], in_=self.rstd[:]), outs=[self.rstd], ins=[self.rstd])
            for kc in range(KC):
                eng = "dve"
                S.op(eng, lambda e, kc=kc: e.scalar_tensor_tensor(out=hf[kc][:, :], in0=self.xs[:, kc, :],
                                                                   scalar=self.gv[:, kc:kc + 1], in1=self.rstd[:],
                                                                   op0=ALU.mult, op1=ALU.mult),
                     outs=[hf[kc]], ins=[self.xs, self.gv, self.rstd])
            for sub in range(NSUB):
                for half in range(2):
                    ps = S.psum()
                    for q in range(4):
                        kc = half * 4 + q
                        S.op("pe", lambda e, ps=ps, q=q, kc=kc, sub=sub: e.transpose(
                            out=ps[:, q * P:(q + 1) * P], in_=hf[kc][:, sub * P:(sub + 1) * P], identity=self.ident[:]),
                             outs=[ps], ins=[hf[kc], self.ident])
                    if half == 0:
                        S.op("act", lambda e, ps=ps: e.copy(out=self.tok[:, 0:512], in_=ps[:, :]), outs=[self.tok], ins=[ps])
                    else:
                        S.op("dve", lambda e, ps=ps: e.tensor_copy(out=self.tok[:, 512:1024], in_=ps[:, :]), outs=[self.tok], ins=[ps])
                t0 = c0 + sub * P
                S.dma("sp", self.out[t0:t0 + P, :], self.tok[:], ins=[self.tok])

    def phase_dbg(self):
        S = self.S
        for sup in range(self.NTOK // ST):
            c0 = sup * ST
            xv = self.xT.rearrange("(kc p) t -> p kc t", p=P)
            S.dma("sp", self.xs[:], xv[:, :, c0:c0 + ST], outs=[self.xs], ins=[self.r_xT[sup]])
            S.dma("sp", self.dbg_x.rearrange("(kc p) t -> p kc t", p=P)[:, :, c0:c0 + ST], self.xs[:], ins=[self.xs])
            yv = self.yT.rearrange("(c p) t -> p c t", p=P)
            S.dma("sp", self.big_b[:], yv[:, :, c0:c0 + ST], outs=[self.big_b], ins=[r[sup] for r in self.r_yT])
            S.dma("sp", self.dbg_y.rearrange("(c p) t -> p c t", p=P)[:, :, c0:c0 + ST], self.big_b[:], ins=[self.big_b])

    def ffn_load(self, l, j):
        S = self.S
        off = self.w_off
        HW = DFF // 2
        self.wup, off = self.warena_view(off, [KC, 2 * HW])
        self.wdn, off = self.warena_view(off, [12, D])
        wu = self.W["ffn_w_up"][l]
        self.load_w(self.wup[:, :, 0:HW], wu[:, j * HW:(j + 1) * HW], self.warena)
        self.load_w(self.wup[:, :, HW:2 * HW], wu[:, DFF + j * HW:DFF + (j + 1) * HW], self.warena)
        self.load_w(self.wdn, self.W["ffn_w_down"][l][j * HW:(j + 1) * HW, :], self.warena)
        cw = self.W["ffn_conv_w"][l]
        cb = self.W["ffn_conv_b"][l]
        for i in range(3):
            for g in range(2):
                src = cw[i, g * DFF + j * HW: g * DFF + (j + 1) * HW]
                self.load_vec(self.vec[:, i * 24 + g * 12: i * 24 + g * 12 + 12], src, self.vec)
        for g in range(2):
            src = cb[g * DFF + j * HW: g * DFF + (j + 1) * HW]
            self.load_vec(self.vec[:, 72 + g * 12: 72 + g * 12 + 12], src, self.vec)

    def ffn_super(self, sup, first_in_seq, mid_hook=None):
        S = self.S
        HW = DFF // 2
        mT = self.big_b
        if first_in_seq:
            S.op("pool", lambda e: e.memset(self.tail[:], 0.0), outs=[self.tail])
        for c in range(12):
            acc = []
            for g in range(2):
                idx = g * 12 + c
                ps = S.psum()
                for kc in range(KC):
                    self.mm(ps, ps[:, :], self.wup[:, kc, g * HW + c * P: g * HW + (c + 1) * P], self.hT[:, kc, :],
                            kc == 0, kc == KC - 1, [self.warena, self.hT])
                ub = self.ubuf[g * 2 + (c % 2)]
                S.op("act", lambda e, ub=ub, ps=ps: e.copy(out=ub[:, 2:2 + ST], in_=ps[:, :]), outs=[ub], ins=[ps])
                S.op("pool", lambda e, ub=ub, idx=idx: e.tensor_copy(out=ub[:, 0:2], in_=self.tail[:, idx, :]), outs=[ub], ins=[self.tail])
                a = self.work[g * 2 + (c % 2)]
                w0 = self.vec[:, 0 * 24 + idx: 0 * 24 + idx + 1]
                w1 = self.vec[:, 1 * 24 + idx: 1 * 24 + idx + 1]
                w2 = self.vec[:, 2 * 24 + idx: 2 * 24 + idx + 1]
                bb = self.vec[:, 72 + idx: 72 + idx + 1]
                S.op("act", lambda e, a=a, ub=ub, w2=w2, bb=bb: e.activation(out=a[:, :], in_=ub[:, 2:2 + ST], func=AF.Identity,
                                                                             scale=w2, bias=bb), outs=[a], ins=[ub, self.vec])
                S.op("dve", lambda e, a=a, ub=ub, w1=w1: e.scalar_tensor_tensor(out=a[:, :], in0=ub[:, 1:1 + ST], scalar=w1,
                                                                                in1=a[:, :], op0=ALU.mult, op1=ALU.add),
                     outs=[a], ins=[ub, self.vec, a])
                S.op("dve", lambda e, a=a, ub=ub, w0=w0: e.scalar_tensor_tensor(out=a[:, :], in0=ub[:, 0:ST], scalar=w0,
                                                                                in1=a[:, :], op0=ALU.mult, op1=ALU.add),
                     outs=[a], ins=[ub, self.vec, a])
                S.op("pool", lambda e, ub=ub, idx=idx: e.tensor_copy(out=self.tail[:, idx, :], in_=ub[:, ST:ST + 2]), outs=[self.tail], ins=[ub])
                acc.append(a)
            gl = self.work[4 + (c % 2)]
            self.gelu(gl[:, :], gl, acc[0][:, :], acc[0], self.work[6 + (c % 2)], self.work[8 + (c % 2)])
            S.op("dve", lambda e, c=c, gl=gl, v=acc[1]: e.tensor_tensor(out=mT[:, c, :], in0=gl[:, :], in1=v[:, :], op=ALU.mult),
                 outs=[mT], ins=[gl, acc[1]])
        if mid_hook is not None:
            mid_hook()
        for ec in range(KC):
            ps = S.psum()
            for c in range(12):
                self.mm(ps, ps[:, :], self.wdn[:, c, ec * P:(ec + 1) * P], mT[:, c, :], c == 0, c == 11, [self.warena, mT])
            S.op("dve", lambda e, ec=ec, ps=ps: e.tensor_tensor(out=self.xs[:, ec, :], in0=self.xs[:, ec, :], in1=ps[:, :], op=ALU.add),
                 outs=[self.xs], ins=[self.xs, ps])

    def phase_F1(self, l):
        self.phase_F(l, 0)

    def phase_F2(self, l):
        self.phase_F(l, 1)

    def phase_F(self, l, j):
        self.w_off = 0
        self.ffn_load(l, j)
        self.S.barrier()
        nsup = self.NTOK // ST
        hv = self.hTs.rearrange("(kc p) t -> p kc t", p=P)

        def load_h(sp_):
            self.S.dma("sp", self.hT[:], hv[:, :, sp_ * ST:(sp_ + 1) * ST], outs=[self.hT], ins=[self.r_hT[sp_]])

        load_h(0)
        for sup in range(nsup):
            self.load_xs(sup)
            self.ffn_super(sup, sup % self.NST == 0, mid_hook=(lambda sup=sup: load_h(sup + 1)) if sup + 1 < nsup else None)
            self.store_x(sup)

    def do_KV(self, l, wkv, s):
        S = self.S
        gb = self.tok
        memn = [self.work[0], self.work[1]]
        for s in [s]:
            for half in range(2):
                r0 = s * MEM + half * P
                mt = self.xs
                mview = self.xs[:, 0:2, :].rearrange("p a b -> p (a b)")
                gview = self.xs[:, 2:4, :].rearrange("p a b -> p (a b)")
                S.dma("sp", mview, self.mem_in[r0:r0 + P, :], outs=[self.xs])
                S.dma("sp", gview, self.W["mem_norm"][l:l + 1, :].broadcast_to([P, D]), outs=[self.xs])
                sq = self.xs[:, 4:6, :].rearrange("p a b -> p (a b)")
                S.op("act", lambda e, sq=sq, mview=mview: e.activation(out=sq, in_=mview, func=AF.Square, accum_out=self.vec[:, 200:201]),
                     outs=[self.xs, self.vec], ins=[self.xs])
                S.op("dve", lambda e: e.tensor_scalar(out=self.vec[:, 201:202], in0=self.vec[:, 200:201], scalar1=1.0 / D, scalar2=EPS,
                                                      op0=ALU.mult, op1=ALU.add), outs=[self.vec], ins=[self.vec])
                S.op("act", lambda e: e.activation(out=self.vec[:, 201:202], in_=self.vec[:, 201:202], func=AF.Sqrt), outs=[self.vec], ins=[self.vec])
                S.op("dve", lambda e: e.reciprocal(out=self.vec[:, 201:202], in_=self.vec[:, 201:202]), outs=[self.vec], ins=[self.vec])
                S.op("dve", lambda e, mview=mview, gview=gview: e.scalar_tensor_tensor(out=self.tok[:], in0=mview, scalar=self.vec[:, 201:202],
                                                                                     in1=gview, op0=ALU.mult, op1=ALU.mult),
                     outs=[self.tok], ins=[self.xs, self.vec])
                for h2 in range(2):
                    ps = S.psum()
                    for q in range(4):
                        kc = h2 * 4 + q
                        S.op("pe", lambda e, ps=ps, q=q, kc=kc: e.transpose(out=ps[:, q * P:(q + 1) * P], in_=self.tok[:, kc * P:(kc + 1) * P],
                                                                            identity=self.ident[:]), outs=[ps], ins=[self.tok, self.ident])
                    S.op("act", lambda e, ps=ps, h2=h2, half=half: e.copy(out=self.hT[:, h2 * 4:h2 * 4 + 4, half * P:(half + 1) * P],
                                                                        in_=ps[:, :].rearrange("p (a b) -> p a b", a=4)), outs=[self.hT], ins=[ps])
            for c in range(KC):
                ps = S.psum()
                for kc in range(KC):
                    self.mm(ps, ps[:, 0:MEM], wkv[:, kc, c * P:(c + 1) * P], self.hT[:, kc, 0:MEM], kc == 0, kc == KC - 1, [self.warena, self.hT])
                S.op("act", lambda e, ps=ps, c=c, s=s: e.copy(out=self.kT[s][:, c, :], in_=ps[:, 0:MEM]), outs=[self.kT[s]], ins=[ps])
            for half in range(2):
                for cb in range(2):
                    ps = S.psum()
                    for kc in range(KC):
                        self.mm(ps, ps[:, :], self.hT[:, kc, half * P:(half + 1) * P], wkv[:, kc, D + cb * 512: D + (cb + 1) * 512],
                                kc == 0, kc == KC - 1, [self.warena, self.hT])
                    S.op("dve", lambda e, ps=ps, half=half, cb=cb, s=s: e.tensor_copy(out=self.vv[s][:, half, cb * 512:(cb + 1) * 512], in_=ps[:, :]),
                         outs=[self.vv[s]], ins=[ps])

    def phase_X(self, l):
        S = self.S
        off = 0
        wq, off = self.warena_view(off, [KC, D])
        wo, off = self.warena_view(off, [KC, D])
        self.load_w(wq, self.W["xa_w_q"][l], self.warena)
        self.load_w(wo, self.W["xa_w_o"][l], self.warena)
        wkv, off = self.warena_view(off, [KC, 2 * D])
        self.load_w(wkv, self.W["xa_w_kv"][l], self.warena)
        gx = self.vec[:, 208:216]
        gf = self.vec[:, 216:224]
        self.load_vec(gx, self.W["xa_norm"][l], self.vec)
        self.load_vec(gf, self.W["ffn_norm"][l], self.vec)
        qT = self.big_c
        oT = self.big_c
        S.barrier()
        for sup in range(self.NTOK // ST):
            s = sup // self.NST
            if sup % self.NST == 0:
                self.do_KV(l, wkv, s)
            S.op("pool", lambda e: e.tensor_copy(out=self.gv[:], in_=gx), outs=[self.gv], ins=[self.vec])
            self.load_x_norm(sup)
            for c in range(KC):
                ps = S.psum()
                for kc in range(KC):
                    self.mm(ps, ps[:, :], wq[:, kc, c * P:(c + 1) * P], self.hT[:, kc, :], kc == 0, kc == KC - 1, [self.warena, self.hT])
                S.op("act", lambda e, ps=ps, c=c: e.copy(out=qT[:, c, :], in_=ps[:, :]), outs=[qT], ins=[ps])
            oTn = self.big_b
            for sub in range(NSUB):
                for h in range(4):
                    ps = S.psum()
                    for dc in range(2):
                        self.mm(ps, ps[:, 0:MEM], qT[:, 2 * h + dc, sub * P:(sub + 1) * P], self.kT[s][:, 2 * h + dc, :], dc == 0, dc == 1,
                                [qT, self.kT[s]])
                    mx = self.vec[:, 224 + h: 225 + h]
                    sm = self.vec[:, 228 + h: 229 + h]
                    S.op("dve", lambda e, ps=ps, mx=mx: e.reduce_max(out=mx, in_=ps[:, 0:MEM], axis=AX.X), outs=[self.vec], ins=[ps])
                    S.op("dve", lambda e, mx=mx: e.tensor_scalar(out=mx, in0=mx, scalar1=-1.0 / 16.0, scalar2=None, op0=ALU.mult),
                         outs=[self.vec], ins=[self.vec])
                    pw = self.work[h]
                    S.op("act", lambda e, ps=ps, pw=pw, mx=mx, sm=sm: e.activation(out=pw[:, 0:MEM], in_=ps[:, 0:MEM], func=AF.Exp, scale=1.0 / 16.0,
                                                                                   bias=mx, accum_out=sm), outs=[pw, self.vec], ins=[ps, self.vec])
                    S.op("dve", lambda e, sm=sm: e.reciprocal(out=sm, in_=sm), outs=[self.vec], ins=[self.vec])
                    S.op("dve", lambda e, pw=pw, sm=sm: e.tensor_scalar(out=pw[:, 0:MEM], in0=pw[:, 0:MEM], scalar1=sm, scalar2=None, op0=ALU.mult),
                         outs=[pw], ins=[pw, self.vec])
                    ps2 = S.psum()
                    for mh in range(2):
                        S.op("pe", lambda e, ps2=ps2, pw=pw, mh=mh: e.transpose(out=ps2[:, mh * P:(mh + 1) * P], in_=pw[:, mh * P:(mh + 1) * P],
                                                                                identity=self.ident[:]), outs=[ps2], ins=[pw, self.ident])
                    pT = self.workb[h]
                    S.op("act", lambda e, ps2=ps2, pT=pT: e.copy(out=pT[:, 0:MEM], in_=ps2[:, 0:MEM]), outs=[pT], ins=[ps2])
                    ps3 = S.psum()
                    for dc in range(2):
                        for mh in range(2):
                            self.mm(ps3, ps3[:, dc * P:(dc + 1) * P], self.vv[s][:, mh, (2 * h + dc) * P:(2 * h + dc + 1) * P],
                                    pT[:, mh * P:(mh + 1) * P], mh == 0, mh == 1, [self.vv[s], pT])
                    S.op("dve", lambda e, ps3=ps3, h=h, sub=sub: e.tensor_copy(out=oTn[:, 2 * h:2 * h + 2, sub * P:(sub + 1) * P],
                                                                            in_=ps3[:, 0:2 * P].rearrange("p (a b) -> p a b", a=2)),
                         outs=[oTn], ins=[ps3])
            for c in range(KC):
                ps = S.psum()
                for kc in range(KC):
                    self.mm(ps, ps[:, :], wo[:, kc, c * P:(c + 1) * P], oTn[:, kc, :], kc == 0, kc == KC - 1, [self.warena, oTn])
                S.op("dve", lambda e, ps=ps, c=c: e.tensor_tensor(out=self.xs[:, c, :], in0=self.xs[:, c, :], in1=ps[:, :], op=ALU.add),
                     outs=[self.xs], ins=[self.xs, ps])
            S.op("pool", lambda e: e.tensor_copy(out=self.gv[:], in_=gf), outs=[self.gv], ins=[self.vec])
            self.norm_from_xs()
            c0 = sup * ST
            hv = self.hTs.rearrange("(kc p) t -> p kc t", p=P)
            S.dma("sp", hv[:, :, c0:c0 + ST], self.hT[:], outs=[self.r_hT[sup]], ins=[self.hT])
            self.store_x(sup)

    def phase_M2(self, l):
        S = self.S
        off = 0
        wg, off = self.warena_view(off, [KC, 3 * D])
        wb, off = self.warena_view(off, [12, D])
        wo, off = self.warena_view(off, [KC, D])
        import os
        SK = os.environ.get("M2_SKIP", "")
        if "g" not in SK:
            self.load_w(wg, self.W["w_in"][l][:, 3592:DIN], self.warena)
        if "b" not in SK:
            self.load_w(wb, self.W["w_branch"][l].rearrange("k c d -> (k c) d"), self.warena)
        if "o" not in SK:
            self.load_w(wo, self.W["w_out"][l], self.warena)
        self.load_vec(self.gv[:], self.W["mix_norm"][l], self.gv)
        bg = self.vec[:, 0:24]
        if "v" not in SK:
            self.load_vec(bg, self.W["b_gate"][l], self.vec)
        if "c" in SK:
            return
        yT = self.big_b
        mg = self.big_c
        S.barrier()
        for sup in range(self.NTOK // ST):
            c0 = sup * ST
            self.load_x_norm(sup)
            yv = self.yT.rearrange("(c p) t -> p c t", p=P)
            for k in range(3):
                S.dma("sp", yT[:, 4 * k:4 * k + 4, :], yv[:, 4 * k:4 * k + 4, c0:c0 + ST], outs=[yT], ins=[self.r_yT[k][sup]])
            for dch in range(KC):
                acc = self.work[dch % 2]
                for k in range(3):
                    ps = S.psum()
                    for kc in range(KC):
                        self.mm(ps, ps[:, :], wg[:, kc, k * D + dch * P: k * D + (dch + 1) * P], self.hT[:, kc, :], kc == 0, kc == KC - 1,
                                [self.warena, self.hT])
                    gt = self.work[2 + (k % 2)]
                    bcol = self.vec[:, k * 8 + dch: k * 8 + dch + 1]
                    S.op("act", lambda e, ps=ps, gt=gt, bcol=bcol: e.activation(out=gt[:, :], in_=ps[:, :], func=AF.Sigmoid, bias=bcol, scale=1.0),
                         outs=[gt], ins=[ps, self.vec])
                    ps2 = S.psum()
                    for c in range(4):
                        self.mm(ps2, ps2[:, :], wb[:, 4 * k + c, dch * P:(dch + 1) * P], yT[:, 4 * k + c, :], c == 0, c == 3, [self.warena, yT])
                    if k == 0:
                        S.op("dve", lambda e, acc=acc, gt=gt, ps2=ps2: e.tensor_tensor(out=acc[:, :], in0=gt[:, :], in1=ps2[:, :], op=ALU.mult),
                             outs=[acc], ins=[gt, ps2])
                    else:
                        S.op("dve", lambda e, gt=gt, ps2=ps2: e.tensor_tensor(out=gt[:, :], in0=gt[:, :], in1=ps2[:, :], op=ALU.mult),
                             outs=[gt], ins=[gt, ps2])
                        if k == 1:
                            S.op("pool", lambda e, acc=acc, gt=gt: e.tensor_tensor(out=acc[:, :], in0=acc[:, :], in1=gt[:, :], op=ALU.add),
                                 outs=[acc], ins=[acc, gt])
                        else:
                            S.op("pool", lambda e, acc=acc, gt=gt, dch=dch: e.tensor_tensor(out=mg[:, dch, :], in0=acc[:, :], in1=gt[:, :], op=ALU.add),
                                 outs=[mg], ins=[acc, gt])
            for ec in range(KC):
                ps = S.psum()
                for dch in range(KC):
                    self.mm(ps, ps[:, :], wo[:, dch, ec * P:(ec + 1) * P], mg[:, dch, :], dch == 0, dch == KC - 1, [self.warena, mg])
                S.op("dve", lambda e, ec=ec, ps=ps: e.tensor_tensor(out=self.xs[:, ec, :], in0=self.xs[:, ec, :], in1=ps[:, :], op=ALU.add),
                     outs=[self.xs], ins=[self.xs, ps])
            self.store_x(sup)

    def store_y(self, k, sup, src_ap, src_res, c, ncols=ST, coff=0):
        c0 = sup * ST + coff
        yv = self.yT.rearrange("(c p) t -> p c t", p=P)
        self.S.dma("sp", yv[:, 4 * k + c, c0:c0 + ncols], src_ap, outs=[self.r_yT[k][sup]], ins=[src_res])

    def phase_A(self, l):
        S = self.S
        off = 0
        wa, off = self.warena_view(off, [KC, 1024])
        wr, off = self.warena_view(off, [4, P])
        wi, off = self.warena_view(off, [4, P])
        self.load_w(wa, self.W["w_in"][l][:, 0:1024], self.warena)
        self.load_vec(self.gv[:], self.W["mix_norm"][l], self.gv)
        stg = self.tok
        S.op("pool", lambda e: e.memset(stg[:], 0.0), outs=[stg])
        for wi_, nm in enumerate(["a_w_r", "a_w_i"]):
            for h in range(8):
                c, hh = h // 2, h % 2
                S.dma("sp", stg[hh * 64:(hh + 1) * 64, wi_ * 512 + c * P + hh * 64: wi_ * 512 + c * P + hh * 64 + 64],
                      self.W[nm][l, h], outs=[stg])
        S.barrier()
        S.op("dve", lambda e: e.tensor_copy(out=wr, in_=stg[:, 0:512].rearrange("p (a b) -> p a b", a=4)), outs=[self.warena], ins=[stg])
        S.op("dve", lambda e: e.tensor_copy(out=wi, in_=stg[:, 512:1024].rearrange("p (a b) -> p a b", a=4)), outs=[self.warena], ins=[stg])
        V = self.vec
        cw = V[:, 0:16]
        for i in range(4):
            self.load_vec(V[:, i * 4:(i + 1) * 4], self.W["a_conv_w"][l, i], V)
        self.load_vec(V[:, 16:20], self.W["a_conv_b"][l], V)
        self.load_vec(V[:, 20:24], self.W["a_b_r"][l], V)
        self.load_vec(V[:, 24:28], self.W["a_b_i"][l], V)
        self.load_vec(V[:, 28:32], self.W["a_lam"][l], V)
        S.barrier()
        S.op("act", lambda e: e.activation(out=V[:, 32:36], in_=V[:, 28:32], func=AF.Exp, scale=-1.0), outs=[V], ins=[V])
        S.op("act", lambda e: e.activation(out=V[:, 32:36], in_=V[:, 32:36], func=AF.Ln, bias=self.epsc[:, 1:2], scale=1.0), outs=[V], ins=[V, self.epsc])
        S.op("dve", lambda e: e.tensor_scalar(out=V[:, 36:40], in0=V[:, 32:36], scalar1=-8.0, scalar2=None, op0=ALU.mult), outs=[V], ins=[V])
        S.op("dve", lambda e: e.tensor_scalar(out=V[:, 40:44], in0=V[:, 32:36], scalar1=-16.0, scalar2=None, op0=ALU.mult), outs=[V], ins=[V])
        Wk = self.work
        S.barrier()
        for sup in range(self.NTOK // ST):
            first = (sup % self.NST == 0)
            self.load_norm_prefetch(sup)
            if first:
                S.op("pool", lambda e: e.memset(self.halo[:, 0:4, :], 0.0), outs=[self.halo])
                S.op("pool", lambda e: e.memset(self.hprev[:], 0.0), outs=[self.hprev])
            for c in range(4):
                ps = S.psum()
                for kc in range(KC):
                    self.mm(ps, ps[:, :], wa[:, kc, c * P:(c + 1) * P], self.hT[:, kc, :], kc == 0, kc == KC - 1, [self.warena, self.hT])
                ub = self.ubuf[c % 2]
                S.op("act", lambda e, ub=ub, ps=ps: e.copy(out=ub[:, 3:3 + ST], in_=ps[:, :]), outs=[ub], ins=[ps])
                S.op("pool", lambda e, ub=ub, c=c: e.tensor_copy(out=ub[:, 0:3], in_=self.halo[:, c, 0:3]), outs=[ub], ins=[self.halo])
                xc = Wk[0]
                S.op("dve", lambda e, ub=ub, c=c, xc=xc: e.tensor_scalar(out=xc[:, :], in0=ub[:, 3:3 + ST], scalar1=V[:, 12 + c:13 + c],
                                                                      scalar2=V[:, 16 + c:17 + c], op0=ALU.mult, op1=ALU.add),
                     outs=[xc], ins=[ub, V])
                for i in range(3):
                    S.op("dve", lambda e, ub=ub, c=c, i=i, xc=xc: e.scalar_tensor_tensor(out=xc[:, :], in0=ub[:, i:i + ST], scalar=V[:, i * 4 + c:i * 4 + c + 1],
                                                                                      in1=xc[:, :], op0=ALU.mult, op1=ALU.add),
                         outs=[xc], ins=[ub, V, xc])
                S.op("pool", lambda e, ub=ub, c=c: e.tensor_copy(out=self.halo[:, c, 0:3], in_=ub[:, ST:ST + 3]), outs=[self.halo], ins=[ub])
                xcb = self.workb[0]
                S.op("act", lambda e, xcb=xcb, xc=xc: e.copy(out=xcb[:, :], in_=xc[:, :]), outs=[xcb], ins=[xc])
                psg = S.psum()
                for kc in range(KC):
                    self.mm(psg, psg[:, :], wa[:, kc, 512 + c * P: 512 + (c + 1) * P], self.hT[:, kc, :], kc == 0, kc == KC - 1, [self.warena, self.hT])
                ga = Wk[1]
                S.op("act", lambda e, ga=ga, psg=psg: e.copy(out=ga[:, :], in_=psg[:, :]), outs=[ga], ins=[psg])
                gg = Wk[2]
                self.gelu(gg[:, :], gg, ga[:, :], ga, Wk[3], Wk[4])
                psr = S.psum()
                self.mm(psr, psr[:, :], wr[:, c, :], xcb[:, :], True, True, [self.warena, xcb])
                psi = S.psum()
                self.mm(psi, psi[:, :], wi[:, c, :], xcb[:, :], True, True, [self.warena, xcb])
                r = Wk[5]
                ig = Wk[6]
                S.op("act", lambda e, r=r, psr=psr, c=c: e.activation(out=r[:, :], in_=psr[:, :], func=AF.Sigmoid, bias=V[:, 20 + c:21 + c], scale=1.0),
                     outs=[r], ins=[psr, V])
                S.op("act", lambda e, ig=ig, psi=psi, c=c: e.activation(out=ig[:, :], in_=psi[:, :], func=AF.Sigmoid, bias=V[:, 24 + c:25 + c], scale=1.0),
                     outs=[ig], ins=[psi, V])
                a = Wk[7]
                a2 = Wk[8]
                S.op("act", lambda e, a=a, r=r, c=c: e.activation(out=a[:, :], in_=r[:, :], func=AF.Exp, scale=V[:, 36 + c:37 + c]), outs=[a], ins=[r, V])
                S.op("act", lambda e, a2=a2, r=r, c=c: e.activation(out=a2[:, :], in_=r[:, :], func=AF.Exp, scale=V[:, 40 + c:41 + c]), outs=[a2], ins=[r, V])
                S.op("act", lambda e, a2=a2: e.activation(out=a2[:, :], in_=a2[:, :], func=AF.Relu, scale=-1.0, bias=self.epsc[:, 1:2]),
                     outs=[a2], ins=[a2, self.epsc])
                S.op("act", lambda e, a2=a2: e.activation(out=a2[:, :], in_=a2[:, :], func=AF.Sqrt), outs=[a2], ins=[a2])
                if first:
                    S.op("dve", lambda e, a2=a2: e.memset(a2[:, 0:1], 1.0), outs=[a2])
                S.op("dve", lambda e, a2=a2, ig=ig: e.tensor_tensor(out=a2[:, :], in0=a2[:, :], in1=ig[:, :], op=ALU.mult), outs=[a2], ins=[a2, ig])
                S.op("dve", lambda e, a2=a2, xc=xc: e.tensor_tensor(out=a2[:, :], in0=a2[:, :], in1=xc[:, :], op=ALU.mult), outs=[a2], ins=[a2, xc])
                hs = Wk[9]
                S.op("dve", lambda e, hs=hs, a=a, a2=a2, c=c: e.tensor_tensor_scan(out=hs[:, :], data0=a[:, :], data1=a2[:, :],
                                                                                 initial=self.hprev[:, c:c + 1], op0=ALU.mult, op1=ALU.add),
                     outs=[hs], ins=[a, a2, self.hprev])
                S.op("dve", lambda e, hs=hs, c=c: e.tensor_copy(out=self.hprev[:, c:c + 1], in_=hs[:, ST - 1:ST]), outs=[self.hprev], ins=[hs])
                ya = self.workb[1 + (c % 2)]
                S.op("dve", lambda e, ya=ya, hs=hs, gg=gg: e.tensor_tensor(out=ya[:, :], in0=hs[:, :], in1=gg[:, :], op=ALU.mult), outs=[ya], ins=[hs, gg])
                self.store_y(0, sup, ya[:, :], ya, c)

    def sincos(self, sin_ap, cos_ap, ang_ap, n, ta, tb, tc, res_out, res_in, res_tmp, res_out2=None):
        S = self.S
        I32 = mybir.dt.int32
        S.op("dve", lambda e: e.tensor_scalar(out=ta, in0=ang_ap, scalar1=1.0 / (2.0 * np.pi), scalar2=None, op0=ALU.mult), outs=[res_tmp], ins=[res_in])
        S.op("dve", lambda e: e.tensor_copy(out=tb.bitcast(I32), in_=ta), outs=[res_tmp], ins=[res_tmp])
        S.op("dve", lambda e: e.tensor_copy(out=tc, in_=tb.bitcast(I32)), outs=[res_tmp], ins=[res_tmp])
        S.op("dve", lambda e: e.tensor_tensor(out=ta, in0=ta, in1=tc, op=ALU.subtract), outs=[res_tmp], ins=[res_tmp])
        res_out2 = res_out if res_out2 is None else res_out2
        for (dst, shift, res_out) in [(sin_ap, 0.0, res_out), (cos_ap, 0.25, res_out2)]:
            S.op("dve", lambda e, shift=shift: e.tensor_scalar(out=tb, in0=ta, scalar1=shift, scalar2=None, op0=ALU.add), outs=[res_tmp], ins=[res_tmp])
            for _ in range(2):
                S.op("dve", lambda e: e.tensor_scalar(out=tc, in0=tb, scalar1=0.5, scalar2=None, op0=ALU.is_gt), outs=[res_tmp], ins=[res_tmp])
                S.op("dve", lambda e: e.tensor_tensor(out=tb, in0=tb, in1=tc, op=ALU.subtract), outs=[res_tmp], ins=[res_tmp])
            S.op("dve", lambda e: e.tensor_scalar(out=tc, in0=tb, scalar1=-0.5, scalar2=None, op0=ALU.is_lt), outs=[res_tmp], ins=[res_tmp])
            S.op("dve", lambda e: e.tensor_tensor(out=tb, in0=tb, in1=tc, op=ALU.add), outs=[res_tmp], ins=[res_tmp])
            S.op("act", lambda e, dst=dst: e.activation(out=dst, in_=tb, func=AF.Sin, scale=6.28318), outs=[res_out], ins=[res_tmp])

    def phase_C(self, l):
        S = self.S
        off = 0
        wuc, off = self.warena_view(off, [KC, 512])
        wgl, off = self.warena_view(off, [4, 512])
        BrT, off = self.warena_view(off, [16, P])
        BiT, off = self.warena_view(off, [16, P])
        CrT, off = self.warena_view(off, [16, P])
        nCrT, off = self.warena_view(off, [16, P])
        nCiT, off = self.warena_view(off, [16, P])
        WA = self.warena
        self.load_w(wuc, self.W["w_in"][l][:, 3080:3592], WA)
        self.load_w(wgl, self.W["c_glu_w"][l], WA)
        self.load_vec(self.gv[:], self.W["mix_norm"][l], self.gv)
        V = self.vec
        self.load_vec(V[:, 0:4], self.W["c_d"][l], V)
        self.load_vec(V[:, 4:8], self.W["c_glu_b"][l], V)
        cs = self.cset
        for slot, nm in [(0, "c_lam_re"), (1, "c_lam_im")]:
            S.dma("sp", cs[:, slot, :], self.W[nm][l].rearrange("(st g2) p -> g2 p st", g2=2).rearrange("g2 p st -> (g2 p) st"), outs=[cs])
        ldt = self.W["c_log_dt"][l].rearrange("(st g2) -> g2 st", g2=2)
        for g2 in range(2):
            S.dma("sp", cs[g2 * 64:(g2 + 1) * 64, 2, :], ldt[g2:g2 + 1, :].broadcast_to([64, 16]), outs=[cs])
        sl = lambda k: cs[:, k, :]
        S.barrier()
        S.op("act", lambda e: e.activation(out=sl(2), in_=sl(2), func=AF.Exp), outs=[cs], ins=[cs])
        S.op("dve", lambda e: e.tensor_tensor(out=sl(3), in0=sl(0), in1=sl(2), op=ALU.mult), outs=[cs], ins=[cs])
        S.op("act", lambda e: e.activation(out=sl(12), in_=sl(3), func=AF.Exp), outs=[cs], ins=[cs])
        S.op("dve", lambda e: e.tensor_tensor(out=sl(4), in0=sl(1), in1=sl(2), op=ALU.mult), outs=[cs], ins=[cs])
        self.sincos(sl(5), sl(6), sl(4), 16, sl(16), sl(17), sl(18), cs, cs, cs)
        S.op("dve", lambda e: e.tensor_tensor(out=sl(7), in0=sl(12), in1=sl(6), op=ALU.mult), outs=[cs], ins=[cs])
        S.op("dve", lambda e: e.tensor_tensor(out=sl(8), in0=sl(12), in1=sl(5), op=ALU.mult), outs=[cs], ins=[cs])
        S.op("dve", lambda e: e.tensor_tensor(out=sl(9), in0=sl(0), in1=sl(0), op=ALU.mult), outs=[cs], ins=[cs])
        S.op("dve", lambda e: e.tensor_tensor(out=sl(15), in0=sl(1), in1=sl(1), op=ALU.mult), outs=[cs], ins=[cs])
        S.op("dve", lambda e: e.tensor_tensor(out=sl(9), in0=sl(9), in1=sl(15), op=ALU.add), outs=[cs], ins=[cs])
        S.op("dve", lambda e: e.reciprocal(out=sl(9), in_=sl(9)), outs=[cs], ins=[cs])
        S.op("dve", lambda e: e.tensor_scalar(out=sl(19), in0=sl(7), scalar1=-1.0, scalar2=None, op0=ALU.add), outs=[cs], ins=[cs])
        S.op("dve", lambda e: e.tensor_tensor(out=sl(10), in0=sl(19), in1=sl(0), op=ALU.mult), outs=[cs], ins=[cs])
        S.op("dve", lambda e: e.tensor_tensor(out=sl(15), in0=sl(8), in1=sl(1), op=ALU.mult), outs=[cs], ins=[cs])
        S.op("dve", lambda e: e.tensor_tensor(out=sl(10), in0=sl(10), in1=sl(15), op=ALU.add), outs=[cs], ins=[cs])
        S.op("dve", lambda e: e.tensor_tensor(out=sl(10), in0=sl(10), in1=sl(9), op=ALU.mult), outs=[cs], ins=[cs])
        S.op("dve", lambda e: e.tensor_tensor(out=sl(11), in0=sl(8), in1=sl(0), op=ALU.mult), outs=[cs], ins=[cs])
        S.op("dve", lambda e: e.tensor_tensor(out=sl(15), in0=sl(19), in1=sl(1), op=ALU.mult), outs=[cs], ins=[cs])
        S.op("dve", lambda e: e.tensor_tensor(out=sl(11), in0=sl(11), in1=sl(15), op=ALU.subtract), outs=[cs], ins=[cs])
        S.op("dve", lambda e: e.tensor_tensor(out=sl(11), in0=sl(11), in1=sl(9), op=ALU.mult), outs=[cs], ins=[cs])
        S.op("dve", lambda e: e.tensor_scalar(out=sl(20), in0=sl(4), scalar1=128.0, scalar2=None, op0=ALU.mult), outs=[cs], ins=[cs])
        self.sincos(sl(14), sl(13), sl(20), 16, sl(16), sl(17), sl(18), cs, cs, cs)
        S.op("dve", lambda e: e.tensor_scalar(out=sl(22), in0=sl(14), scalar1=-1.0, scalar2=None, op0=ALU.mult), outs=[cs], ins=[cs])
        xs = self.xs
        tA = xs[:, 0:4, :].rearrange("p a b -> p (a b)")
        tB = xs[:, 4:8, :].rearrange("p a b -> p (a b)")
        tC = self.tok[:, :]
        homes = [(self.lvmask, self.lvmask[:, i, :]) for i in range(14)] + [(self.tok, self.tok[:, i * P:(i + 1) * P]) for i in range(8)] \
            + [(self.bx[0], self.bx[0][:, i * P:(i + 1) * P]) for i in range(4)] + [(self.bx[1], self.bx[1][:, i * P:(i + 1) * P]) for i in range(4)] \
            + [(self.Sst, self.Sst[:, i, :]) for i in range(4)]
        cosH = homes[0:16]
        sinH = homes[16:32]
        for st in range(16):
            S.op("dve", lambda e, st=st: e.tensor_scalar(out=tA[:, 0:P], in0=self.svec[:, :], scalar1=cs[:, 4, st:st + 1], scalar2=None, op0=ALU.mult),
                 outs=[xs], ins=[self.svec, cs])
            self.sincos(sinH[st][1], cosH[st][1], tA[:, 0:P], P, tA[:, P:2 * P], tA[:, 2 * P:3 * P], tA[:, 3 * P:4 * P], sinH[st][0], xs, xs, res_out2=cosH[st][0])
        Bre = xs[:, 0, 0:256].rearrange("p (a b) -> p a b", a=16)
        Bim = xs[:, 0, 256:512].rearrange("p (a b) -> p a b", a=16)
        bbr = xs[:, 1, 0:256].rearrange("p (a b) -> p a b", a=16)
        bbi = xs[:, 1, 256:512].rearrange("p (a b) -> p a b", a=16)
        tmp = xs[:, 2, 0:256].rearrange("p (a b) -> p a b", a=16)
        Cre = xs[:, 3, 0:256].rearrange("p (a b) -> p a b", a=16)
        Cim = xs[:, 3, 256:512].rearrange("p (a b) -> p a b", a=16)
        for dst, nm in [(Bre, "c_b_re"), (Bim, "c_b_im")]:
            src = self.W[nm][l].rearrange("(st g2) p c -> g2 p st c", g2=2)
            for g2 in range(2):
                S.dma("sp", dst[g2 * 64:(g2 + 1) * 64, :, :], src[g2], outs=[xs])
        for dst, nm in [(Cre, "c_c_re"), (Cim, "c_c_im")]:
            src = self.W[nm][l].rearrange("(st g2) c p -> g2 st p c", g2=2)
            for g2 in range(2):
                for st_ in range(16):
                    S.dma("sp", dst[g2 * 64:(g2 + 1) * 64, st_, :], src[g2, st_], outs=[xs])
        S.barrier()
        frb = cs[:, 10, :].unsqueeze(2).broadcast_to([P, 16, 16])
        fib = cs[:, 11, :].unsqueeze(2).broadcast_to([P, 16, 16])
        TT = lambda o, a, b, op: S.op("dve", lambda e: e.tensor_tensor(out=o, in0=a, in1=b, op=op), outs=[xs], ins=[xs, cs])
        TT(bbr, Bre, frb, ALU.mult)
        TT(tmp, Bim, fib, ALU.mult)
        TT(bbr, bbr, tmp, ALU.subtract)
        TT(bbi, Bim, frb, ALU.mult)
        TT(tmp, Bre, fib, ALU.mult)
        TT(bbi, bbi, tmp, ALU.add)
        pad = xs[:, 4:8, :].rearrange("p a b -> p (a b)").rearrange("p (a b c) -> p a b c", a=4, b=4)
        padf = xs[:, 4:8, :].rearrange("p a b -> p (a b)").rearrange("p (s c) -> p s c", s=16)

        def fill_pad(src, scale):
            S.op("pool", lambda e: e.memset(xs[:, 4:8, :], 0.0), outs=[xs])
            s4 = src.rearrange("p (a b) c -> p a b c", b=4)
            for g2 in range(2):
                for b in range(4):
                    co = 32 * b + 16 * g2
                    S.op("dve", lambda e, g2=g2, b=b, co=co: e.tensor_scalar(out=pad[g2 * 64:(g2 + 1) * 64, :, b, co:co + 16], in0=s4[g2 * 64:(g2 + 1) * 64, :, b, :],
                                                                           scalar1=scale, scalar2=None, op0=ALU.mult), outs=[xs], ins=[xs])

        for src, dstT in [(bbr, BrT), (bbi, BiT)]:
            fill_pad(src, 1.0)
            for q4 in range(4):
                ps = S.psum()
                for q in range(4):
                    st = q4 * 4 + q
                    S.op("pe", lambda e, ps=ps, q=q, st=st: e.transpose(out=ps[:, q * P:(q + 1) * P], in_=padf[:, st, :], identity=self.ident[:]), outs=[ps], ins=[xs, self.ident])
                S.op("act", lambda e, ps=ps, q4=q4, dstT=dstT: e.copy(out=dstT[:, q4 * 4:q4 * 4 + 4, :], in_=ps[:, :].rearrange("p (a b) -> p a b", a=4)), outs=[WA], ins=[ps])
        for src, dstT, sc in [(Cre, CrT, 1.0), (Cre, nCrT, -1.0), (Cim, nCiT, -1.0)]:
            fill_pad(src, sc)
            S.op("act", lambda e, dstT=dstT: e.copy(out=dstT, in_=padf), outs=[WA], ins=[xs])

        import os
        CSTOP = os.environ.get("C_STOP", "")
        if CSTOP == "setup":
            return
        ucf = self.work[0:4]
        yv = self.work[4:8]
        ucb = self.big_b
        car = self.ccar
        S.barrier()
        for sup in range(self.NTOK // ST):
            first = (sup % self.NST == 0)
            self.load_norm_prefetch(sup)
            if first:
                S.op("pool", lambda e: e.memset(car[:], 0.0), outs=[car])
            for c in range(4):
                ps = S.psum()
                for kc in range(KC):
                    self.mm(ps, ps[:, :], wuc[:, kc, c * P:(c + 1) * P], self.hT[:, kc, :], kc == 0, kc == KC - 1, [WA, self.hT])
                S.op("act", lambda e, ps=ps, c=c: e.copy(out=ucf[c][:, :], in_=ps[:, :]), outs=[ucf[c]], ins=[ps])
                S.op("dve", lambda e, c=c: e.tensor_copy(out=ucb[:, c, :], in_=ucf[c][:, :]), outs=[ucb], ins=[ucf[c]])
            v4 = lambda ap: ap.rearrange("p (j s) -> p j s", j=4)
            t1, t2, t3, t4 = [self.ubuf[k] for k in range(4)]
            wrT, wiT = self.work[8], self.work[9]
            mbs = self.workb[0:4]
            rhoT = self.rstd
            psy_box = [None]

            def demod(st):
                c = st // 4
                cosB = cosH[st][1].unsqueeze(1).broadcast_to([P, 4, P])
                sinB = sinH[st][1].unsqueeze(1).broadcast_to([P, 4, P])
                for k, (src, tb, tr) in enumerate([(wrT, cosB, cosH[st][0]), (wiT, sinB, sinH[st][0]), (wiT, cosB, cosH[st][0]), (wrT, sinB, sinH[st][0])]):
                    S.op("dve", lambda e, k=k, src=src, tb=tb: e.tensor_tensor(out=v4(mbs[k][:, :]), in0=v4(src[:, :]), in1=tb, op=ALU.mult),
                         outs=[mbs[k]], ins=[src, tr])
                if st % 4 == 0:
                    psy_box[0] = S.psb[6 + (c % 2)]
                psy = psy_box[0]
                for k, lw in enumerate([CrT, nCrT, nCiT, nCiT]):
                    self.mm(psy, psy[:, :], lw[:, st, :], mbs[k][:, :], st % 4 == 0 and k == 0, st % 4 == 3 and k == 3, [WA, mbs[k]])
                if st % 4 == 3:
                    S.op("dve", lambda e, psy=psy, c=c: e.scalar_tensor_tensor(out=yv[c][:, :], in0=ucf[c][:, :], scalar=V[:, c:c + 1], in1=psy[:, :],
                                                                             op0=ALU.mult, op1=ALU.add), outs=[yv[c]], ins=[ucf[c], V, psy])

            S.psmod = 6
            for st in range(16):
                c = st // 4
                psr, psi = S.psum(), S.psum()
                self.mm(psr, psr[:, :], BrT[:, st, :], ucb[:, c, :], True, True, [WA, ucb])
                self.mm(psi, psi[:, :], BiT[:, st, :], ucb[:, c, :], True, True, [WA, ucb])
                cosB = cosH[st][1].unsqueeze(1).broadcast_to([P, 4, P])
                sinB = sinH[st][1].unsqueeze(1).broadcast_to([P, 4, P])
                cR, sR = cosH[st][0], sinH[st][0]
                S.op("dve", lambda e, psr=psr, cosB=cosB: e.tensor_tensor(out=v4(t1[:, 0:ST]), in0=v4(psr[:, :]), in1=cosB, op=ALU.mult), outs=[t1], ins=[psr, cR])
                S.op("dve", lambda e, psi=psi, sinB=sinB: e.tensor_tensor(out=v4(t2[:, 0:ST]), in0=v4(psi[:, :]), in1=sinB, op=ALU.mult), outs=[t2], ins=[psi, sR])
                S.op("dve", lambda e, psi=psi, cosB=cosB: e.tensor_tensor(out=v4(t3[:, 0:ST]), in0=v4(psi[:, :]), in1=cosB, op=ALU.mult), outs=[t3], ins=[psi, cR])
                S.op("dve", lambda e, psr=psr, sinB=sinB: e.tensor_tensor(out=v4(t4[:, 0:ST]), in0=v4(psr[:, :]), in1=sinB, op=ALU.mult), outs=[t4], ins=[psr, sR])
                S.op("pool", lambda e: e.tensor_tensor(out=t1[:, 0:ST], in0=t1[:, 0:ST], in1=t2[:, 0:ST], op=ALU.add), outs=[t1], ins=[t1, t2])
                S.op("pool", lambda e: e.tensor_tensor(out=t3[:, 0:ST], in0=t3[:, 0:ST], in1=t4[:, 0:ST], op=ALU.subtract), outs=[t3], ins=[t3, t4])
                if st > 0:
                    demod(st - 1)
                S.op("dve", lambda e, st=st: e.tensor_scalar(out=rhoT[:, 0:P], in0=self.ones_f[:, :], scalar1=cs[:, 12, st:st + 1], scalar2=None, op0=ALU.mult),
                     outs=[rhoT], ins=[self.ones_f, cs])
                for j in range(NSUB):
                    js = slice(j * P, (j + 1) * P)
                    S.op("dve", lambda e, js=js, st=st: e.tensor_tensor_scan(out=wrT[:, js], data0=rhoT[:, 0:P], data1=t1[:, js], initial=car[:, 0, st:st + 1],
                                                                           op0=ALU.mult, op1=ALU.add), outs=[wrT], ins=[rhoT, t1, car])
                    S.op("dve", lambda e, js=js, st=st: e.tensor_tensor_scan(out=wiT[:, js], data0=rhoT[:, 0:P], data1=t3[:, js], initial=car[:, 1, st:st + 1],
                                                                           op0=ALU.mult, op1=ALU.add), outs=[wiT], ins=[rhoT, t3, car])
                    le = j * P + P - 1
                    S.op("dve", lambda e, le=le, st=st: e.tensor_scalar(out=cs[:, 23, 0:1], in0=wrT[:, le:le + 1], scalar1=cs[:, 13, st:st + 1], scalar2=None, op0=ALU.mult),
                         outs=[self.ctmp], ins=[wrT, cs])
                    S.op("dve", lambda e, le=le, st=st: e.tensor_scalar(out=cs[:, 23, 1:2], in0=wiT[:, le:le + 1], scalar1=cs[:, 13, st:st + 1], scalar2=None, op0=ALU.mult),
                         outs=[self.ctmp], ins=[wiT, cs])
                    S.op("dve", lambda e, le=le, st=st: e.scalar_tensor_tensor(out=car[:, 0, st:st + 1], in0=wiT[:, le:le + 1], scalar=cs[:, 22, st:st + 1], in1=cs[:, 23, 0:1],
                                                                             op0=ALU.mult, op1=ALU.add), outs=[car], ins=[wiT, cs, self.ctmp])
                    S.op("dve", lambda e, le=le, st=st: e.scalar_tensor_tensor(out=car[:, 1, st:st + 1], in0=wrT[:, le:le + 1], scalar=cs[:, 14, st:st + 1], in1=cs[:, 23, 1:2],
                                                                             op0=ALU.mult, op1=ALU.add), outs=[car], ins=[wrT, cs, self.ctmp])
            demod(15)
            S.psmod = 8
            ygb = self.workb[2:6]
            for c in range(4):
                self.gelu(yv[c][:, :], yv[c], yv[c][:, :], yv[c], self.work[8], self.work[9])
                S.op("act", lambda e, c=c: e.copy(out=ygb[c][:, :], in_=yv[c][:, :]), outs=[ygb[c]], ins=[yv[c]])
            for ec in range(4):
                ps = S.psum()
                for c in range(4):
                    self.mm(ps, ps[:, :], wgl[:, c, ec * P:(ec + 1) * P], ygb[c][:, :], c == 0, c == 3, [WA, ygb[c]])
                sg = self.work[8]
                S.op("act", lambda e, ps=ps, ec=ec, sg=sg: e.activation(out=sg[:, :], in_=ps[:, :], func=AF.Sigmoid, bias=V[:, 4 + ec:5 + ec], scale=1.0), outs=[sg], ins=[ps, V])
                S.op("dve", lambda e, ec=ec, sg=sg: e.tensor_tensor(out=ucb[:, 4 + ec, :], in0=sg[:, :], in1=yv[ec][:, :], op=ALU.mult), outs=[ucb], ins=[sg, yv[ec]])
                self.store_y(2, sup, ucb[:, 4 + ec, :], ucb, ec)

    def phase_B(self, l):
        S = self.S
        off = 0
        wqkv, off = self.warena_view(off, [KC, 1536])
        wz, off = self.warena_view(off, [KC, 512])
        wba, off = self.warena_view(off, [KC, 8])
        WA = self.warena
        self.load_w(wqkv, self.W["w_in"][l][:, 1024:2560], WA)
        self.load_w(wz, self.W["w_in"][l][:, 2560:3072], WA)
        self.load_w(wba, self.W["w_in"][l][:, 3072:3080], WA)
        self.load_vec(self.gv[:], self.W["mix_norm"][l], self.gv)
        V = self.vec
        for i in range(4):
            self.load_vec(V[:, 16 + i * 12:16 + (i + 1) * 12], self.W["b_conv_w"][l, i], V)
        S.dma("sp", V[:, 80:84], self.W["b_a_log"][l:l + 1, :].broadcast_to([P, 4]), outs=[V])
        S.dma("sp", V[:, 84:88], self.W["b_dt_bias"][l:l + 1, :].broadcast_to([P, 4]), outs=[V])
        S.dma("sp", self.bnb[:], self.W["b_norm"][l:l + 1, :].broadcast_to([P, P]), outs=[self.bnb])
        S.dma("sp", self.lvmask[:], self.C["lvmask"].rearrange("l p q -> p l q"), outs=[self.lvmask])
        S.barrier()
        S.op("act", lambda e: e.activation(out=V[:, 88:92], in_=V[:, 80:84], func=AF.Exp), outs=[V], ins=[V])
        S.op("dve", lambda e: e.tensor_scalar(out=V[:, 88:92], in0=V[:, 88:92], scalar1=-1.0, scalar2=None, op0=ALU.mult), outs=[V], ins=[V])
        xs = self.xs
        vT = self.work[0:4]
        zs = self.work[4:8]
        Sst = self.Sst
        bgt = self.bgt
        H4 = lambda t: t[:, 0:ST].rearrange("p (h d) -> p h d", h=4)
        T0, T1 = self.work[8], self.work[9]
        T2, T3, T4, T5 = self.ubuf
        T6, T7 = self.bx
        T8 = self.rstd
        tokA = self.tok
        T9 = tokA[:, 0:512]
        T10 = tokA[:, 512:1024]
        R9 = R10 = tokA
        lvm = self.lvmask
        bc = lambda ap2: ap2.unsqueeze(1).broadcast_to([P, 4, P])
        S.barrier()
        for sup in range(self.NTOK // ST):
            first = (sup % self.NST == 0)
            self.load_x_norm(sup)
            if first:
                S.op("pool", lambda e: e.memset(self.halo[:, 4:16, :], 0.0), outs=[self.halo])
                S.op("pool", lambda e: e.memset(Sst[:], 0.0), outs=[Sst])
            for sub in range(NSUB):
                sc = slice(sub * P, (sub + 1) * P)
                ps = S.psum()
                for kc in range(KC):
                    self.mm(ps, ps[:, :], self.hT[:, kc, sc], wz[:, kc, :], kc == 0, kc == KC - 1, [WA, self.hT])
                S.op("act", lambda e, ps=ps, sub=sub: e.activation(out=zs[sub][:, :], in_=ps[:, :], func=AF.Silu), outs=[zs[sub]], ins=[ps])
                ps = S.psum()
                for kc in range(KC):
                    self.mm(ps, ps[:, 0:8], self.hT[:, kc, sc], wba[:, kc, :], kc == 0, kc == KC - 1, [WA, self.hT])
                S.op("act", lambda e, ps=ps, sub=sub: e.activation(out=bgt[:, sub, 0:4], in_=ps[:, 0:4], func=AF.Sigmoid), outs=[bgt], ins=[ps])
                S.op("dve", lambda e, sub=sub: e.tensor_scalar(out=bgt[:, sub, 8:12], in0=bgt[:, sub, 0:4], scalar1=-1.0, scalar2=None, op0=ALU.mult), outs=[bgt], ins=[bgt])
                S.op("dve", lambda e, ps=ps, sub=sub: e.tensor_tensor(out=bgt[:, sub, 4:8], in0=ps[:, 4:8], in1=V[:, 84:88], op=ALU.add), outs=[bgt], ins=[ps, V])
                S.op("act", lambda e, sub=sub: e.activation(out=bgt[:, sub, 4:8], in_=bgt[:, sub, 4:8], func=AF.Exp), outs=[bgt], ins=[bgt])
                S.op("act", lambda e, sub=sub: e.activation(out=bgt[:, sub, 4:8], in_=bgt[:, sub, 4:8], func=AF.Ln, bias=self.epsc[:, 1:2], scale=1.0), outs=[bgt], ins=[bgt, self.epsc])
                S.op("dve", lambda e, sub=sub: e.tensor_tensor(out=bgt[:, sub, 4:8], in0=bgt[:, sub, 4:8], in1=V[:, 88:92], op=ALU.mult), outs=[bgt], ins=[bgt, V])
            for c in range(12):
                ps = S.psum()
                for kc in range(KC):
                    self.mm(ps, ps[:, :], wqkv[:, kc, c * P:(c + 1) * P], self.hT[:, kc, :], kc == 0, kc == KC - 1, [WA, self.hT])
                ub = T2 if c % 2 == 0 else T3
                S.op("act", lambda e, ub=ub, ps=ps: e.copy(out=ub[:, 3:3 + ST], in_=ps[:, :]), outs=[ub], ins=[ps])
                S.op("pool", lambda e, ub=ub, c=c: e.tensor_copy(out=ub[:, 0:3], in_=self.halo[:, 4 + c, 0:3]), outs=[ub], ins=[self.halo])
                if c < 8:
                    dst, dres = xs[:, c, :], xs
                else:
                    dst, dres = vT[c - 8][:, :], vT[c - 8]
                S.op("dve", lambda e, ub=ub, c=c, dst=dst: e.tensor_scalar(out=dst, in0=ub[:, 3:3 + ST], scalar1=V[:, 16 + 36 + c:16 + 36 + c + 1], scalar2=None, op0=ALU.mult),
                     outs=[dres], ins=[ub, V])
                for i in range(3):
                    S.op("dve", lambda e, ub=ub, c=c, i=i, dst=dst: e.scalar_tensor_tensor(out=dst, in0=ub[:, i:i + ST], scalar=V[:, 16 + i * 12 + c:16 + i * 12 + c + 1], in1=dst,
                                                                                        op0=ALU.mult, op1=ALU.add), outs=[dres], ins=[ub, V, dres])
                S.op("pool", lambda e, ub=ub, c=c: e.tensor_copy(out=self.halo[:, 4 + c, 0:3], in_=ub[:, ST:ST + 3]), outs=[self.halo], ins=[ub])
                S.op("act", lambda e, dst=dst: e.activation(out=dst, in_=dst, func=AF.Silu), outs=[dres], ins=[dres])
                if c < 8:
                    S.op("act", lambda e, dst=dst: e.activation(out=T0[:, :], in_=dst, func=AF.Square), outs=[T0], ins=[dres])
                    ps2 = S.psum()
                    self.mm(ps2, ps2[:, :], self.ones_f[:, :], T0[:, :], True, True, [self.ones_f, T0])
                    S.op("act", lambda e, ps2=ps2: e.activation(out=T1[:, :], in_=ps2[:, :], func=AF.Sqrt, bias=self.epsc[:, 0:1], scale=1.0), outs=[T1], ins=[ps2, self.epsc])
                    S.op("dve", lambda e: e.reciprocal(out=T1[:, :], in_=T1[:, :]), outs=[T1], ins=[T1])
                    qs = float(128.0 ** -0.5) if c < 4 else 1.0
                    S.op("dve", lambda e, dst=dst, qs=qs: e.scalar_tensor_tensor(out=dst, in0=dst, scalar=qs, in1=T1[:, :], op0=ALU.mult, op1=ALU.mult), outs=[dres], ins=[dres, T1])
            yout = self.big_c
            for sub in range(NSUB):
                sc = slice(sub * P, (sub + 1) * P)
                g4 = bgt[:, sub, 4:8]
                for h in range(4):
                    S.op("dve", lambda e, h=h, sub=sub: e.tensor_scalar(out=H4(T1)[:, h, :], in0=self.uincl[:, :], scalar1=bgt[:, sub, 4 + h:5 + h], scalar2=None, op0=ALU.mult),
                         outs=[T1], ins=[self.uincl, bgt])
                psd, psdT, pse = S.psum(), S.psum(), S.psum()
                for h in range(4):
                    self.mm(psd, psd[:, h * P:(h + 1) * P], H4(T1)[:, h, :], self.lstrict[:, :], True, True, [T1, self.lstrict])
                for h in range(4):
                    self.mm(psdT, psdT[:, h * P:(h + 1) * P], self.lstrict[:, :], H4(T1)[:, h, :], True, True, [T1, self.lstrict])
                self.mm(pse, pse[:, 0:4], self.uincl[:, :], g4, True, True, [self.uincl, bgt])
                self.mm(pse, pse[:, 4:8], self.lstrict[:, :], g4, True, True, [self.lstrict, bgt])
                self.mm(pse, pse[:, 8:12], self.ones_f[:, :], g4, True, True, [self.ones_f, bgt])
                eg = self.eg
                S.op("act", lambda e, pse=pse: e.activation(out=eg[:, 0:12], in_=pse[:, 0:12], func=AF.Exp), outs=[eg], ins=[pse])
                S.op("dve", lambda e: e.tensor_scalar(out=eg[:, 12:16], in0=eg[:, 0:4], scalar1=-1.0, scalar2=None, op0=ALU.mult), outs=[eg], ins=[eg])
                S.op("act", lambda e, psd=psd: e.activation(out=T2[:, 0:ST], in_=psd[:, :], func=AF.Exp), outs=[T2], ins=[psd])
                S.op("act", lambda e, psdT=psdT: e.activation(out=T3[:, 0:ST], in_=psdT[:, :], func=AF.Exp), outs=[T3], ins=[psdT])
                psk = S.psum()
                for h in range(4):
                    self.mm(psk, psk[:, h * P:(h + 1) * P], xs[:, 4 + h, sc], xs[:, 4 + h, sc], True, True, [xs])
                S.op("dve", lambda e, psk=psk: e.tensor_tensor(out=T0[:, :], in0=psk[:, :], in1=T2[:, 0:ST], op=ALU.mult), outs=[T0], ins=[psk, T2])
                for h in range(4):
                    S.op("dve", lambda e, h=h, sub=sub: e.scalar_tensor_tensor(out=H4(T4)[:, h, :], in0=H4(T0)[:, h, :], scalar=bgt[:, sub, 8 + h:9 + h], in1=self.lstrict[:, :],
                                                                             op0=ALU.mult, op1=ALU.mult), outs=[T4], ins=[T0, bgt, self.lstrict])
                pst = S.psum()
                for h in range(4):
                    S.op("pe", lambda e, pst=pst, h=h: e.transpose(out=pst[:, h * P:(h + 1) * P], in_=H4(T4)[:, h, :], identity=self.ident[:]), outs=[pst], ins=[T4, self.ident])
                S.op("act", lambda e, pst=pst: e.copy(out=T5[:, 0:ST], in_=pst[:, :]), outs=[T5], ins=[pst])
                psq = S.psum()
                for h in range(4):
                    self.mm(psq, psq[:, h * P:(h + 1) * P], xs[:, 4 + h, sc], xs[:, h, sc], True, True, [xs])
                S.op("dve", lambda e, psq=psq: e.tensor_tensor(out=T0[:, :], in0=psq[:, :], in1=T3[:, 0:ST], op=ALU.mult), outs=[T0], ins=[psq, T3])
                S.op("dve", lambda e: e.tensor_tensor(out=H4(T6), in0=H4(T0), in1=bc(self.mincl_t[:, :]), op=ALU.mult), outs=[T6], ins=[T0, self.mincl_t])
                S.op("dve", lambda e: e.tensor_tensor(out=H4(T7), in0=H4(T4), in1=bc(lvm[:, 0, :]), op=ALU.mult), outs=[T7], ins=[T4, lvm])
                S.op("dve", lambda e: e.tensor_tensor(out=H4(T7), in0=H4(T7), in1=bc(self.ident[:, :]), op=ALU.add), outs=[T7], ins=[T7, self.ident])
                S.op("dve", lambda e: e.tensor_tensor(out=H4(T8), in0=H4(T5), in1=bc(lvm[:, 7, :]), op=ALU.mult), outs=[T8], ins=[T5, lvm])
                S.op("dve", lambda e: e.tensor_tensor(out=H4(T8), in0=H4(T8), in1=bc(self.ident[:, :]), op=ALU.add), outs=[T8], ins=[T8, self.ident])
                for lv in range(1, 7):
                    last = (lv == 6)
                    psP2 = S.psum()
                    for h in range(4):
                        self.mm(psP2, psP2[:, h * P:(h + 1) * P], H4(T4)[:, h, :], H4(T8)[:, h, :], True, True, [T4, T8])
                    S.op("act", lambda e, psP2=psP2: e.copy(out=T10, in_=psP2[:, :]), outs=[R10], ins=[psP2])
                    if not last:
                        psP = S.psum()
                        for h in range(4):
                            self.mm(psP, psP[:, h * P:(h + 1) * P], H4(T5)[:, h, :], H4(T7)[:, h, :], True, True, [T5, T7])
                        S.op("act", lambda e, psP=psP: e.copy(out=T9, in_=psP[:, :]), outs=[R9], ins=[psP])
                    psQ2 = S.psum()
                    for h in range(4):
                        self.mm(psQ2, psQ2[:, h * P:(h + 1) * P], H4(T7)[:, h, :], T10.rearrange("p (h d) -> p h d", h=4)[:, h, :], True, True, [T7, R10])
                    if not last:
                        psQ = S.psum()
                        for h in range(4):
                            self.mm(psQ, psQ[:, h * P:(h + 1) * P], H4(T8)[:, h, :], T9.rearrange("p (h d) -> p h d", h=4)[:, h, :], True, True, [T8, R9])
                        S.op("dve", lambda e, psQ=psQ, lv=lv: e.tensor_tensor(out=H4(T0), in0=psQ[:, :].rearrange("p (h d) -> p h d", h=4), in1=bc(lvm[:, lv, :]), op=ALU.mult),
                             outs=[T0], ins=[psQ, lvm])
                    S.op("dve", lambda e, psQ2=psQ2, lv=lv: e.tensor_tensor(out=H4(T1), in0=psQ2[:, :].rearrange("p (h d) -> p h d", h=4), in1=bc(lvm[:, 7 + lv, :]), op=ALU.mult),
                         outs=[T1], ins=[psQ2, lvm])
                    if not last:
                        S.op("dve", lambda e: e.tensor_tensor(out=T7[:, 0:ST], in0=T7[:, 0:ST], in1=T0[:, :], op=ALU.add), outs=[T7], ins=[T7, T0])
                    S.op("dve", lambda e: e.tensor_tensor(out=T8[:, 0:ST], in0=T8[:, 0:ST], in1=T1[:, :], op=ALU.add), outs=[T8], ins=[T8, T1])
                psks, psv = S.psum(), S.psum()
                for h in range(4):
                    self.mm(psks, psks[:, h * P:(h + 1) * P], xs[:, 4 + h, sc], Sst[:, h, :], True, True, [xs, Sst])
                for h in range(4):
                    S.op("pe", lambda e, psv=psv, h=h, sc=sc: e.transpose(out=psv[:, h * P:(h + 1) * P], in_=vT[h][:, sc], identity=self.ident[:]), outs=[psv], ins=[vT[h], self.ident])
                S.op("act", lambda e, psv=psv: e.copy(out=T9, in_=psv[:, :]), outs=[R9], ins=[psv])
                for h in range(4):
                    hs = slice(h * P, (h + 1) * P)
                    S.op("dve", lambda e, psks=psks, h=h, hs=hs: e.scalar_tensor_tensor(out=T1[:, hs], in0=psks[:, hs], scalar=eg[:, 12 + h:13 + h], in1=T9[:, hs], op0=ALU.mult, op1=ALU.add),
                         outs=[T1], ins=[psks, eg, R9])
                    S.op("dve", lambda e, h=h, hs=hs, sub=sub: e.tensor_scalar(out=T1[:, hs], in0=T1[:, hs], scalar1=bgt[:, sub, h:h + 1], scalar2=None, op0=ALU.mult), outs=[T1], ins=[T1, bgt])
                psn = S.psum()
                for h in range(4):
                    hs = slice(h * P, (h + 1) * P)
                    self.mm(psn, psn[:, hs], H4(T8)[:, h, :], T1[:, hs], True, True, [T8, T1])
                S.op("act", lambda e, psn=psn: e.copy(out=T10, in_=psn[:, :]), outs=[R10], ins=[psn])
                psqs, pso2, pskt = S.psum(), S.psum(), S.psum()
                for h in range(4):
                    hs = slice(h * P, (h + 1) * P)
                    self.mm(psqs, psqs[:, hs], xs[:, h, sc], Sst[:, h, :], True, True, [xs, Sst])
                for h in range(4):
                    hs = slice(h * P, (h + 1) * P)
                    self.mm(pso2, pso2[:, hs], H4(T6)[:, h, :], T10[:, hs], True, True, [T6, R10])
                S.op("act", lambda e, pso2=pso2: e.copy(out=T2[:, 0:ST], in_=pso2[:, :]), outs=[T2], ins=[pso2])
                for h in range(4):
                    hs = slice(h * P, (h + 1) * P)
                    S.op("dve", lambda e, psqs=psqs, h=h, hs=hs: e.scalar_tensor_tensor(out=T0[:, hs], in0=psqs[:, hs], scalar=eg[:, h:h + 1], in1=T2[:, hs], op0=ALU.mult, op1=ALU.add),
                         outs=[T0], ins=[psqs, eg, T2])
                for h in range(4):
                    S.op("pe", lambda e, pskt=pskt, h=h, sc=sc: e.transpose(out=pskt[:, h * P:(h + 1) * P], in_=xs[:, 4 + h, sc], identity=self.ident[:]), outs=[pskt], ins=[xs, self.ident])
                for h in range(4):
                    hs = slice(h * P, (h + 1) * P)
                    S.op("dve", lambda e, pskt=pskt, h=h, hs=hs: e.tensor_scalar(out=T3[:, hs], in0=pskt[:, hs], scalar1=eg[:, 4 + h:5 + h], scalar2=None, op0=ALU.mult), outs=[T3], ins=[pskt, eg])
                psu = S.psum()
                for h in range(4):
                    hs = slice(h * P, (h + 1) * P)
                    self.mm(psu, psu[:, hs], T3[:, hs], T10[:, hs], True, True, [T3, R10])
                for h in range(4):
                    hs = slice(h * P, (h + 1) * P)
                    S.op("dve", lambda e, psu=psu, h=h, hs=hs: e.scalar_tensor_tensor(out=Sst[:, h, :], in0=Sst[:, h, :], scalar=eg[:, 8 + h:9 + h], in1=psu[:, hs], op0=ALU.mult, op1=ALU.add),
                         outs=[Sst], ins=[Sst, eg, psu])
                for h in range(4):
                    hs = slice(h * P, (h + 1) * P)
                    S.op("act", lambda e, h=h, hs=hs: e.activation(out=T1[:, hs], in_=T0[:, hs], func=AF.Square, accum_out=eg[:, 16 + h:17 + h]), outs=[T1, eg], ins=[T0])
                S.op("dve", lambda e: e.tensor_scalar(out=eg[:, 16:20], in0=eg[:, 16:20], scalar1=1.0 / 128.0, scalar2=EPS, op0=ALU.mult, op1=ALU.add), outs=[eg], ins=[eg])
                S.op("act", lambda e: e.activation(out=eg[:, 16:20], in_=eg[:, 16:20], func=AF.Sqrt), outs=[eg], ins=[eg])
                S.op("dve", lambda e: e.reciprocal(out=eg[:, 16:20], in_=eg[:, 16:20]), outs=[eg], ins=[eg])
                for h in range(4):
                    hs = slice(h * P, (h + 1) * P)
                    S.op("dve", lambda e, h=h, hs=hs: e.scalar_tensor_tensor(out=T0[:, hs], in0=T0[:, hs], scalar=eg[:, 16 + h:17 + h], in1=self.bnb[:, :], op0=ALU.mult, op1=ALU.mult),
                         outs=[T0], ins=[T0, eg, self.bnb])
                S.op("dve", lambda e, sub=sub: e.tensor_tensor(out=T0[:, :], in0=T0[:, :], in1=zs[sub][:, :], op=ALU.mult), outs=[T0], ins=[T0, zs[sub]])
                psy = S.psum()
                for h in range(4):
                    S.op("pe", lambda e, psy=psy, h=h: e.transpose(out=psy[:, h * P:(h + 1) * P], in_=T0[:, h * P:(h + 1) * P], identity=self.ident[:]), outs=[psy], ins=[T0, self.ident])
                S.op("act", lambda e, psy=psy, sc=sc: e.copy(out=yout[:, 0:4, sc], in_=psy[:, :].rearrange("p (h d) -> p h d", h=4)), outs=[yout], ins=[psy])
            for c in range(4):
                self.store_y(1, sup, yout[:, c, :], yout, c)


def build_cfg(cfg):
    return Prog(cfg)


_CACHE = {}


def run_prog(cfg, inputs_per_core):
    key = repr(sorted((k, str(v)) for k, v in cfg.items()))
    if key not in _CACHE:
        _CACHE[key] = build_cfg(cfg)
    prog = _CACHE[key]
    hc = host_consts()
    in_maps = []
    for ipc in inputs_per_core:
        m = dict(ipc)
        for k, v in hc.items():
            m["c_" + k] = v
        in_maps.append(m)
    import os
    if os.environ.get("K_TRACE"):
        res = run_bass_kernel_spmd(prog.nc, in_maps, core_ids=list(range(len(in_maps))), trace=True)
        print("EXEC_TIME_NS", res.exec_time_ns)
    else:
        res = run_bass_kernel_spmd(prog.nc, in_maps, core_ids=list(range(len(in_maps))))
    return res


def kernel(**inputs):
    ncores = 8
    x = np.ascontiguousarray(inputs["x"], dtype=np.float32)
    mem = np.ascontiguousarray(inputs["mem"], dtype=np.float32)
    B, SEQ, _ = x.shape
    nseq = B // ncores
    cfg = {"depth": int(inputs["w_in"].shape[0]), "seq": SEQ, "nseq": nseq}
    per_core = []
    for c in range(ncores):
        m = {k: np.ascontiguousarray(inputs[k], dtype=np.float32) for k in WEIGHT_NAMES}
        m["x"] = x[c * nseq:(c + 1) * nseq].reshape(nseq * SEQ, D)
        m["mem"] = mem[c * nseq:(c + 1) * nseq].reshape(nseq * MEM, D)
        per_core.append(m)
    res = run_prog(cfg, per_core)
    out = np.concatenate([np.asarray(r["out"]).reshape(nseq, SEQ, D) for r in res.results], axis=0)
    return out.astype(np.float32)
```

```python
import numpy as np
from contextlib import ExitStack
import concourse.bass as bass
import concourse.mybir as mybir
from concourse.bass_utils import run_bass_kernel_spmd

F32 = mybir.dt.float32
BF16 = mybir.dt.bfloat16
ALU = mybir.AluOpType
AF = mybir.ActivationFunctionType
AX = mybir.AxisListType

D = 1024
KC = 8
MEM = 256
DIN = 6664
DFF = 3072
EPS = 1e-6
P = 128
ST = 512
NSUB = ST // P
import os as _os
SAME_ENG_SYNC = _os.environ.get("K_SES", "1") == "1"
GELU_C = 0.7978845608028654


class Res:
    __slots__ = ("w", "r")

    def __init__(self):
        self.w = None
        self.r = {}


class Tl:
    def __init__(self, S, name, shape, dtype, psum=False):
        if psum:
            self.t = S.es.enter_context(S.nc.psum_tensor(name, shape, dtype))
        else:
            self.t = S.es.enter_context(S.nc.sbuf_tensor(name, shape, dtype))
        self.res = Res()

    def __getitem__(self, k):
        return self.t[k]


def _res(x):
    return x.res if isinstance(x, Tl) else x


class Sched:
    NDS = 40

    def __init__(self, nc, es):
        self.nc = nc
        self.es = es
        self.names = ["pe", "act", "dve", "pool", "sp"]
        self.esem = {k: es.enter_context(nc.semaphore("sem_" + k)) for k in self.names}
        self.ecnt = {k: 0 for k in self.names}
        self.prog = {k: [] for k in self.names}
        self.seen = {k: {} for k in self.names}
        self.dsem = [es.enter_context(nc.semaphore("dsem%d" % i)) for i in range(self.NDS)]
        self.dcnt = [0] * self.NDS
        self.dnext = 0
        self.psb = [Tl(self, "psb%d" % i, [P, 512], F32, psum=True) for i in range(8)]
        self.psn = 0
        self.psmod = 8
        self.ninstr = 0

    def psum(self):
        self.psn = self.psn % self.psmod
        t = self.psb[self.psn]
        self.psn = (self.psn + 1) % self.psmod
        return t

    def _wait(self, e, toks):
        for (sem, val, key) in toks:
            if self.seen[e].get(key, 0) < val:
                self.prog[e].append(lambda eng, s=sem, v=val: eng.wait_ge(s, v))
                self.seen[e][key] = val
                self.ninstr += 1

    def _deps(self, e, outs, ins):
        toks = []
        for r in ins:
            r = _res(r)
            if r.w is not None:
                toks.append(r.w)
        for r in outs:
            r = _res(r)
            if r.w is not None:
                toks.append(r.w)
            toks.extend(r.r.values())
        if e == "pe" or not SAME_ENG_SYNC:
            toks = [t for t in toks if t[2] != e]
        self._wait(e, toks)

    def _mark(self, tok, outs, ins):
        for r in ins:
            _res(r).r[tok[2]] = tok
        for r in outs:
            r = _res(r)
            r.w = tok
            r.r = {}

    def op(self, e, fn, outs=(), ins=()):
        self._deps(e, outs, ins)
        self.ecnt[e] += 1
        sem = self.esem[e]
        self.prog[e].append(lambda eng, f=fn, s=sem: f(eng).then_inc(s, 1))
        self.ninstr += 1
        self._mark((sem, self.ecnt[e], e), outs, ins)

    def dma(self, e, out_ap, in_ap, outs=(), ins=()):
        k = self.dnext
        self.dnext = (k + 1) % self.NDS
        key = "d%d" % k
        if self.dcnt[k] > 0:
            self._wait(e, [(self.dsem[k], self.dcnt[k], key)])
        self._deps(e, outs, ins)
        self.dcnt[k] += 16
        sem = self.dsem[k]
        self.prog[e].append(lambda eng, o=out_ap, i=in_ap, s=sem: eng.dma_start(out=o, in_=i).then_inc(s, 16))
        self.ninstr += 1
        self._mark((sem, self.dcnt[k], key), outs, ins)

    def barrier(self):
        toks = [(self.esem[k], self.ecnt[k], k) for k in self.names if self.ecnt[k] > 0]
        toks += [(self.dsem[k], self.dcnt[k], "d%d" % k) for k in range(self.NDS) if self.dcnt[k] > 0]
        for e in self.names:
            self._wait(e, [t for t in toks if t[2] != e])

    def emit(self):
        blk = self.es.enter_context(self.nc.Block())

        @blk.tensor
        def _(e):
            for f in self.prog["pe"]:
                f(e)

        @blk.scalar
        def _(e):
            for f in self.prog["act"]:
                f(e)

        @blk.vector
        def _(e):
            for f in self.prog["dve"]:
                f(e)

        @blk.gpsimd
        def _(e):
            for f in self.prog["pool"]:
                f(e)

        @blk.sync
        def _(e):
            for f in self.prog["sp"]:
                f(e)


WEIGHT_NAMES = ["mix_norm", "w_in", "b_gate", "a_conv_w", "a_conv_b", "a_w_r", "a_b_r", "a_w_i", "a_b_i",
                "a_lam", "b_conv_w", "b_a_log", "b_dt_bias", "b_norm", "c_lam_re", "c_lam_im", "c_log_dt",
                "c_b_re", "c_b_im", "c_c_re", "c_c_im", "c_d", "c_glu_w", "c_glu_b", "w_branch", "w_out",
                "xa_norm", "mem_norm", "xa_w_q", "xa_w_kv", "xa_w_o", "ffn_norm", "ffn_w_up", "ffn_conv_w",
                "ffn_conv_b", "ffn_w_down", "final_norm"]


def host_consts():
    i = np.arange(P)
    c = {}
    c["ident"] = np.eye(P, dtype=np.float32)
    c["uincl"] = (i[:, None] <= i[None, :]).astype(np.float32)
    c["lstrict"] = (i[:, None] > i[None, :]).astype(np.float32)
    c["mincl_t"] = (i[:, None] <= i[None, :]).astype(np.float32)
    lv = []
    for b in [1, 2, 4, 8, 16, 32, 64]:
        m = ((i[:, None] // (2 * b) == i[None, :] // (2 * b)) & (i[:, None] % (2 * b) >= b)
             & (i[None, :] % (2 * b) < b)).astype(np.float32)
        lv.append(m)
    c["lvmask"] = np.stack(lv + [m.T for m in lv], 0).astype(np.float32)
    c["svec"] = np.tile(np.arange(P, dtype=np.float32)[None, :], (P, 1))
    c["ones"] = np.ones((P, P), np.float32)
    return c


class Prog:
    def __init__(self, cfg):
        self.cfg = cfg
        self.L = cfg["depth"]
        self.SEQ = cfg["seq"]
        self.NSEQ = cfg["nseq"]
        self.NTOK = self.SEQ * self.NSEQ
        self.NST = self.SEQ // ST
        import os
        dflt = os.environ.get("K_PHASES", "A,B,C,M2,X,F1,F2").split(",")
        self.phases = cfg.get("phases", dflt)
        self.build()

    def build(self):
        nc = bass.Bass("TRN2", target_bir_lowering=False)
        self.nc = nc
        L, NTOK = self.L, self.NTOK
        shp = {"mix_norm": [L, D], "w_in": [L, D, DIN], "b_gate": [L, 3 * D], "a_conv_w": [L, 4, 512],
               "a_conv_b": [L, 512], "a_w_r": [L, 8, 64, 64], "a_b_r": [L, 512], "a_w_i": [L, 8, 64, 64],
               "a_b_i": [L, 512], "a_lam": [L, 512], "b_conv_w": [L, 4, 1536], "b_a_log": [L, 4],
               "b_dt_bias": [L, 4], "b_norm": [L, 128], "c_lam_re": [L, 32, 64], "c_lam_im": [L, 32, 64],
               "c_log_dt": [L, 32], "c_b_re": [L, 32, 64, 16], "c_b_im": [L, 32, 64, 16],
               "c_c_re": [L, 32, 16, 64], "c_c_im": [L, 32, 16, 64], "c_d": [L, 512],
               "c_glu_w": [L, 512, 512], "c_glu_b": [L, 512], "w_branch": [L, 3, 512, D], "w_out": [L, D, D],
               "xa_norm": [L, D], "mem_norm": [L, D], "xa_w_q": [L, D, D], "xa_w_kv": [L, D, 2 * D],
               "xa_w_o": [L, D, D], "ffn_norm": [L, D], "ffn_w_up": [L, D, 2 * DFF],
               "ffn_conv_w": [L, 3, 2 * DFF], "ffn_conv_b": [L, 2 * DFF], "ffn_w_down": [L, DFF, D],
               "final_norm": [D]}
        self.W = {k: nc.dram_tensor(k, shp[k], F32, kind="ExternalInput").ap() for k in WEIGHT_NAMES}
        self.x_in = nc.dram_tensor("x", [NTOK, D], F32, kind="ExternalInput").ap()
        self.mem_in = nc.dram_tensor("mem", [self.NSEQ * MEM, D], F32, kind="ExternalInput").ap()
        hc = host_consts()
        self.C = {k: nc.dram_tensor("c_" + k, list(v.shape), F32, kind="ExternalInput").ap() for k, v in hc.items()}
        self.out = nc.dram_tensor("out", [NTOK, D], F32, kind="ExternalOutput").ap()
        self.xT = nc.dram_tensor("xT_s", [D, NTOK], F32, kind="Internal").ap()
        self.hTs = nc.dram_tensor("hT_s", [D, NTOK], BF16, kind="Internal").ap()
        self.yT = nc.dram_tensor("yT_s", [1536, NTOK], BF16, kind="Internal").ap()
        self.dbg = self.cfg.get("debug", False)
        if self.dbg:
            self.dbg_y = nc.dram_tensor("dbg_y", [1536, NTOK], BF16, kind="ExternalOutput").ap()
            self.dbg_x = nc.dram_tensor("dbg_x", [D, NTOK], F32, kind="ExternalOutput").ap()
        nsup = NTOK // ST
        self.r_xT = [Res() for _ in range(nsup)]
        self.r_hT = [Res() for _ in range(nsup)]
        self.r_yT = [[Res() for _ in range(nsup)] for _ in range(3)]

        with ExitStack() as es:
            es.enter_context(nc.allow_non_contiguous_dma(reason="small parameter vectors / layout loads"))
            S = Sched(nc, es)
            self.S = S
            self.alloc()
            self.load_consts()
            self.phase_in()
            for l in range(L):
                for ph in self.phases:
                    S.barrier()
                    getattr(self, "phase_" + ph)(l)
            S.barrier()
            if self.dbg:
                self.phase_dbg()
            self.phase_out()
            S.barrier()
            S.emit()
        self.ninstr = S.ninstr

    def alloc(self):
        S = self.S
        T = lambda n, s, d=F32: Tl(S, n, s, d)
        self.warena = T("warena", [P, 44 * 1024], BF16)
        self.ident = T("ident", [P, P])
        self.identb = T("identb", [P, P], BF16)
        self.ones_b = T("ones_b", [P, P], BF16)
        self.ones_f = T("ones_f", [P, P])
        self.uincl = T("uincl", [P, P])
        self.lstrict = T("lstrict", [P, P])
        self.mincl_t = T("mincl_t", [P, P])
        self.lvmask = T("lvmask", [P, 14, P])
        self.svec = T("svec", [P, P])
        self.epsc = T("epsc", [P, 4])
        self.xs = T("xs", [P, KC, ST])
        self.rstd = T("rstd", [P, ST])
        self.hT = T("hT", [P, KC, ST], BF16)
        self.gv = T("gv", [P, KC])
        self.vec = T("vec", [P, 256])
        self.work = [T("work%d" % i, [P, ST]) for i in range(10)]
        self.workb = [T("workb%d" % i, [P, ST], BF16) for i in range(6)]
        self.big_b = T("big_b", [P, 12, ST], BF16)
        self.xsq = self.big_b
        self.big_c = T("big_c", [P, KC, ST], BF16)
        self.tok = T("tok", [P, D])
        self.kT = [T("kT0", [P, KC, MEM], BF16)] * self.NSEQ
        self.vv = [T("vv0", [P, 2, D], BF16)] * self.NSEQ
        self.tail = T("tail", [P, 24, 2])
        self.halo = T("halo", [P, 16, 4])
        self.hprev = T("hprev", [P, 4])
        self.ubuf = [T("ubuf%d" % i, [P, ST + 4]) for i in range(4)]
        self.cset = T("cset", [P, 24, 16])
        self.bx = [T("bx%d" % i, [P, ST]) for i in range(2)]
        self.Sst = T("Sst", [P, 4, P])
        self.bnb = T("bnb", [P, P])
        self.bgt = T("bgt", [P, 4, 12])
        self.eg = T("eg", [P, 24])
        self.ccar = T("ccar", [P, 4, 16])
        self.ctmp = Res()

    def warena_view(self, off, shape):
        n = int(np.prod(shape))
        ap = self.warena[:, off:off + n]
        if len(shape) == 2:
            ap = ap.rearrange("p (a b) -> p a b", a=shape[0])
        return ap, off + n

    def load_consts(self):
        S = self.S
        for nm, tl in [("ident", self.ident), ("uincl", self.uincl), ("lstrict", self.lstrict),
                       ("mincl_t", self.mincl_t), ("svec", self.svec), ("ones", self.ones_f)]:
            S.dma("sp", tl[:], self.C[nm], outs=[tl])
        S.dma("sp", self.lvmask[:], self.C["lvmask"].rearrange("l p q -> p l q"), outs=[self.lvmask])
        S.op("dve", lambda e: e.tensor_copy(out=self.identb[:], in_=self.ident[:]), outs=[self.identb], ins=[self.ident])
        S.op("pool", lambda e: e.memset(self.epsc[:, 0:1], EPS), outs=[self.epsc])
        S.op("pool", lambda e: e.memset(self.epsc[:, 1:2], 1.0), outs=[self.epsc])
        S.op("dve", lambda e: e.tensor_scalar(out=self.ones_b[:], in0=self.ones_f[:], scalar1=1.0 / D, scalar2=None,
                                              op0=ALU.mult), outs=[self.ones_b], ins=[self.ones_f])

    def load_w(self, dst_ap, src_ap, dst_res):
        S = self.S
        nk = src_ap.shape[0] // P
        v = src_ap.rearrange("(kc p) n -> p kc n", p=P)
        for kc in range(nk):
            S.dma("pool", dst_ap[:, kc, :], v[:, kc, :], outs=[dst_res])

    def load_vec(self, dst_ap, src_ap, dst_res):
        self.S.dma("sp", dst_ap, src_ap.rearrange("(c p) -> p c", p=P), outs=[dst_res])

    def mm(self, ps, ps_ap, lhsT, rhs, start, stop, ins):
        self.S.op("pe", lambda e: e.matmul(ps_ap, lhsT=lhsT, rhs=rhs, start=start, stop=stop), outs=[ps], ins=ins)

    def load_x_norm(self, sup, gres_ready=True, want_h=True, from_hts=False):
        S = self.S
        c0 = sup * ST
        xv = self.xT.rearrange("(kc p) t -> p kc t", p=P)
        S.dma("sp", self.xs[:], xv[:, :, c0:c0 + ST], outs=[self.xs], ins=[self.r_xT[sup]])
        if not want_h:
            return
        if from_hts:
            hv = self.hTs.rearrange("(kc p) t -> p kc t", p=P)
            S.dma("sp", self.hT[:], hv[:, :, c0:c0 + ST], outs=[self.hT], ins=[self.r_hT[sup]])
            return
        self.norm_from_xs()

    def load_xs(self, sup):
        c0 = sup * ST
        xv = self.xT.rearrange("(kc p) t -> p kc t", p=P)
        self.S.dma("sp", self.xs[:], xv[:, :, c0:c0 + ST], outs=[self.xs], ins=[self.r_xT[sup]])

    def load_norm_prefetch(self, sup):
        nsup = self.NTOK // ST
        if sup == 0:
            self.load_xs(0)
        self.norm_from_xs()
        if sup + 1 < nsup:
            self.load_xs(sup + 1)

    def norm_from_xs(self):
        S = self.S
        S.op("act", lambda e: e.activation(out=self.xsq[:, 0:KC, :], in_=self.xs[:], func=AF.Square), outs=[self.xsq], ins=[self.xs])
        ps = S.psum()
        for kc in range(KC):
            self.mm(ps, ps[:, :], self.ones_b[:], self.xsq[:, kc, :], kc == 0, kc == KC - 1, [self.ones_b, self.xsq])
        S.op("act", lambda e: e.activation(out=self.rstd[:], in_=ps[:, :], func=AF.Sqrt, bias=self.epsc[:, 0:1], scale=1.0), outs=[self.rstd], ins=[ps, self.epsc])
        S.op("dve", lambda e: e.reciprocal(out=self.rstd[:], in_=self.rstd[:]), outs=[self.rstd], ins=[self.rstd])
        for kc in range(KC):
            eng = "dve"
            S.op(eng, lambda e, kc=kc: e.scalar_tensor_tensor(out=self.hT[:, kc, :], in0=self.xs[:, kc, :],
                                                               scalar=self.gv[:, kc:kc + 1], in1=self.rstd[:],
                                                               op0=ALU.mult, op1=ALU.mult),
                 outs=[self.hT], ins=[self.xs, self.gv, self.rstd])

    def store_x(self, sup):
        c0 = sup * ST
        xv = self.xT.rearrange("(kc p) t -> p kc t", p=P)
        self.S.dma("sp", xv[:, :, c0:c0 + ST], self.xs[:], outs=[self.r_xT[sup]], ins=[self.xs])

    def gelu(self, out_ap, out_res, in_ap, in_res, wa, wb):
        S = self.S
        n = in_ap.shape[-1]
        S.op("act", lambda e: e.activation(out=wa[:, :n], in_=in_ap, func=AF.Square), outs=[wa], ins=[in_res])
        S.op("pool", lambda e: e.tensor_scalar(out=wa[:, :n], in0=wa[:, :n], scalar1=0.044715, scalar2=1.0, op0=ALU.mult,
                                               op1=ALU.add), outs=[wa], ins=[wa])
        S.op("pool", lambda e: e.tensor_tensor(out=wb[:, :n], in0=wa[:, :n], in1=in_ap, op=ALU.mult), outs=[wb], ins=[wa, in_res])
        S.op("act", lambda e: e.activation(out=wb[:, :n], in_=wb[:, :n], func=AF.Sigmoid, scale=2.0 * GELU_C), outs=[wb], ins=[wb])
        S.op("dve", lambda e: e.tensor_tensor(out=out_ap, in0=wb[:, :n], in1=in_ap, op=ALU.mult), outs=[out_res], ins=[wb, in_res])

    def phase_in(self):
        S = self.S
        for sup in range(self.NTOK // ST):
            for sub in range(NSUB):
                t0 = sup * ST + sub * P
                S.dma("sp", self.tok[:], self.x_in[t0:t0 + P, :], outs=[self.tok])
                for half in range(2):
                    ps = S.psum()
                    for q in range(4):
                        kc = half * 4 + q
                        S.op("pe", lambda e, ps=ps, q=q, kc=kc: e.transpose(out=ps[:, q * P:(q + 1) * P],
                                                                            in_=self.tok[:, kc * P:(kc + 1) * P],
                                                                            identity=self.ident[:]),
                             outs=[ps], ins=[self.tok, self.ident])
                    eng = "act" if half == 0 else "dve"
                    if eng == "act":
                        S.op("act", lambda e, ps=ps, half=half, sub=sub: e.copy(
                            out=self.xs[:, half * 4:half * 4 + 4, sub * P:(sub + 1) * P],
                            in_=ps[:, :].rearrange("p (a b) -> p a b", a=4)), outs=[self.xs], ins=[ps])
                    else:
                        S.op("dve", lambda e, ps=ps, half=half, sub=sub: e.tensor_copy(
                            out=self.xs[:, half * 4:half * 4 + 4, sub * P:(sub + 1) * P],
                            in_=ps[:, :].rearrange("p (a b) -> p a b", a=4)), outs=[self.xs], ins=[ps])
            self.store_x(sup)

    def phase_out(self):
        S = self.S
        self.load_vec(self.gv[:], self.W["final_norm"], self.gv)
        hf = self.work
        for sup in range(self.NTOK // ST):
            c0 = sup * ST
            xv = self.xT.rearrange("(kc p) t -> p kc t", p=P)
            S.dma("sp", self.xs[:], xv[:, :, c0:c0 + ST], outs=[self.xs], ins=[self.r_xT[sup]])
            S.op("act", lambda e: e.activation(out=self.xsq[:, 0:KC, :], in_=self.xs[:], func=AF.Square), outs=[self.xsq], ins=[self.xs])
            ps = S.psum()
            for kc in range(KC):
                self.mm(ps, ps[:, :], self.ones_b[:], self.xsq[:, kc, :], kc == 0, kc == KC - 1, [self.ones_b, self.xsq])
            S.op("act", lambda e, ps=ps: e.activation(out=self.rstd[:], in_=ps[:, :], func=AF.Sqrt, bias=self.epsc[:, 0:1], scale=1.0), outs=[self.rstd], ins=[ps, self.epsc])
            S.op("dve", lambda e: e.reciprocal(out=self.rstd[:], in_=self.rstd[:]), outs=[self.rstd], ins=[self.rstd])
            for kc in range(KC):
                eng = "dve"
                S.op(eng, lambda e, kc=kc: e.scalar_tensor_tensor(out=hf[kc][:, :], in0=self.xs[:, kc, :],
                                                                   scalar=self.gv[:, kc:kc + 1], in1=self.rstd[:],
                                                                   op0=ALU.mult, op1=ALU.mult),
                     outs=[hf[kc]], ins=[self.xs, self.gv, self.rstd])
            for sub in range(NSUB):
                for half in range(2):
                    ps = S.psum()
                    for q in range(4):
                        kc = half * 4 + q
                        S.op("pe", lambda e, ps=ps, q=q, kc=kc, sub=sub: e.transpose(
                            out=ps[:, q * P:(q + 1) * P], in_=hf[kc][:, sub * P:(sub + 1) * P], identity=self.ident[:]),
                             outs=[ps], ins=[hf[kc], self.ident])
                    if half == 0:
                        S.op("act", lambda e, ps=ps: e.copy(out=self.tok[:, 0:512], in_=ps[:, :]), outs=[self.tok], ins=[ps])
                    else:
                        S.op("dve", lambda e, ps=ps: e.tensor_copy(out=self.tok[:, 512:1024], in_=ps[:, :]), outs=[self.tok], ins=[ps])
                t0 = c0 + sub * P
                S.dma("sp", self.out[t0:t0 + P, :], self.tok[:], ins=[self.tok])

    def phase_dbg(self):
        S = self.S
        for sup in range(self.NTOK // ST):
            c0 = sup * ST
            xv = self.xT.rearrange("(kc p) t -> p kc t", p=P)
            S.dma("sp", self.xs[:], xv[:, :, c0:c0 + ST], outs=[self.xs], ins=[self.r_xT[sup]])
            S.dma("sp", self.dbg_x.rearrange("(kc p) t -> p kc t", p=P)[:, :, c0:c0 + ST], self.xs[:], ins=[self.xs])
            yv = self.yT.rearrange("(c p) t -> p c t", p=P)
            S.dma("sp", self.big_b[:], yv[:, :, c0:c0 + ST], outs=[self.big_b], ins=[r[sup] for r in self.r_yT])
            S.dma("sp", self.dbg_y.rearrange("(c p) t -> p c t", p=P)[:, :, c0:c0 + ST], self.big_b[:], ins=[self.big_b])

    def ffn_load(self, l, j):
        S = self.S
        off = self.w_off
        HW = DFF // 2
        self.wup, off = self.warena_view(off, [KC, 2 * HW])
        self.wdn, off = self.warena_view(off, [12, D])
        wu = self.W["ffn_w_up"][l]
        self.load_w(self.wup[:, :, 0:HW], wu[:, j * HW:(j + 1) * HW], self.warena)
        self.load_w(self.wup[:, :, HW:2 * HW], wu[:, DFF + j * HW:DFF + (j + 1) * HW], self.warena)
        self.load_w(self.wdn, self.W["ffn_w_down"][l][j * HW:(j + 1) * HW, :], self.warena)
        cw = self.W["ffn_conv_w"][l]
        cb = self.W["ffn_conv_b"][l]
        for i in range(3):
            for g in range(2):
                src = cw[i, g * DFF + j * HW: g * DFF + (j + 1) * HW]
                self.load_vec(self.vec[:, i * 24 + g * 12: i * 24 + g * 12 + 12], src, self.vec)
        for g in range(2):
            src = cb[g * DFF + j * HW: g * DFF + (j + 1) * HW]
            self.load_vec(self.vec[:, 72 + g * 12: 72 + g * 12 + 12], src, self.vec)

    def ffn_super(self, sup, first_in_seq, mid_hook=None):
        S = self.S
        HW = DFF // 2
        mT = self.big_b
        if first_in_seq:
            S.op("pool", lambda e: e.memset(self.tail[:], 0.0), outs=[self.tail])
        for c in range(12):
            acc = []
            for g in range(2):
                idx = g * 12 + c
                ps = S.psum()
                for kc in range(KC):
                    self.mm(ps, ps[:, :], self.wup[:, kc, g * HW + c * P: g * HW + (c + 1) * P], self.hT[:, kc, :],
                            kc == 0, kc == KC - 1, [self.warena, self.hT])
                ub = self.ubuf[g * 2 + (c % 2)]
                S.op("act", lambda e, ub=ub, ps=ps: e.copy(out=ub[:, 2:2 + ST], in_=ps[:, :]), outs=[ub], ins=[ps])
                S.op("pool", lambda e, ub=ub, idx=idx: e.tensor_copy(out=ub[:, 0:2], in_=self.tail[:, idx, :]), outs=[ub], ins=[self.tail])
                a = self.work[g * 2 + (c % 2)]
                w0 = self.vec[:, 0 * 24 + idx: 0 * 24 + idx + 1]
                w1 = self.vec[:, 1 * 24 + idx: 1 * 24 + idx + 1]
                w2 = self.vec[:, 2 * 24 + idx: 2 * 24 + idx + 1]
                bb = self.vec[:, 72 + idx: 72 + idx + 1]
                S.op("act", lambda e, a=a, ub=ub, w2=w2, bb=bb: e.activation(out=a[:, :], in_=ub[:, 2:2 + ST], func=AF.Identity,
                                                                             scale=w2, bias=bb), outs=[a], ins=[ub, self.vec])
                S.op("dve", lambda e, a=a, ub=ub, w1=w1: e.scalar_tensor_tensor(out=a[:, :], in0=ub[:, 1:1 + ST], scalar=w1,
                                                                                in1=a[:, :], op0=ALU.mult, op1=ALU.add),
                     outs=[a], ins=[ub, self.vec, a])
                S.op("dve", lambda e, a=a, ub=ub, w0=w0: e.scalar_tensor_tensor(out=a[:, :], in0=ub[:, 0:ST], scalar=w0,
                                                                                in1=a[:, :], op0=ALU.mult, op1=ALU.add),
                     outs=[a], ins=[ub, self.vec, a])
                S.op("pool", lambda e, ub=ub, idx=idx: e.tensor_copy(out=self.tail[:, idx, :], in_=ub[:, ST:ST + 2]), outs=[self.tail], ins=[ub])
                acc.append(a)
            gl = self.work[4 + (c % 2)]
            self.gelu(gl[:, :], gl, acc[0][:, :], acc[0], self.work[6 + (c % 2)], self.work[8 + (c % 2)])
            S.op("dve", lambda e, c=c, gl=gl, v=acc[1]: e.tensor_tensor(out=mT[:, c, :], in0=gl[:, :], in1=v[:, :], op=ALU.mult),
                 outs=[mT], ins=[gl, acc[1]])
        if mid_hook is not None:
            mid_hook()
        for ec in range(KC):
            ps = S.psum()
            for c in range(12):
                self.mm(ps, ps[:, :], self.wdn[:, c, ec * P:(ec + 1) * P], mT[:, c, :], c == 0, c == 11, [self.warena, mT])
            S.op("dve", lambda e, ec=ec, ps=ps: e.tensor_tensor(out=self.xs[:, ec, :], in0=self.xs[:, ec, :], in1=ps[:, :], op=ALU.add),
                 outs=[self.xs], ins=[self.xs, ps])

    def phase_F1(self, l):
        self.phase_F(l, 0)

    def phase_F2(self, l):
        self.phase_F(l, 1)

    def phase_F(self, l, j):
        self.w_off = 0
        self.ffn_load(l, j)
        self.S.barrier()
        nsup = self.NTOK // ST
        hv = self.hTs.rearrange("(kc p) t -> p kc t", p=P)

        def load_h(sp_):
            self.S.dma("sp", self.hT[:], hv[:, :, sp_ * ST:(sp_ + 1) * ST], outs=[self.hT], ins=[self.r_hT[sp_]])

        load_h(0)
        for sup in range(nsup):
            self.load_xs(sup)
            self.ffn_super(sup, sup % self.NST == 0, mid_hook=(lambda sup=sup: load_h(sup + 1)) if sup + 1 < nsup else None)
            self.store_x(sup)

    def do_KV(self, l, wkv, s):
        S = self.S
        gb = self.tok
        memn = [self.work[0], self.work[1]]
        for s in [s]:
            for half in range(2):
                r0 = s * MEM + half * P
                mt = self.xs
                mview = self.xs[:, 0:2, :].rearrange("p a b -> p (a b)")
                gview = self.xs[:, 2:4, :].rearrange("p a b -> p (a b)")
                S.dma("sp", mview, self.mem_in[r0:r0 + P, :], outs=[self.xs])
                S.dma("sp", gview, self.W["mem_norm"][l:l + 1, :].broadcast_to([P, D]), outs=[self.xs])
                sq = self.xs[:, 4:6, :].rearrange("p a b -> p (a b)")
                S.op("act", lambda e, sq=sq, mview=mview: e.activation(out=sq, in_=mview, func=AF.Square, accum_out=self.vec[:, 200:201]),
                     outs=[self.xs, self.vec], ins=[self.xs])
                S.op("dve", lambda e: e.tensor_scalar(out=self.vec[:, 201:202], in0=self.vec[:, 200:201], scalar1=1.0 / D, scalar2=EPS,
                                                      op0=ALU.mult, op1=ALU.add), outs=[self.vec], ins=[self.vec])
                S.op("act", lambda e: e.activation(out=self.vec[:, 201:202], in_=self.vec[:, 201:202], func=AF.Sqrt), outs=[self.vec], ins=[self.vec])
                S.op("dve", lambda e: e.reciprocal(out=self.vec[:, 201:202], in_=self.vec[:, 201:202]), outs=[self.vec], ins=[self.vec])
                S.op("dve", lambda e, mview=mview, gview=gview: e.scalar_tensor_tensor(out=self.tok[:], in0=mview, scalar=self.vec[:, 201:202],
                                                                                     in1=gview, op0=ALU.mult, op1=ALU.mult),
                     outs=[self.tok], ins=[self.xs, self.vec])
                for h2 in range(2):
                    ps = S.psum()
                    for q in range(4):
                        kc = h2 * 4 + q
                        S.op("pe", lambda e, ps=ps, q=q, kc=kc: e.transpose(out=ps[:, q * P:(q + 1) * P], in_=self.tok[:, kc * P:(kc + 1) * P],
                                                                            identity=self.ident[:]), outs=[ps], ins=[self.tok, self.ident])
                    S.op("act", lambda e, ps=ps, h2=h2, half=half: e.copy(out=self.hT[:, h2 * 4:h2 * 4 + 4, half * P:(half + 1) * P],
                                                                        in_=ps[:, :].rearrange("p (a b) -> p a b", a=4)), outs=[self.hT], ins=[ps])
            for c in range(KC):
                ps = S.psum()
                for kc in range(KC):
                    self.mm(ps, ps[:, 0:MEM], wkv[:, kc, c * P:(c + 1) * P], self.hT[:, kc, 0:MEM], kc == 0, kc == KC - 1, [self.warena, self.hT])
                S.op("act", lambda e, ps=ps, c=c, s=s: e.copy(out=self.kT[s][:, c, :], in_=ps[:, 0:MEM]), outs=[self.kT[s]], ins=[ps])
            for half in range(2):
                for cb in range(2):
                    ps = S.psum()
                    for kc in range(KC):
                        self.mm(ps, ps[:, :], self.hT[:, kc, half * P:(half + 1) * P], wkv[:, kc, D + cb * 512: D + (cb + 1) * 512],
                                kc == 0, kc == KC - 1, [self.warena, self.hT])
                    S.op("dve", lambda e, ps=ps, half=half, cb=cb, s=s: e.tensor_copy(out=self.vv[s][:, half, cb * 512:(cb + 1) * 512], in_=ps[:, :]),
                         outs=[self.vv[s]], ins=[ps])

    def phase_X(self, l):
        S = self.S
        off = 0
        wq, off = self.warena_view(off, [KC, D])
        wo, off = self.warena_view(off, [KC, D])
        self.load_w(wq, self.W["xa_w_q"][l], self.warena)
        self.load_w(wo, self.W["xa_w_o"][l], self.warena)
        wkv, off = self.warena_view(off, [KC, 2 * D])
        self.load_w(wkv, self.W["xa_w_kv"][l], self.warena)
        gx = self.vec[:, 208:216]
        gf = self.vec[:, 216:224]
        self.load_vec(gx, self.W["xa_norm"][l], self.vec)
        self.load_vec(gf, self.W["ffn_norm"][l], self.vec)
        qT = self.big_c
        oT = self.big_c
        S.barrier()
        for sup in range(self.NTOK // ST):
            s = sup // self.NST
            if sup % self.NST == 0:
                self.do_KV(l, wkv, s)
            S.op("pool", lambda e: e.tensor_copy(out=self.gv[:], in_=gx), outs=[self.gv], ins=[self.vec])
            self.load_x_norm(sup)
            for c in range(KC):
                ps = S.psum()
                for kc in range(KC):
                    self.mm(ps, ps[:, :], wq[:, kc, c * P:(c + 1) * P], self.hT[:, kc, :], kc == 0, kc == KC - 1, [self.warena, self.hT])
                S.op("act", lambda e, ps=ps, c=c: e.copy(out=qT[:, c, :], in_=ps[:, :]), outs=[qT], ins=[ps])
            oTn = self.big_b
            for sub in range(NSUB):
                sc = slice(sub * P, (sub + 1) * P)
                pss = [S.psum(), S.psum()]
                for h in range(4):
                    pb = pss[h // 2]
                    for dc in range(2):
                        self.mm(pb, pb[:, (h % 2) * MEM:(h % 2 + 1) * MEM], qT[:, 2 * h + dc, sc], self.kT[s][:, 2 * h + dc, :], dc == 0, dc == 1,
                                [qT, self.kT[s]])
                mx4 = self.vec[:, 224:228]
                sm4 = self.vec[:, 228:232]
                pws = [self.work[0], self.work[1]]
                pTs = [self.workb[0], self.workb[1]]
                for h in range(4):
                    S.op("dve", lambda e, h=h, pss=pss: e.reduce_max(out=self.vec[:, 224 + h:225 + h], in_=pss[h // 2][:, (h % 2) * MEM:(h % 2 + 1) * MEM], axis=AX.X),
                         outs=[self.vec], ins=[pss[h // 2]])
                S.op("dve", lambda e: e.tensor_scalar(out=mx4, in0=mx4, scalar1=-1.0 / 16.0, scalar2=None, op0=ALU.mult), outs=[self.vec], ins=[self.vec])
                for h in range(4):
                    b, hh = h // 2, h % 2
                    S.op("act", lambda e, b=b, hh=hh, h=h, pss=pss: e.activation(out=pws[b][:, hh * MEM:(hh + 1) * MEM], in_=pss[b][:, hh * MEM:(hh + 1) * MEM], func=AF.Exp,
                                                                        scale=1.0 / 16.0, bias=self.vec[:, 224 + h:225 + h]), outs=[pws[b]], ins=[pss[b], self.vec])
                for h in range(4):
                    S.op("dve", lambda e, h=h: e.reduce_sum(out=self.vec[:, 228 + h:229 + h], in_=pws[h // 2][:, (h % 2) * MEM:(h % 2 + 1) * MEM], axis=AX.X),
                         outs=[self.vec], ins=[pws[h // 2]])
                S.op("dve", lambda e: e.reciprocal(out=sm4, in_=sm4), outs=[self.vec], ins=[self.vec])
                for h in range(4):
                    S.op("dve", lambda e, h=h: e.tensor_scalar(out=pws[h // 2][:, (h % 2) * MEM:(h % 2 + 1) * MEM], in0=pws[h // 2][:, (h % 2) * MEM:(h % 2 + 1) * MEM],
                                                               scalar1=self.vec[:, 228 + h:229 + h], scalar2=None, op0=ALU.mult), outs=[pws[h // 2]], ins=[pws[h // 2], self.vec])
                for b in range(2):
                    ps2 = S.psum()
                    for q in range(4):
                        S.op("pe", lambda e, ps2=ps2, b=b, q=q: e.transpose(out=ps2[:, q * P:(q + 1) * P], in_=pws[b][:, q * P:(q + 1) * P], identity=self.ident[:]),
                             outs=[ps2], ins=[pws[b], self.ident])
                    S.op("act", lambda e, ps2=ps2, b=b: e.copy(out=pTs[b][:, :], in_=ps2[:, :]), outs=[pTs[b]], ins=[ps2])
                for b in range(2):
                    ps3 = S.psum()
                    for hh in range(2):
                        h = 2 * b + hh
                        for dc in range(2):
                            for mh in range(2):
                                self.mm(ps3, ps3[:, (hh * 2 + dc) * P:(hh * 2 + dc + 1) * P], self.vv[s][:, mh, (2 * h + dc) * P:(2 * h + dc + 1) * P],
                                        pTs[b][:, (hh * 2 + mh) * P:(hh * 2 + mh + 1) * P], mh == 0, mh == 1, [self.vv[s], pTs[b]])
                    S.op("dve", lambda e, ps3=ps3, b=b, sc=sc: e.tensor_copy(out=oTn[:, 4 * b:4 * b + 4, sc], in_=ps3[:, :].rearrange("p (a b) -> p a b", a=4)),
                         outs=[oTn], ins=[ps3])
            for c in range(KC):
                ps = S.psum()
                for kc in range(KC):
                    self.mm(ps, ps[:, :], wo[:, kc, c * P:(c + 1) * P], oTn[:, kc, :], kc == 0, kc == KC - 1, [self.warena, oTn])
                S.op("dve", lambda e, ps=ps, c=c: e.tensor_tensor(out=self.xs[:, c, :], in0=self.xs[:, c, :], in1=ps[:, :], op=ALU.add),
                     outs=[self.xs], ins=[self.xs, ps])
            S.op("pool", lambda e: e.tensor_copy(out=self.gv[:], in_=gf), outs=[self.gv], ins=[self.vec])
            self.norm_from_xs()
            c0 = sup * ST
            hv = self.hTs.rearrange("(kc p) t -> p kc t", p=P)
            S.dma("sp", hv[:, :, c0:c0 + ST], self.hT[:], outs=[self.r_hT[sup]], ins=[self.hT])
            self.store_x(sup)

    def phase_M2(self, l):
        S = self.S
        off = 0
        wg, off = self.warena_view(off, [KC, 3 * D])
        wb, off = self.warena_view(off, [12, D])
        wo, off = self.warena_view(off, [KC, D])
        import os
        SK = os.environ.get("M2_SKIP", "")
        if "g" not in SK:
            self.load_w(wg, self.W["w_in"][l][:, 3592:DIN], self.warena)
        if "b" not in SK:
            self.load_w(wb, self.W["w_branch"][l].rearrange("k c d -> (k c) d"), self.warena)
        if "o" not in SK:
            self.load_w(wo, self.W["w_out"][l], self.warena)
        self.load_vec(self.gv[:], self.W["mix_norm"][l], self.gv)
        bg = self.vec[:, 0:24]
        if "v" not in SK:
            self.load_vec(bg, self.W["b_gate"][l], self.vec)
        if "c" in SK:
            return
        yT = self.big_b
        mg = self.big_c
        S.barrier()
        for sup in range(self.NTOK // ST):
            c0 = sup * ST
            self.load_x_norm(sup)
            yv = self.yT.rearrange("(c p) t -> p c t", p=P)
            for k in range(3):
                S.dma("sp", yT[:, 4 * k:4 * k + 4, :], yv[:, 4 * k:4 * k + 4, c0:c0 + ST], outs=[yT], ins=[self.r_yT[k][sup]])
            for dch in range(KC):
                acc = self.work[dch % 2]
                for k in range(3):
                    ps = S.psum()
                    for kc in range(KC):
                        self.mm(ps, ps[:, :], wg[:, kc, k * D + dch * P: k * D + (dch + 1) * P], self.hT[:, kc, :], kc == 0, kc == KC - 1,
                                [self.warena, self.hT])
                    gt = self.work[2 + (k % 2)]
                    bcol = self.vec[:, k * 8 + dch: k * 8 + dch + 1]
                    S.op("act", lambda e, ps=ps, gt=gt, bcol=bcol: e.activation(out=gt[:, :], in_=ps[:, :], func=AF.Sigmoid, bias=bcol, scale=1.0),
                         outs=[gt], ins=[ps, self.vec])
                    ps2 = S.psum()
                    for c in range(4):
                        self.mm(ps2, ps2[:, :], wb[:, 4 * k + c, dch * P:(dch + 1) * P], yT[:, 4 * k + c, :], c == 0, c == 3, [self.warena, yT])
                    if k == 0:
                        S.op("dve", lambda e, acc=acc, gt=gt, ps2=ps2: e.tensor_tensor(out=acc[:, :], in0=gt[:, :], in1=ps2[:, :], op=ALU.mult),
                             outs=[acc], ins=[gt, ps2])
                    else:
                        S.op("dve", lambda e, gt=gt, ps2=ps2: e.tensor_tensor(out=gt[:, :], in0=gt[:, :], in1=ps2[:, :], op=ALU.mult),
                             outs=[gt], ins=[gt, ps2])
                        if k == 1:
                            S.op("pool", lambda e, acc=acc, gt=gt: e.tensor_tensor(out=acc[:, :], in0=acc[:, :], in1=gt[:, :], op=ALU.add),
                                 outs=[acc], ins=[acc, gt])
                        else:
                            S.op("pool", lambda e, acc=acc, gt=gt, dch=dch: e.tensor_tensor(out=mg[:, dch, :], in0=acc[:, :], in1=gt[:, :], op=ALU.add),
                                 outs=[mg], ins=[acc, gt])
            for ec in range(KC):
                ps = S.psum()
                for dch in range(KC):
                    self.mm(ps, ps[:, :], wo[:, dch, ec * P:(ec + 1) * P], mg[:, dch, :], dch == 0, dch == KC - 1, [self.warena, mg])
                S.op("dve", lambda e, ec=ec, ps=ps: e.tensor_tensor(out=self.xs[:, ec, :], in0=self.xs[:, ec, :], in1=ps[:, :], op=ALU.add),
                     outs=[self.xs], ins=[self.xs, ps])
            self.store_x(sup)

    def store_y(self, k, sup, src_ap, src_res, c, ncols=ST, coff=0):
        c0 = sup * ST + coff
        yv = self.yT.rearrange("(c p) t -> p c t", p=P)
        self.S.dma("sp", yv[:, 4 * k + c, c0:c0 + ncols], src_ap, outs=[self.r_yT[k][sup]], ins=[src_res])

    def phase_A(self, l):
        S = self.S
        off = 0
        wa, off = self.warena_view(off, [KC, 1024])
        wr, off = self.warena_view(off, [4, P])
        wi, off = self.warena_view(off, [4, P])
        self.load_w(wa, self.W["w_in"][l][:, 0:1024], self.warena)
        self.load_vec(self.gv[:], self.W["mix_norm"][l], self.gv)
        stg = self.tok
        S.op("pool", lambda e: e.memset(stg[:], 0.0), outs=[stg])
        for wi_, nm in enumerate(["a_w_r", "a_w_i"]):
            for h in range(8):
                c, hh = h // 2, h % 2
                S.dma("sp", stg[hh * 64:(hh + 1) * 64, wi_ * 512 + c * P + hh * 64: wi_ * 512 + c * P + hh * 64 + 64],
                      self.W[nm][l, h], outs=[stg])
        S.barrier()
        S.op("dve", lambda e: e.tensor_copy(out=wr, in_=stg[:, 0:512].rearrange("p (a b) -> p a b", a=4)), outs=[self.warena], ins=[stg])
        S.op("dve", lambda e: e.tensor_copy(out=wi, in_=stg[:, 512:1024].rearrange("p (a b) -> p a b", a=4)), outs=[self.warena], ins=[stg])
        V = self.vec
        cw = V[:, 0:16]
        for i in range(4):
            self.load_vec(V[:, i * 4:(i + 1) * 4], self.W["a_conv_w"][l, i], V)
        self.load_vec(V[:, 16:20], self.W["a_conv_b"][l], V)
        self.load_vec(V[:, 20:24], self.W["a_b_r"][l], V)
        self.load_vec(V[:, 24:28], self.W["a_b_i"][l], V)
        self.load_vec(V[:, 28:32], self.W["a_lam"][l], V)
        S.barrier()
        S.op("act", lambda e: e.activation(out=V[:, 32:36], in_=V[:, 28:32], func=AF.Exp, scale=-1.0), outs=[V], ins=[V])
        S.op("act", lambda e: e.activation(out=V[:, 32:36], in_=V[:, 32:36], func=AF.Ln, bias=self.epsc[:, 1:2], scale=1.0), outs=[V], ins=[V, self.epsc])
        S.op("dve", lambda e: e.tensor_scalar(out=V[:, 36:40], in0=V[:, 32:36], scalar1=-8.0, scalar2=None, op0=ALU.mult), outs=[V], ins=[V])
        S.op("dve", lambda e: e.tensor_scalar(out=V[:, 40:44], in0=V[:, 32:36], scalar1=-16.0, scalar2=None, op0=ALU.mult), outs=[V], ins=[V])
        Wk = self.work
        S.barrier()
        for sup in range(self.NTOK // ST):
            first = (sup % self.NST == 0)
            self.load_norm_prefetch(sup)
            if first:
                S.op("pool", lambda e: e.memset(self.halo[:, 0:4, :], 0.0), outs=[self.halo])
                S.op("pool", lambda e: e.memset(self.hprev[:], 0.0), outs=[self.hprev])
            for c in range(4):
                ps = S.psum()
                for kc in range(KC):
                    self.mm(ps, ps[:, :], wa[:, kc, c * P:(c + 1) * P], self.hT[:, kc, :], kc == 0, kc == KC - 1, [self.warena, self.hT])
                ub = self.ubuf[c % 2]
                S.op("act", lambda e, ub=ub, ps=ps: e.copy(out=ub[:, 3:3 + ST], in_=ps[:, :]), outs=[ub], ins=[ps])
                S.op("pool", lambda e, ub=ub, c=c: e.tensor_copy(out=ub[:, 0:3], in_=self.halo[:, c, 0:3]), outs=[ub], ins=[self.halo])
                xc = Wk[0]
                S.op("dve", lambda e, ub=ub, c=c, xc=xc: e.tensor_scalar(out=xc[:, :], in0=ub[:, 3:3 + ST], scalar1=V[:, 12 + c:13 + c],
                                                                      scalar2=V[:, 16 + c:17 + c], op0=ALU.mult, op1=ALU.add),
                     outs=[xc], ins=[ub, V])
                for i in range(3):
                    S.op("dve", lambda e, ub=ub, c=c, i=i, xc=xc: e.scalar_tensor_tensor(out=xc[:, :], in0=ub[:, i:i + ST], scalar=V[:, i * 4 + c:i * 4 + c + 1],
                                                                                      in1=xc[:, :], op0=ALU.mult, op1=ALU.add),
                         outs=[xc], ins=[ub, V, xc])
                S.op("pool", lambda e, ub=ub, c=c: e.tensor_copy(out=self.halo[:, c, 0:3], in_=ub[:, ST:ST + 3]), outs=[self.halo], ins=[ub])
                xcb = self.workb[0]
                S.op("act", lambda e, xcb=xcb, xc=xc: e.copy(out=xcb[:, :], in_=xc[:, :]), outs=[xcb], ins=[xc])
                psg = S.psum()
                for kc in range(KC):
                    self.mm(psg, psg[:, :], wa[:, kc, 512 + c * P: 512 + (c + 1) * P], self.hT[:, kc, :], kc == 0, kc == KC - 1, [self.warena, self.hT])
                ga = Wk[1]
                S.op("act", lambda e, ga=ga, psg=psg: e.copy(out=ga[:, :], in_=psg[:, :]), outs=[ga], ins=[psg])
                gg = Wk[2]
                self.gelu(gg[:, :], gg, ga[:, :], ga, Wk[3], Wk[4])
                psr = S.psum()
                self.mm(psr, psr[:, :], wr[:, c, :], xcb[:, :], True, True, [self.warena, xcb])
                psi = S.psum()
                self.mm(psi, psi[:, :], wi[:, c, :], xcb[:, :], True, True, [self.warena, xcb])
                r = Wk[5]
                ig = Wk[6]
                S.op("act", lambda e, r=r, psr=psr, c=c: e.activation(out=r[:, :], in_=psr[:, :], func=AF.Sigmoid, bias=V[:, 20 + c:21 + c], scale=1.0),
                     outs=[r], ins=[psr, V])
                S.op("act", lambda e, ig=ig, psi=psi, c=c: e.activation(out=ig[:, :], in_=psi[:, :], func=AF.Sigmoid, bias=V[:, 24 + c:25 + c], scale=1.0),
                     outs=[ig], ins=[psi, V])
                a = Wk[7]
                a2 = Wk[8]
                S.op("act", lambda e, a=a, r=r, c=c: e.activation(out=a[:, :], in_=r[:, :], func=AF.Exp, scale=V[:, 36 + c:37 + c]), outs=[a], ins=[r, V])
                S.op("act", lambda e, a2=a2, r=r, c=c: e.activation(out=a2[:, :], in_=r[:, :], func=AF.Exp, scale=V[:, 40 + c:41 + c]), outs=[a2], ins=[r, V])
                S.op("act", lambda e, a2=a2: e.activation(out=a2[:, :], in_=a2[:, :], func=AF.Relu, scale=-1.0, bias=self.epsc[:, 1:2]),
                     outs=[a2], ins=[a2, self.epsc])
                S.op("act", lambda e, a2=a2: e.activation(out=a2[:, :], in_=a2[:, :], func=AF.Sqrt), outs=[a2], ins=[a2])
                if first:
                    S.op("dve", lambda e, a2=a2: e.memset(a2[:, 0:1], 1.0), outs=[a2])
                S.op("dve", lambda e, a2=a2, ig=ig: e.tensor_tensor(out=a2[:, :], in0=a2[:, :], in1=ig[:, :], op=ALU.mult), outs=[a2], ins=[a2, ig])
                S.op("dve", lambda e, a2=a2, xc=xc: e.tensor_tensor(out=a2[:, :], in0=a2[:, :], in1=xc[:, :], op=ALU.mult), outs=[a2], ins=[a2, xc])
                hs = Wk[9]
                S.op("dve", lambda e, hs=hs, a=a, a2=a2, c=c: e.tensor_tensor_scan(out=hs[:, :], data0=a[:, :], data1=a2[:, :],
                                                                                 initial=self.hprev[:, c:c + 1], op0=ALU.mult, op1=ALU.add),
                     outs=[hs], ins=[a, a2, self.hprev])
                S.op("dve", lambda e, hs=hs, c=c: e.tensor_copy(out=self.hprev[:, c:c + 1], in_=hs[:, ST - 1:ST]), outs=[self.hprev], ins=[hs])
                ya = self.workb[1 + (c % 2)]
                S.op("dve", lambda e, ya=ya, hs=hs, gg=gg: e.tensor_tensor(out=ya[:, :], in0=hs[:, :], in1=gg[:, :], op=ALU.mult), outs=[ya], ins=[hs, gg])
                self.store_y(0, sup, ya[:, :], ya, c)

    def sincos(self, sin_ap, cos_ap, ang_ap, n, ta, tb, tc, res_out, res_in, res_tmp, res_out2=None):
        S = self.S
        I32 = mybir.dt.int32
        S.op("dve", lambda e: e.tensor_scalar(out=ta, in0=ang_ap, scalar1=1.0 / (2.0 * np.pi), scalar2=None, op0=ALU.mult), outs=[res_tmp], ins=[res_in])
        S.op("dve", lambda e: e.tensor_copy(out=tb.bitcast(I32), in_=ta), outs=[res_tmp], ins=[res_tmp])
        S.op("dve", lambda e: e.tensor_copy(out=tc, in_=tb.bitcast(I32)), outs=[res_tmp], ins=[res_tmp])
        S.op("dve", lambda e: e.tensor_tensor(out=ta, in0=ta, in1=tc, op=ALU.subtract), outs=[res_tmp], ins=[res_tmp])
        res_out2 = res_out if res_out2 is None else res_out2
        for (dst, shift, res_out) in [(sin_ap, 0.0, res_out), (cos_ap, 0.25, res_out2)]:
            S.op("dve", lambda e, shift=shift: e.tensor_scalar(out=tb, in0=ta, scalar1=shift, scalar2=None, op0=ALU.add), outs=[res_tmp], ins=[res_tmp])
            for _ in range(2):
                S.op("dve", lambda e: e.tensor_scalar(out=tc, in0=tb, scalar1=0.5, scalar2=None, op0=ALU.is_gt), outs=[res_tmp], ins=[res_tmp])
                S.op("dve", lambda e: e.tensor_tensor(out=tb, in0=tb, in1=tc, op=ALU.subtract), outs=[res_tmp], ins=[res_tmp])
            S.op("dve", lambda e: e.tensor_scalar(out=tc, in0=tb, scalar1=-0.5, scalar2=None, op0=ALU.is_lt), outs=[res_tmp], ins=[res_tmp])
            S.op("dve", lambda e: e.tensor_tensor(out=tb, in0=tb, in1=tc, op=ALU.add), outs=[res_tmp], ins=[res_tmp])
            S.op("act", lambda e, dst=dst: e.activation(out=dst, in_=tb, func=AF.Sin, scale=6.28318), outs=[res_out], ins=[res_tmp])

    def phase_C(self, l):
        S = self.S
        off = 0
        wuc, off = self.warena_view(off, [KC, 512])
        wgl, off = self.warena_view(off, [4, 512])
        BrT, off = self.warena_view(off, [16, P])
        BiT, off = self.warena_view(off, [16, P])
        CrT, off = self.warena_view(off, [16, P])
        nCrT, off = self.warena_view(off, [16, P])
        nCiT, off = self.warena_view(off, [16, P])
        WA = self.warena
        self.load_w(wuc, self.W["w_in"][l][:, 3080:3592], WA)
        self.load_w(wgl, self.W["c_glu_w"][l], WA)
        self.load_vec(self.gv[:], self.W["mix_norm"][l], self.gv)
        V = self.vec
        self.load_vec(V[:, 0:4], self.W["c_d"][l], V)
        self.load_vec(V[:, 4:8], self.W["c_glu_b"][l], V)
        cs = self.cset
        for slot, nm in [(0, "c_lam_re"), (1, "c_lam_im")]:
            S.dma("sp", cs[:, slot, :], self.W[nm][l].rearrange("(st g2) p -> g2 p st", g2=2).rearrange("g2 p st -> (g2 p) st"), outs=[cs])
        ldt = self.W["c_log_dt"][l].rearrange("(st g2) -> g2 st", g2=2)
        for g2 in range(2):
            S.dma("sp", cs[g2 * 64:(g2 + 1) * 64, 2, :], ldt[g2:g2 + 1, :].broadcast_to([64, 16]), outs=[cs])
        sl = lambda k: cs[:, k, :]
        S.barrier()
        S.op("act", lambda e: e.activation(out=sl(2), in_=sl(2), func=AF.Exp), outs=[cs], ins=[cs])
        S.op("dve", lambda e: e.tensor_tensor(out=sl(3), in0=sl(0), in1=sl(2), op=ALU.mult), outs=[cs], ins=[cs])
        S.op("act", lambda e: e.activation(out=sl(12), in_=sl(3), func=AF.Exp), outs=[cs], ins=[cs])
        S.op("dve", lambda e: e.tensor_tensor(out=sl(4), in0=sl(1), in1=sl(2), op=ALU.mult), outs=[cs], ins=[cs])
        self.sincos(sl(5), sl(6), sl(4), 16, sl(16), sl(17), sl(18), cs, cs, cs)
        S.op("dve", lambda e: e.tensor_tensor(out=sl(7), in0=sl(12), in1=sl(6), op=ALU.mult), outs=[cs], ins=[cs])
        S.op("dve", lambda e: e.tensor_tensor(out=sl(8), in0=sl(12), in1=sl(5), op=ALU.mult), outs=[cs], ins=[cs])
        S.op("dve", lambda e: e.tensor_tensor(out=sl(9), in0=sl(0), in1=sl(0), op=ALU.mult), outs=[cs], ins=[cs])
        S.op("dve", lambda e: e.tensor_tensor(out=sl(15), in0=sl(1), in1=sl(1), op=ALU.mult), outs=[cs], ins=[cs])
        S.op("dve", lambda e: e.tensor_tensor(out=sl(9), in0=sl(9), in1=sl(15), op=ALU.add), outs=[cs], ins=[cs])
        S.op("dve", lambda e: e.reciprocal(out=sl(9), in_=sl(9)), outs=[cs], ins=[cs])
        S.op("dve", lambda e: e.tensor_scalar(out=sl(19), in0=sl(7), scalar1=-1.0, scalar2=None, op0=ALU.add), outs=[cs], ins=[cs])
        S.op("dve", lambda e: e.tensor_tensor(out=sl(10), in0=sl(19), in1=sl(0), op=ALU.mult), outs=[cs], ins=[cs])
        S.op("dve", lambda e: e.tensor_tensor(out=sl(15), in0=sl(8), in1=sl(1), op=ALU.mult), outs=[cs], ins=[cs])
        S.op("dve", lambda e: e.tensor_tensor(out=sl(10), in0=sl(10), in1=sl(15), op=ALU.add), outs=[cs], ins=[cs])
        S.op("dve", lambda e: e.tensor_tensor(out=sl(10), in0=sl(10), in1=sl(9), op=ALU.mult), outs=[cs], ins=[cs])
        S.op("dve", lambda e: e.tensor_tensor(out=sl(11), in0=sl(8), in1=sl(0), op=ALU.mult), outs=[cs], ins=[cs])
        S.op("dve", lambda e: e.tensor_tensor(out=sl(15), in0=sl(19), in1=sl(1), op=ALU.mult), outs=[cs], ins=[cs])
        S.op("dve", lambda e: e.tensor_tensor(out=sl(11), in0=sl(11), in1=sl(15), op=ALU.subtract), outs=[cs], ins=[cs])
        S.op("dve", lambda e: e.tensor_tensor(out=sl(11), in0=sl(11), in1=sl(9), op=ALU.mult), outs=[cs], ins=[cs])
        S.op("dve", lambda e: e.tensor_scalar(out=sl(20), in0=sl(4), scalar1=128.0, scalar2=None, op0=ALU.mult), outs=[cs], ins=[cs])
        self.sincos(sl(14), sl(13), sl(20), 16, sl(16), sl(17), sl(18), cs, cs, cs)
        S.op("dve", lambda e: e.tensor_scalar(out=sl(22), in0=sl(14), scalar1=-1.0, scalar2=None, op0=ALU.mult), outs=[cs], ins=[cs])
        xs = self.xs
        tA = xs[:, 0:4, :].rearrange("p a b -> p (a b)")
        tB = xs[:, 4:8, :].rearrange("p a b -> p (a b)")
        tC = self.tok[:, :]
        homes = [(self.lvmask, self.lvmask[:, i, :]) for i in range(14)] + [(self.tok, self.tok[:, i * P:(i + 1) * P]) for i in range(8)] \
            + [(self.bx[0], self.bx[0][:, i * P:(i + 1) * P]) for i in range(4)] + [(self.bx[1], self.bx[1][:, i * P:(i + 1) * P]) for i in range(4)] \
            + [(self.Sst, self.Sst[:, i, :]) for i in range(4)]
        cosH = homes[0:16]
        sinH = homes[16:32]
        for st in range(16):
            S.op("dve", lambda e, st=st: e.tensor_scalar(out=tA[:, 0:P], in0=self.svec[:, :], scalar1=cs[:, 4, st:st + 1], scalar2=None, op0=ALU.mult),
                 outs=[xs], ins=[self.svec, cs])
            self.sincos(sinH[st][1], cosH[st][1], tA[:, 0:P], P, tA[:, P:2 * P], tA[:, 2 * P:3 * P], tA[:, 3 * P:4 * P], sinH[st][0], xs, xs, res_out2=cosH[st][0])
        Bre = xs[:, 0, 0:256].rearrange("p (a b) -> p a b", a=16)
        Bim = xs[:, 0, 256:512].rearrange("p (a b) -> p a b", a=16)
        bbr = xs[:, 1, 0:256].rearrange("p (a b) -> p a b", a=16)
        bbi = xs[:, 1, 256:512].rearrange("p (a b) -> p a b", a=16)
        tmp = xs[:, 2, 0:256].rearrange("p (a b) -> p a b", a=16)
        Cre = xs[:, 3, 0:256].rearrange("p (a b) -> p a b", a=16)
        Cim = xs[:, 3, 256:512].rearrange("p (a b) -> p a b", a=16)
        for dst, nm in [(Bre, "c_b_re"), (Bim, "c_b_im")]:
            src = self.W[nm][l].rearrange("(st g2) p c -> g2 p st c", g2=2)
            for g2 in range(2):
                S.dma("sp", dst[g2 * 64:(g2 + 1) * 64, :, :], src[g2], outs=[xs])
        for dst, nm in [(Cre, "c_c_re"), (Cim, "c_c_im")]:
            src = self.W[nm][l].rearrange("(st g2) c p -> g2 st p c", g2=2)
            for g2 in range(2):
                for st_ in range(16):
                    S.dma("sp", dst[g2 * 64:(g2 + 1) * 64, st_, :], src[g2, st_], outs=[xs])
        S.barrier()
        frb = cs[:, 10, :].unsqueeze(2).broadcast_to([P, 16, 16])
        fib = cs[:, 11, :].unsqueeze(2).broadcast_to([P, 16, 16])
        TT = lambda o, a, b, op: S.op("dve", lambda e: e.tensor_tensor(out=o, in0=a, in1=b, op=op), outs=[xs], ins=[xs, cs])
        TT(bbr, Bre, frb, ALU.mult)
        TT(tmp, Bim, fib, ALU.mult)
        TT(bbr, bbr, tmp, ALU.subtract)
        TT(bbi, Bim, frb, ALU.mult)
        TT(tmp, Bre, fib, ALU.mult)
        TT(bbi, bbi, tmp, ALU.add)
        pad = xs[:, 4:8, :].rearrange("p a b -> p (a b)").rearrange("p (a b c) -> p a b c", a=4, b=4)
        padf = xs[:, 4:8, :].rearrange("p a b -> p (a b)").rearrange("p (s c) -> p s c", s=16)

        def fill_pad(src, scale):
            S.op("pool", lambda e: e.memset(xs[:, 4:8, :], 0.0), outs=[xs])
            s4 = src.rearrange("p (a b) c -> p a b c", b=4)
            for g2 in range(2):
                for b in range(4):
                    co = 32 * b + 16 * g2
                    S.op("dve", lambda e, g2=g2, b=b, co=co: e.tensor_scalar(out=pad[g2 * 64:(g2 + 1) * 64, :, b, co:co + 16], in0=s4[g2 * 64:(g2 + 1) * 64, :, b, :],
                                                                           scalar1=scale, scalar2=None, op0=ALU.mult), outs=[xs], ins=[xs])

        for src, dstT in [(bbr, BrT), (bbi, BiT)]:
            fill_pad(src, 1.0)
            for q4 in range(4):
                ps = S.psum()
                for q in range(4):
                    st = q4 * 4 + q
                    S.op("pe", lambda e, ps=ps, q=q, st=st: e.transpose(out=ps[:, q * P:(q + 1) * P], in_=padf[:, st, :], identity=self.ident[:]), outs=[ps], ins=[xs, self.ident])
                S.op("act", lambda e, ps=ps, q4=q4, dstT=dstT: e.copy(out=dstT[:, q4 * 4:q4 * 4 + 4, :], in_=ps[:, :].rearrange("p (a b) -> p a b", a=4)), outs=[WA], ins=[ps])
        for src, dstT, sc in [(Cre, CrT, 1.0), (Cre, nCrT, -1.0), (Cim, nCiT, -1.0)]:
            fill_pad(src, sc)
            S.op("act", lambda e, dstT=dstT: e.copy(out=dstT, in_=padf), outs=[WA], ins=[xs])

        import os
        CSTOP = os.environ.get("C_STOP", "")
        if CSTOP == "setup":
            return
        ucf = self.work[0:4]
        yv = self.work[4:8]
        ucb = self.big_b
        car = self.ccar
        S.barrier()
        for sup in range(self.NTOK // ST):
            first = (sup % self.NST == 0)
            self.load_norm_prefetch(sup)
            if first:
                S.op("pool", lambda e: e.memset(car[:], 0.0), outs=[car])
            for c in range(4):
                ps = S.psum()
                for kc in range(KC):
                    self.mm(ps, ps[:, :], wuc[:, kc, c * P:(c + 1) * P], self.hT[:, kc, :], kc == 0, kc == KC - 1, [WA, self.hT])
                S.op("act", lambda e, ps=ps, c=c: e.copy(out=ucf[c][:, :], in_=ps[:, :]), outs=[ucf[c]], ins=[ps])
                S.op("dve", lambda e, c=c: e.tensor_copy(out=ucb[:, c, :], in_=ucf[c][:, :]), outs=[ucb], ins=[ucf[c]])
            v4 = lambda ap: ap.rearrange("p (j s) -> p j s", j=4)
            t1, t2, t3, t4 = [self.ubuf[k] for k in range(4)]
            wrT, wiT = self.work[8], self.work[9]
            mbs = self.workb[0:4]
            rhoT = self.rstd
            psy_box = [None]

            def demod(st):
                c = st // 4
                cosB = cosH[st][1].unsqueeze(1).broadcast_to([P, 4, P])
                sinB = sinH[st][1].unsqueeze(1).broadcast_to([P, 4, P])
                for k, (src, tb, tr) in enumerate([(wrT, cosB, cosH[st][0]), (wiT, sinB, sinH[st][0]), (wiT, cosB, cosH[st][0]), (wrT, sinB, sinH[st][0])]):
                    S.op("dve", lambda e, k=k, src=src, tb=tb: e.tensor_tensor(out=v4(mbs[k][:, :]), in0=v4(src[:, :]), in1=tb, op=ALU.mult),
                         outs=[mbs[k]], ins=[src, tr])
                if st % 4 == 0:
                    psy_box[0] = S.psb[6 + (c % 2)]
                psy = psy_box[0]
                for k, lw in enumerate([CrT, nCrT, nCiT, nCiT]):
                    self.mm(psy, psy[:, :], lw[:, st, :], mbs[k][:, :], st % 4 == 0 and k == 0, st % 4 == 3 and k == 3, [WA, mbs[k]])
                if st % 4 == 3:
                    S.op("dve", lambda e, psy=psy, c=c: e.scalar_tensor_tensor(out=yv[c][:, :], in0=ucf[c][:, :], scalar=V[:, c:c + 1], in1=psy[:, :],
                                                                             op0=ALU.mult, op1=ALU.add), outs=[yv[c]], ins=[ucf[c], V, psy])

            S.psmod = 6
            for st in range(16):
                c = st // 4
                psr, psi = S.psum(), S.psum()
                self.mm(psr, psr[:, :], BrT[:, st, :], ucb[:, c, :], True, True, [WA, ucb])
                self.mm(psi, psi[:, :], BiT[:, st, :], ucb[:, c, :], True, True, [WA, ucb])
                cosB = cosH[st][1].unsqueeze(1).broadcast_to([P, 4, P])
                sinB = sinH[st][1].unsqueeze(1).broadcast_to([P, 4, P])
                cR, sR = cosH[st][0], sinH[st][0]
                S.op("dve", lambda e, psr=psr, cosB=cosB: e.tensor_tensor(out=v4(t1[:, 0:ST]), in0=v4(psr[:, :]), in1=cosB, op=ALU.mult), outs=[t1], ins=[psr, cR])
                S.op("dve", lambda e, psi=psi, sinB=sinB: e.tensor_tensor(out=v4(t2[:, 0:ST]), in0=v4(psi[:, :]), in1=sinB, op=ALU.mult), outs=[t2], ins=[psi, sR])
                S.op("dve", lambda e, psi=psi, cosB=cosB: e.tensor_tensor(out=v4(t3[:, 0:ST]), in0=v4(psi[:, :]), in1=cosB, op=ALU.mult), outs=[t3], ins=[psi, cR])
                S.op("dve", lambda e, psr=psr, sinB=sinB: e.tensor_tensor(out=v4(t4[:, 0:ST]), in0=v4(psr[:, :]), in1=sinB, op=ALU.mult), outs=[t4], ins=[psr, sR])
                S.op("pool", lambda e: e.tensor_tensor(out=t1[:, 0:ST], in0=t1[:, 0:ST], in1=t2[:, 0:ST], op=ALU.add), outs=[t1], ins=[t1, t2])
                S.op("pool", lambda e: e.tensor_tensor(out=t3[:, 0:ST], in0=t3[:, 0:ST], in1=t4[:, 0:ST], op=ALU.subtract), outs=[t3], ins=[t3, t4])
                if st > 0:
                    demod(st - 1)
                S.op("dve", lambda e, st=st: e.tensor_scalar(out=rhoT[:, 0:P], in0=self.ones_f[:, :], scalar1=cs[:, 12, st:st + 1], scalar2=None, op0=ALU.mult),
                     outs=[rhoT], ins=[self.ones_f, cs])
                for j in range(NSUB):
                    js = slice(j * P, (j + 1) * P)
                    S.op("dve", lambda e, js=js, st=st: e.tensor_tensor_scan(out=wrT[:, js], data0=rhoT[:, 0:P], data1=t1[:, js], initial=car[:, 0, st:st + 1],
                                                                           op0=ALU.mult, op1=ALU.add), outs=[wrT], ins=[rhoT, t1, car])
                    S.op("dve", lambda e, js=js, st=st: e.tensor_tensor_scan(out=wiT[:, js], data0=rhoT[:, 0:P], data1=t3[:, js], initial=car[:, 1, st:st + 1],
                                                                           op0=ALU.mult, op1=ALU.add), outs=[wiT], ins=[rhoT, t3, car])
                    le = j * P + P - 1
                    S.op("dve", lambda e, le=le, st=st: e.tensor_scalar(out=cs[:, 23, 0:1], in0=wrT[:, le:le + 1], scalar1=cs[:, 13, st:st + 1], scalar2=None, op0=ALU.mult),
                         outs=[self.ctmp], ins=[wrT, cs])
                    S.op("dve", lambda e, le=le, st=st: e.tensor_scalar(out=cs[:, 23, 1:2], in0=wiT[:, le:le + 1], scalar1=cs[:, 13, st:st + 1], scalar2=None, op0=ALU.mult),
                         outs=[self.ctmp], ins=[wiT, cs])
                    S.op("dve", lambda e, le=le, st=st: e.scalar_tensor_tensor(out=car[:, 0, st:st + 1], in0=wiT[:, le:le + 1], scalar=cs[:, 22, st:st + 1], in1=cs[:, 23, 0:1],
                                                                             op0=ALU.mult, op1=ALU.add), outs=[car], ins=[wiT, cs, self.ctmp])
                    S.op("dve", lambda e, le=le, st=st: e.scalar_tensor_tensor(out=car[:, 1, st:st + 1], in0=wrT[:, le:le + 1], scalar=cs[:, 14, st:st + 1], in1=cs[:, 23, 1:2],
                                                                             op0=ALU.mult, op1=ALU.add), outs=[car], ins=[wrT, cs, self.ctmp])
            demod(15)
            S.psmod = 8
            ygb = self.workb[2:6]
            for c in range(4):
                self.gelu(yv[c][:, :], yv[c], yv[c][:, :], yv[c], self.work[8], self.work[9])
                S.op("act", lambda e, c=c: e.copy(out=ygb[c][:, :], in_=yv[c][:, :]), outs=[ygb[c]], ins=[yv[c]])
            for ec in range(4):
                ps = S.psum()
                for c in range(4):
                    self.mm(ps, ps[:, :], wgl[:, c, ec * P:(ec + 1) * P], ygb[c][:, :], c == 0, c == 3, [WA, ygb[c]])
                sg = self.work[8]
                S.op("act", lambda e, ps=ps, ec=ec, sg=sg: e.activation(out=sg[:, :], in_=ps[:, :], func=AF.Sigmoid, bias=V[:, 4 + ec:5 + ec], scale=1.0), outs=[sg], ins=[ps, V])
                S.op("dve", lambda e, ec=ec, sg=sg: e.tensor_tensor(out=ucb[:, 4 + ec, :], in0=sg[:, :], in1=yv[ec][:, :], op=ALU.mult), outs=[ucb], ins=[sg, yv[ec]])
                self.store_y(2, sup, ucb[:, 4 + ec, :], ucb, ec)

    def phase_B(self, l):
        S = self.S
        off = 0
        wqkv, off = self.warena_view(off, [KC, 1536])
        wz, off = self.warena_view(off, [KC, 512])
        wba, off = self.warena_view(off, [KC, 8])
        WA = self.warena
        self.load_w(wqkv, self.W["w_in"][l][:, 1024:2560], WA)
        self.load_w(wz, self.W["w_in"][l][:, 2560:3072], WA)
        self.load_w(wba, self.W["w_in"][l][:, 3072:3080], WA)
        self.load_vec(self.gv[:], self.W["mix_norm"][l], self.gv)
        V = self.vec
        for i in range(4):
            self.load_vec(V[:, 16 + i * 12:16 + (i + 1) * 12], self.W["b_conv_w"][l, i], V)
        S.dma("sp", V[:, 80:84], self.W["b_a_log"][l:l + 1, :].broadcast_to([P, 4]), outs=[V])
        S.dma("sp", V[:, 84:88], self.W["b_dt_bias"][l:l + 1, :].broadcast_to([P, 4]), outs=[V])
        S.dma("sp", self.bnb[:], self.W["b_norm"][l:l + 1, :].broadcast_to([P, P]), outs=[self.bnb])
        S.dma("sp", self.lvmask[:], self.C["lvmask"].rearrange("l p q -> p l q"), outs=[self.lvmask])
        S.barrier()
        S.op("act", lambda e: e.activation(out=V[:, 88:92], in_=V[:, 80:84], func=AF.Exp), outs=[V], ins=[V])
        S.op("dve", lambda e: e.tensor_scalar(out=V[:, 88:92], in0=V[:, 88:92], scalar1=-1.0, scalar2=None, op0=ALU.mult), outs=[V], ins=[V])
        xs = self.xs
        vT = self.work[0:4]
        zs = self.work[4:8]
        Sst = self.Sst
        bgt = self.bgt
        H4 = lambda t: t[:, 0:ST].rearrange("p (h d) -> p h d", h=4)
        T0, T1 = self.work[8], self.work[9]
        T2, T3, T4, T5 = self.ubuf
        T6, T7 = self.bx
        T8 = self.rstd
        tokA = self.tok
        T9 = tokA[:, 0:512]
        T10 = tokA[:, 512:1024]
        R9 = R10 = tokA
        lvm = self.lvmask
        bc = lambda ap2: ap2.unsqueeze(1).broadcast_to([P, 4, P])
        S.barrier()
        for sup in range(self.NTOK // ST):
            first = (sup % self.NST == 0)
            self.load_x_norm(sup)
            if first:
                S.op("pool", lambda e: e.memset(self.halo[:, 4:16, :], 0.0), outs=[self.halo])
                S.op("pool", lambda e: e.memset(Sst[:], 0.0), outs=[Sst])
            for sub in range(NSUB):
                sc = slice(sub * P, (sub + 1) * P)
                ps = S.psum()
                for kc in range(KC):
                    self.mm(ps, ps[:, :], self.hT[:, kc, sc], wz[:, kc, :], kc == 0, kc == KC - 1, [WA, self.hT])
                S.op("act", lambda e, ps=ps, sub=sub: e.activation(out=zs[sub][:, :], in_=ps[:, :], func=AF.Silu), outs=[zs[sub]], ins=[ps])
                ps = S.psum()
                for kc in range(KC):
                    self.mm(ps, ps[:, 0:8], self.hT[:, kc, sc], wba[:, kc, :], kc == 0, kc == KC - 1, [WA, self.hT])
                S.op("act", lambda e, ps=ps, sub=sub: e.activation(out=bgt[:, sub, 0:4], in_=ps[:, 0:4], func=AF.Sigmoid), outs=[bgt], ins=[ps])
                S.op("dve", lambda e, sub=sub: e.tensor_scalar(out=bgt[:, sub, 8:12], in0=bgt[:, sub, 0:4], scalar1=-1.0, scalar2=None, op0=ALU.mult), outs=[bgt], ins=[bgt])
                S.op("dve", lambda e, ps=ps, sub=sub: e.tensor_tensor(out=bgt[:, sub, 4:8], in0=ps[:, 4:8], in1=V[:, 84:88], op=ALU.add), outs=[bgt], ins=[ps, V])
                S.op("act", lambda e, sub=sub: e.activation(out=bgt[:, sub, 4:8], in_=bgt[:, sub, 4:8], func=AF.Exp), outs=[bgt], ins=[bgt])
                S.op("act", lambda e, sub=sub: e.activation(out=bgt[:, sub, 4:8], in_=bgt[:, sub, 4:8], func=AF.Ln, bias=self.epsc[:, 1:2], scale=1.0), outs=[bgt], ins=[bgt, self.epsc])
                S.op("dve", lambda e, sub=sub: e.tensor_tensor(out=bgt[:, sub, 4:8], in0=bgt[:, sub, 4:8], in1=V[:, 88:92], op=ALU.mult), outs=[bgt], ins=[bgt, V])
            for c in range(12):
                ps = S.psum()
                for kc in range(KC):
                    self.mm(ps, ps[:, :], wqkv[:, kc, c * P:(c + 1) * P], self.hT[:, kc, :], kc == 0, kc == KC - 1, [WA, self.hT])
                ub = T2 if c % 2 == 0 else T3
                S.op("act", lambda e, ub=ub, ps=ps: e.copy(out=ub[:, 3:3 + ST], in_=ps[:, :]), outs=[ub], ins=[ps])
                S.op("pool", lambda e, ub=ub, c=c: e.tensor_copy(out=ub[:, 0:3], in_=self.halo[:, 4 + c, 0:3]), outs=[ub], ins=[self.halo])
                if c < 8:
                    dst, dres = xs[:, c, :], xs
                else:
                    dst, dres = vT[c - 8][:, :], vT[c - 8]
                S.op("dve", lambda e, ub=ub, c=c, dst=dst: e.tensor_scalar(out=dst, in0=ub[:, 3:3 + ST], scalar1=V[:, 16 + 36 + c:16 + 36 + c + 1], scalar2=None, op0=ALU.mult),
                     outs=[dres], ins=[ub, V])
                for i in range(3):
                    S.op("dve", lambda e, ub=ub, c=c, i=i, dst=dst: e.scalar_tensor_tensor(out=dst, in0=ub[:, i:i + ST], scalar=V[:, 16 + i * 12 + c:16 + i * 12 + c + 1], in1=dst,
                                                                                        op0=ALU.mult, op1=ALU.add), outs=[dres], ins=[ub, V, dres])
                S.op("pool", lambda e, ub=ub, c=c: e.tensor_copy(out=self.halo[:, 4 + c, 0:3], in_=ub[:, ST:ST + 3]), outs=[self.halo], ins=[ub])
                S.op("act", lambda e, dst=dst: e.activation(out=dst, in_=dst, func=AF.Silu), outs=[dres], ins=[dres])
                if c < 8:
                    S.op("act", lambda e, dst=dst: e.activation(out=T0[:, :], in_=dst, func=AF.Square), outs=[T0], ins=[dres])
                    ps2 = S.psum()
                    self.mm(ps2, ps2[:, :], self.ones_f[:, :], T0[:, :], True, True, [self.ones_f, T0])
                    S.op("act", lambda e, ps2=ps2: e.activation(out=T1[:, :], in_=ps2[:, :], func=AF.Sqrt, bias=self.epsc[:, 0:1], scale=1.0), outs=[T1], ins=[ps2, self.epsc])
                    S.op("dve", lambda e: e.reciprocal(out=T1[:, :], in_=T1[:, :]), outs=[T1], ins=[T1])
                    qs = float(128.0 ** -0.5) if c < 4 else 1.0
                    S.op("dve", lambda e, dst=dst, qs=qs: e.scalar_tensor_tensor(out=dst, in0=dst, scalar=qs, in1=T1[:, :], op0=ALU.mult, op1=ALU.mult), outs=[dres], ins=[dres, T1])
            yout = self.big_c
            for sub in range(NSUB):
                sc = slice(sub * P, (sub + 1) * P)
                g4 = bgt[:, sub, 4:8]
                for h in range(4):
                    S.op("dve", lambda e, h=h, sub=sub: e.tensor_scalar(out=H4(T1)[:, h, :], in0=self.uincl[:, :], scalar1=bgt[:, sub, 4 + h:5 + h], scalar2=None, op0=ALU.mult),
                         outs=[T1], ins=[self.uincl, bgt])
                psd, psdT, pse = S.psum(), S.psum(), S.psum()
                for h in range(4):
                    self.mm(psd, psd[:, h * P:(h + 1) * P], H4(T1)[:, h, :], self.lstrict[:, :], True, True, [T1, self.lstrict])
                for h in range(4):
                    self.mm(psdT, psdT[:, h * P:(h + 1) * P], self.lstrict[:, :], H4(T1)[:, h, :], True, True, [T1, self.lstrict])
                self.mm(pse, pse[:, 0:4], self.uincl[:, :], g4, True, True, [self.uincl, bgt])
                self.mm(pse, pse[:, 4:8], self.lstrict[:, :], g4, True, True, [self.lstrict, bgt])
                self.mm(pse, pse[:, 8:12], self.ones_f[:, :], g4, True, True, [self.ones_f, bgt])
                eg = self.eg
                S.op("act", lambda e, pse=pse: e.activation(out=eg[:, 0:12], in_=pse[:, 0:12], func=AF.Exp), outs=[eg], ins=[pse])
                S.op("dve", lambda e: e.tensor_scalar(out=eg[:, 12:16], in0=eg[:, 0:4], scalar1=-1.0, scalar2=None, op0=ALU.mult), outs=[eg], ins=[eg])
                S.op("act", lambda e, psd=psd: e.activation(out=T2[:, 0:ST], in_=psd[:, :], func=AF.Exp), outs=[T2], ins=[psd])
                S.op("act", lambda e, psdT=psdT: e.activation(out=T3[:, 0:ST], in_=psdT[:, :], func=AF.Exp), outs=[T3], ins=[psdT])
                psk = S.psum()
                for h in range(4):
                    self.mm(psk, psk[:, h * P:(h + 1) * P], xs[:, 4 + h, sc], xs[:, 4 + h, sc], True, True, [xs])
                S.op("dve", lambda e, psk=psk: e.tensor_tensor(out=T0[:, :], in0=psk[:, :], in1=T2[:, 0:ST], op=ALU.mult), outs=[T0], ins=[psk, T2])
                for h in range(4):
                    S.op("dve", lambda e, h=h, sub=sub: e.scalar_tensor_tensor(out=H4(T4)[:, h, :], in0=H4(T0)[:, h, :], scalar=bgt[:, sub, 8 + h:9 + h], in1=self.lstrict[:, :],
                                                                             op0=ALU.mult, op1=ALU.mult), outs=[T4], ins=[T0, bgt, self.lstrict])
                pst = S.psum()
                for h in range(4):
                    S.op("pe", lambda e, pst=pst, h=h: e.transpose(out=pst[:, h * P:(h + 1) * P], in_=H4(T4)[:, h, :], identity=self.ident[:]), outs=[pst], ins=[T4, self.ident])
                S.op("act", lambda e, pst=pst: e.copy(out=T5[:, 0:ST], in_=pst[:, :]), outs=[T5], ins=[pst])
                psq = S.psum()
                for h in range(4):
                    self.mm(psq, psq[:, h * P:(h + 1) * P], xs[:, 4 + h, sc], xs[:, h, sc], True, True, [xs])
                S.op("dve", lambda e, psq=psq: e.tensor_tensor(out=T0[:, :], in0=psq[:, :], in1=T3[:, 0:ST], op=ALU.mult), outs=[T0], ins=[psq, T3])
                S.op("dve", lambda e: e.tensor_tensor(out=H4(T6), in0=H4(T0), in1=bc(self.mincl_t[:, :]), op=ALU.mult), outs=[T6], ins=[T0, self.mincl_t])
                S.op("dve", lambda e: e.tensor_tensor(out=H4(T7), in0=H4(T4), in1=bc(lvm[:, 0, :]), op=ALU.mult), outs=[T7], ins=[T4, lvm])
                S.op("dve", lambda e: e.tensor_tensor(out=H4(T7), in0=H4(T7), in1=bc(self.ident[:, :]), op=ALU.add), outs=[T7], ins=[T7, self.ident])
                S.op("dve", lambda e: e.tensor_tensor(out=H4(T8), in0=H4(T5), in1=bc(lvm[:, 7, :]), op=ALU.mult), outs=[T8], ins=[T5, lvm])
                S.op("dve", lambda e: e.tensor_tensor(out=H4(T8), in0=H4(T8), in1=bc(self.ident[:, :]), op=ALU.add), outs=[T8], ins=[T8, self.ident])
                for lv in range(1, 7):
                    last = (lv == 6)
                    psP2 = S.psum()
                    for h in range(4):
                        self.mm(psP2, psP2[:, h * P:(h + 1) * P], H4(T4)[:, h, :], H4(T8)[:, h, :], True, True, [T4, T8])
                    S.op("act", lambda e, psP2=psP2: e.copy(out=T10, in_=psP2[:, :]), outs=[R10], ins=[psP2])
                    if not last:
                        psP = S.psum()
                        for h in range(4):
                            self.mm(psP, psP[:, h * P:(h + 1) * P], H4(T5)[:, h, :], H4(T7)[:, h, :], True, True, [T5, T7])
                        S.op("act", lambda e, psP=psP: e.copy(out=T9, in_=psP[:, :]), outs=[R9], ins=[psP])
                    psQ2 = S.psum()
                    for h in range(4):
                        self.mm(psQ2, psQ2[:, h * P:(h + 1) * P], H4(T7)[:, h, :], T10.rearrange("p (h d) -> p h d", h=4)[:, h, :], True, True, [T7, R10])
                    if not last:
                        psQ = S.psum()
                        for h in range(4):
                            self.mm(psQ, psQ[:, h * P:(h + 1) * P], H4(T8)[:, h, :], T9.rearrange("p (h d) -> p h d", h=4)[:, h, :], True, True, [T8, R9])
                        S.op("dve", lambda e, psQ=psQ, lv=lv: e.tensor_tensor(out=H4(T0), in0=psQ[:, :].rearrange("p (h d) -> p h d", h=4), in1=bc(lvm[:, lv, :]), op=ALU.mult),
                             outs=[T0], ins=[psQ, lvm])
                    S.op("dve", lambda e, psQ2=psQ2, lv=lv: e.tensor_tensor(out=H4(T1), in0=psQ2[:, :].rearrange("p (h d) -> p h d", h=4), in1=bc(lvm[:, 7 + lv, :]), op=ALU.mult),
                         outs=[T1], ins=[psQ2, lvm])
                    if not last:
                        S.op("dve", lambda e: e.tensor_tensor(out=T7[:, 0:ST], in0=T7[:, 0:ST], in1=T0[:, :], op=ALU.add), outs=[T7], ins=[T7, T0])
                    S.op("dve", lambda e: e.tensor_tensor(out=T8[:, 0:ST], in0=T8[:, 0:ST], in1=T1[:, :], op=ALU.add), outs=[T8], ins=[T8, T1])
                psks, psv = S.psum(), S.psum()
                for h in range(4):
                    self.mm(psks, psks[:, h * P:(h + 1) * P], xs[:, 4 + h, sc], Sst[:, h, :], True, True, [xs, Sst])
                for h in range(4):
                    S.op("pe", lambda e, psv=psv, h=h, sc=sc: e.transpose(out=psv[:, h * P:(h + 1) * P], in_=vT[h][:, sc], identity=self.ident[:]), outs=[psv], ins=[vT[h], self.ident])
                S.op("act", lambda e, psv=psv: e.copy(out=T9, in_=psv[:, :]), outs=[R9], ins=[psv])
                for h in range(4):
                    hs = slice(h * P, (h + 1) * P)
                    S.op("dve", lambda e, psks=psks, h=h, hs=hs: e.scalar_tensor_tensor(out=T1[:, hs], in0=psks[:, hs], scalar=eg[:, 12 + h:13 + h], in1=T9[:, hs], op0=ALU.mult, op1=ALU.add),
                         outs=[T1], ins=[psks, eg, R9])
                    S.op("dve", lambda e, h=h, hs=hs, sub=sub: e.tensor_scalar(out=T1[:, hs], in0=T1[:, hs], scalar1=bgt[:, sub, h:h + 1], scalar2=None, op0=ALU.mult), outs=[T1], ins=[T1, bgt])
                psn = S.psum()
                for h in range(4):
                    hs = slice(h * P, (h + 1) * P)
                    self.mm(psn, psn[:, hs], H4(T8)[:, h, :], T1[:, hs], True, True, [T8, T1])
                S.op("act", lambda e, psn=psn: e.copy(out=T10, in_=psn[:, :]), outs=[R10], ins=[psn])
                psqs, pso2, pskt = S.psum(), S.psum(), S.psum()
                for h in range(4):
                    hs = slice(h * P, (h + 1) * P)
                    self.mm(psqs, psqs[:, hs], xs[:, h, sc], Sst[:, h, :], True, True, [xs, Sst])
                for h in range(4):
                    hs = slice(h * P, (h + 1) * P)
                    self.mm(pso2, pso2[:, hs], H4(T6)[:, h, :], T10[:, hs], True, True, [T6, R10])
                S.op("act", lambda e, pso2=pso2: e.copy(out=T2[:, 0:ST], in_=pso2[:, :]), outs=[T2], ins=[pso2])
                for h in range(4):
                    hs = slice(h * P, (h + 1) * P)
                    S.op("dve", lambda e, psqs=psqs, h=h, hs=hs: e.scalar_tensor_tensor(out=T0[:, hs], in0=psqs[:, hs], scalar=eg[:, h:h + 1], in1=T2[:, hs], op0=ALU.mult, op1=ALU.add),
                         outs=[T0], ins=[psqs, eg, T2])
                for h in range(4):
                    S.op("pe", lambda e, pskt=pskt, h=h, sc=sc: e.transpose(out=pskt[:, h * P:(h + 1) * P], in_=xs[:, 4 + h, sc], identity=self.ident[:]), outs=[pskt], ins=[xs, self.ident])
                for h in range(4):
                    hs = slice(h * P, (h + 1) * P)
                    S.op("dve", lambda e, pskt=pskt, h=h, hs=hs: e.tensor_scalar(out=T3[:, hs], in0=pskt[:, hs], scalar1=eg[:, 4 + h:5 + h], scalar2=None, op0=ALU.mult), outs=[T3], ins=[pskt, eg])
                psu = S.psum()
                for h in range(4):
                    hs = slice(h * P, (h + 1) * P)
                    self.mm(psu, psu[:, hs], T3[:, hs], T10[:, hs], True, True, [T3, R10])
                for h in range(4):
                    hs = slice(h * P, (h + 1) * P)
                    S.op("dve", lambda e, psu=psu, h=h, hs=hs: e.scalar_tensor_tensor(out=Sst[:, h, :], in0=Sst[:, h, :], scalar=eg[:, 8 + h:9 + h], in1=psu[:, hs], op0=ALU.mult, op1=ALU.add),
                         outs=[Sst], ins=[Sst, eg, psu])
                for h in range(4):
                    hs = slice(h * P, (h + 1) * P)
                    S.op("act", lambda e, h=h, hs=hs: e.activation(out=T1[:, hs], in_=T0[:, hs], func=AF.Square, accum_out=eg[:, 16 + h:17 + h]), outs=[T1, eg], ins=[T0])
                S.op("dve", lambda e: e.tensor_scalar(out=eg[:, 16:20], in0=eg[:, 16:20], scalar1=1.0 / 128.0, scalar2=EPS, op0=ALU.mult, op1=ALU.add), outs=[eg], ins=[eg])
                S.op("act", lambda e: e.activation(out=eg[:, 16:20], in_=eg[:, 16:20], func=AF.Sqrt), outs=[eg], ins=[eg])
                S.op("dve", lambda e: e.reciprocal(out=eg[:, 16:20], in_=eg[:, 16:20]), outs=[eg], ins=[eg])
                for h in range(4):
                    hs = slice(h * P, (h + 1) * P)
                    S.op("dve", lambda e, h=h, hs=hs: e.scalar_tensor_tensor(out=T0[:, hs], in0=T0[:, hs], scalar=eg[:, 16 + h:17 + h], in1=self.bnb[:, :], op0=ALU.mult, op1=ALU.mult),
                         outs=[T0], ins=[T0, eg, self.bnb])
                S.op("dve", lambda e, sub=sub: e.tensor_tensor(out=T0[:, :], in0=T0[:, :], in1=zs[sub][:, :], op=ALU.mult), outs=[T0], ins=[T0, zs[sub]])
                psy = S.psum()
                for h in range(4):
                    S.op("pe", lambda e, psy=psy, h=h: e.transpose(out=psy[:, h * P:(h + 1) * P], in_=T0[:, h * P:(h + 1) * P], identity=self.ident[:]), outs=[psy], ins=[T0, self.ident])
                S.op("act", lambda e, psy=psy, sc=sc: e.copy(out=yout[:, 0:4, sc], in_=psy[:, :].rearrange("p (h d) -> p h d", h=4)), outs=[yout], ins=[psy])
            for c in range(4):
                self.store_y(1, sup, yout[:, c, :], yout, c)


def build_cfg(cfg):
    return Prog(cfg)


_CACHE = {}


def run_prog(cfg, inputs_per_core):
    key = repr(sorted((k, str(v)) for k, v in cfg.items()))
    if key not in _CACHE:
        _CACHE[key] = build_cfg(cfg)
    prog = _CACHE[key]
    hc = host_consts()
    in_maps = []
    for ipc in inputs_per_core:
        m = dict(ipc)
        for k, v in hc.items():
            m["c_" + k] = v
        in_maps.append(m)
    import os
    if os.environ.get("K_TRACE"):
        res = run_bass_kernel_spmd(prog.nc, in_maps, core_ids=list(range(len(in_maps))), trace=True)
        print("EXEC_TIME_NS", res.exec_time_ns)
    else:
        res = run_bass_kernel_spmd(prog.nc, in_maps, core_ids=list(range(len(in_maps))))
    return res


def kernel(**inputs):
    ncores = 8
    x = np.ascontiguousarray(inputs["x"], dtype=np.float32)
    mem = np.ascontiguousarray(inputs["mem"], dtype=np.float32)
    B, SEQ, _ = x.shape
    nseq = B // ncores
    cfg = {"depth": int(inputs["w_in"].shape[0]), "seq": SEQ, "nseq": nseq}
    per_core = []
    for c in range(ncores):
        m = {k: np.ascontiguousarray(inputs[k], dtype=np.float32) for k in WEIGHT_NAMES}
        m["x"] = x[c * nseq:(c + 1) * nseq].reshape(nseq * SEQ, D)
        m["mem"] = mem[c * nseq:(c + 1) * nseq].reshape(nseq * MEM, D)
        per_core.append(m)
    res = run_prog(cfg, per_core)
    out = np.concatenate([np.asarray(r["out"]).reshape(nseq, SEQ, D) for r in res.results], axis=0)
    return out.astype(np.float32)
```
